# Optimizing a Trainium2 kernel written in Bass

```python
import jax, jax.numpy as jnp
from jax import lax
import numpy as np

D_MODEL = 1024
BATCH = 8
SEQ = 4096
DEPTH = 1

CHUNK = 64
N_LEFT_CHUNKS = 8
BAND = (N_LEFT_CHUNKS + 1) * CHUNK
A_HEADS = 8
A_HEAD_DIM = 64
A_WIDTH = A_HEADS * A_HEAD_DIM
REL_CLIP = 256
B_HEADS = 8
B_HEAD_DIM = 64
B_WIDTH = B_HEADS * B_HEAD_DIM
IDX_HEADS = 8
IDX_DIM = 32
TOPK_MAX = 256
Q_BLOCK = 128
ROPE_THETA = 500000.0
ROT_DIM_B = B_HEAD_DIM // 4
ROT_DIM_IDX = IDX_DIM // 4
EPS = 1e-6
NEG = -1e30

SPLIT_SIZES = (
    A_WIDTH, A_WIDTH, A_WIDTH, A_WIDTH,
    B_WIDTH, B_HEAD_DIM, B_HEAD_DIM, B_WIDTH,
    IDX_HEADS * IDX_DIM, IDX_DIM, IDX_HEADS,
    D_MODEL, D_MODEL,
)
IN_WIDTH = sum(SPLIT_SIZES)

kernel_name = "hybrid_chunked_relpos_dsa_gated_block"


def rms_norm(x, g):
    xf = x.astype(jnp.float32)
    y = xf * lax.rsqrt(jnp.mean(xf * xf, axis=-1, keepdims=True) + EPS)
    return (y * g.astype(jnp.float32)).astype(x.dtype)


def partial_rope(x, rot_dim):
    S = x.shape[1]
    half = rot_dim // 2
    inv = ROPE_THETA ** (-jnp.arange(half, dtype=jnp.float32) / half)
    ang = jnp.arange(S, dtype=jnp.float32)[:, None] * inv[None, :]
    cos = jnp.cos(ang)[None, :, None, :].astype(x.dtype)
    sin = jnp.sin(ang)[None, :, None, :].astype(x.dtype)
    x1, x2, xp = x[..., :half], x[..., half:rot_dim], x[..., rot_dim:]
    return jnp.concatenate([x1 * cos - x2 * sin, x2 * cos + x1 * sin, xp], axis=-1)


def chunked_relpos_attention(q, k, v, rel_bias):
    B, S, H, d = q.shape
    n_chunks = S // CHUNK
    pad = N_LEFT_CHUNKS * CHUNK
    k_pad = jnp.pad(k, ((0, 0), (pad, 0), (0, 0), (0, 0)))
    v_pad = jnp.pad(v, ((0, 0), (pad, 0), (0, 0), (0, 0)))
    i = jnp.arange(CHUNK)[:, None]
    j = jnp.arange(BAND)[None, :]
    dist = pad + i - j
    bias = rel_bias.astype(jnp.float32)[:, jnp.clip(dist, -REL_CLIP, REL_CLIP) + REL_CLIP]
    scale = d ** -0.5

    def one_chunk(c):
        start = c * CHUNK
        qc = lax.dynamic_slice_in_dim(q, start, CHUNK, axis=1)
        kc = lax.dynamic_slice_in_dim(k_pad, start, BAND, axis=1)
        vc = lax.dynamic_slice_in_dim(v_pad, start, BAND, axis=1)
        s = jnp.einsum('bqhd,bkhd->bhqk', qc, kc).astype(jnp.float32) * scale + bias[None]
        valid = (start - pad + j) >= 0
        s = jnp.where(valid[None, None], s, NEG)
        p = jax.nn.softmax(s, axis=-1).astype(v.dtype)
        return jnp.einsum('bhqk,bkhd->bqhd', p, vc)

    out = lax.map(one_chunk, jnp.arange(n_chunks))
    return out.transpose(1, 0, 2, 3, 4).reshape(B, S, H * d)


def dsa_sparse_attention(q, k, v, iq, ik, iw):
    B, S, H, d = q.shape
    top_k = min(TOPK_MAX, S // 4)
    n_blocks = S // Q_BLOCK
    key_chunk = jnp.arange(S) // CHUNK
    scale = d ** -0.5
    gather = jax.vmap(lambda a, idx: a[idx])

    def one_block(blk):
        start = blk * Q_BLOCK
        qb = lax.dynamic_slice_in_dim(q, start, Q_BLOCK, axis=1)
        iqb = lax.dynamic_slice_in_dim(iq, start, Q_BLOCK, axis=1)
        iwb = lax.dynamic_slice_in_dim(iw, start, Q_BLOCK, axis=1)
        q_chunk = (start + jnp.arange(Q_BLOCK)) // CHUNK
        logits = jax.nn.relu(jnp.einsum('bqhd,bsd->bqhs', iqb, ik))
        score = jnp.einsum('bqhs,bqh->bqs', logits, iwb).astype(jnp.float32)
        adm = key_chunk[None, :] <= q_chunk[:, None]
        score = jnp.where(adm[None], score, -jnp.inf)
        _, idx = lax.top_k(score, top_k)
        sel_valid = (idx // CHUNK) <= q_chunk[None, :, None]
        ks = gather(k, idx)
        vs = gather(v, idx)
        s = jnp.einsum('bqhd,bqkd->bqhk', qb, ks).astype(jnp.float32) * scale
        s = jnp.where(sel_valid[:, :, None, :], s, NEG)
        p = jax.nn.softmax(s, axis=-1).astype(v.dtype)
        return jnp.einsum('bqhk,bqkd->bqhd', p, vs)

    out = lax.map(one_block, jnp.arange(n_blocks))
    return out.transpose(1, 0, 2, 3, 4).reshape(B, S, H * d)


def setup_inputs(seed: int = 0) -> dict:
    key = jax.random.key(seed)
    ks = jax.random.split(key, 10)
    nrm = jax.random.normal
    return {
        "x": nrm(ks[0], (BATCH, SEQ, D_MODEL), jnp.float32),
        "norm_gain": 1.0 + 0.05 * nrm(ks[1], (DEPTH, D_MODEL), jnp.float32),
        "w_in": nrm(ks[2], (DEPTH, D_MODEL, IN_WIDTH), jnp.float32) * D_MODEL ** -0.5,
        "b_merge": 0.05 * nrm(ks[3], (DEPTH, 2, D_MODEL), jnp.float32),
        "rel_bias": 0.2 * nrm(ks[4], (DEPTH, A_HEADS, 2 * REL_CLIP + 1), jnp.float32),
        "w_branch_a": nrm(ks[5], (DEPTH, A_WIDTH, D_MODEL), jnp.float32) * A_WIDTH ** -0.5,
        "w_branch_b": nrm(ks[6], (DEPTH, B_WIDTH, D_MODEL), jnp.float32) * B_WIDTH ** -0.5,
        "w_out": nrm(ks[7], (DEPTH, D_MODEL, D_MODEL), jnp.float32) * D_MODEL ** -0.5,
        "final_norm_gain": 1.0 + 0.05 * nrm(ks[8], (D_MODEL,), jnp.float32),
    }


def reference(x, norm_gain, w_in, b_merge, rel_bias, w_branch_a, w_branch_b, w_out, final_norm_gain):
    B, S, _ = x.shape
    offsets = [int(o) for o in np.cumsum(SPLIT_SIZES)[:-1]]
    h = x
    for layer in range(DEPTH):
        xn = rms_norm(h, norm_gain[layer])
        proj = xn @ w_in[layer]
        qa, ka, va, ga, qb, kb, vb, gb, iq, ik, iw, za, zb = jnp.split(proj, offsets, axis=-1)
        qa = qa.reshape(B, S, A_HEADS, A_HEAD_DIM)
        ka = ka.reshape(B, S, A_HEADS, A_HEAD_DIM)
        va = va.reshape(B, S, A_HEADS, A_HEAD_DIM)
        ya = chunked_relpos_attention(qa, ka, va, rel_bias[layer]) * jax.nn.silu(ga)
        qb = partial_rope(qb.reshape(B, S, B_HEADS, B_HEAD_DIM), ROT_DIM_B)
        kb = partial_rope(kb[:, :, None, :], ROT_DIM_B)[:, :, 0]
        iq = partial_rope(iq.reshape(B, S, IDX_HEADS, IDX_DIM), ROT_DIM_IDX)
        ik = partial_rope(ik[:, :, None, :], ROT_DIM_IDX)[:, :, 0]
        iw = iw * (IDX_HEADS ** -0.5 * IDX_DIM ** -0.5)
        yb = dsa_sparse_attention(qb, kb, vb, iq, ik, iw) * jax.nn.silu(gb)
        gate_a = jax.nn.sigmoid(za + b_merge[layer, 0])
        gate_b = jax.nn.sigmoid(zb + b_merge[layer, 1])
        merged = gate_a * (ya @ w_branch_a[layer]) + gate_b * (yb @ w_branch_b[layer])
        h = h + merged @ w_out[layer]
    return rms_norm(h, final_norm_gain)
```

```python
import contextlib
import numpy as np
import ml_dtypes
import concourse.bass as bass
import concourse.mybir as mybir
from concourse.bass_utils import run_bass_kernel_spmd

F32 = mybir.dt.float32
BF16 = mybir.dt.bfloat16
ALU = mybir.AluOpType
AF = mybir.ActivationFunctionType
AX = mybir.AxisListType

S = 4096
D = 1024
NTILES = 32
CW = 5544
QA, KA, VA, GA, QB, KBO, VBO, GB, IQ, IK, IW, ZA, ZB = (
    0, 512, 1024, 1536, 2048, 2560, 2624, 2688, 3200, 3456, 3488, 3496, 4520)
NBIS = 16
DVE_SHARE = 0.45
NEGM = -30000.0

COMPUTE = ("pe", "act", "dve", "pool")
NDMASEM = 8


class Prog:
    def __init__(self, nc, dma_queues=("sp", "pool")):
        self.nc = nc
        self.ops = {e: [] for e in ("pe", "act", "dve", "pool", "sp")}
        self.cnt = {}
        self.lastw = {}
        self.reads = {}
        self.waited = {e: {} for e in self.ops}
        self.dma_i = {q: 0 for q in dma_queues}
        self.timelines = list(COMPUTE) + [f"dma_{q}{i}" for q in dma_queues for i in range(NDMASEM)]
        for t in self.timelines:
            self.cnt[t] = 0
        self.marked = {e: set() for e in COMPUTE}

    def _need(self, eng, waits, tl, val):
        if tl == eng and eng == "pe":
            return
        if self.waited[eng].get(tl, 0) >= val:
            return
        waits[tl] = max(waits.get(tl, 0), val)

    def _deps(self, eng, tl_self, reads, writes):
        waits = {}
        for r in reads:
            lw = self.lastw.get(r)
            if lw:
                self._need(eng, waits, lw[0], lw[1])
        for w in writes:
            lw = self.lastw.get(w)
            if lw and lw[0] != tl_self:
                self._need(eng, waits, lw[0], lw[1])
            for tl, v in self.reads.get(w, {}).items():
                if tl != tl_self:
                    self._need(eng, waits, tl, v)
        for tl, v in waits.items():
            self.waited[eng][tl] = max(self.waited[eng].get(tl, 0), v)
            if tl in self.marked:
                self.marked[tl].add(v)
        return waits

    def _commit(self, tl, val, reads, writes):
        for r in reads:
            self.reads.setdefault(r, {})[tl] = val
        for w in writes:
            self.lastw[w] = (tl, val)
            self.reads[w] = {}

    def op(self, eng, fn, reads=(), writes=(), inc=True):
        waits = self._deps(eng, eng, reads, writes)
        self.cnt[eng] += 1
        val = self.cnt[eng]
        self.ops[eng].append((waits, fn, (eng, val)))
        self._commit(eng, val, reads, writes)

    def dma(self, q, fn, reads=(), writes=()):
        i = self.dma_i[q]
        self.dma_i[q] += 1
        tl = f"dma_{q}{i % NDMASEM}"
        waits = self._deps(q, tl, reads, writes)
        prev = self.cnt[tl]
        if prev > 0 and self.waited[q].get(tl, 0) < prev:
            waits[tl] = max(waits.get(tl, 0), prev)
            self.waited[q][tl] = prev
        self.cnt[tl] += 16
        self.ops[q].append((waits, fn, (tl, 16)))
        self._commit(tl, self.cnt[tl], reads, writes)

    def finish_waits(self, eng="sp"):
        waits = {}
        for tl in self.timelines:
            if self.cnt[tl] > 0 and self.waited[eng].get(tl, 0) < self.cnt[tl]:
                waits[tl] = self.cnt[tl]
                if tl in self.marked:
                    self.marked[tl].add(self.cnt[tl])
        self.ops[eng].append((waits, None, None))

    def emit(self):
        nc = self.nc
        rank = {}
        for e in COMPUTE:
            rank[e] = {s: i + 1 for i, s in enumerate(sorted(self.marked[e]))}
        with contextlib.ExitStack() as st:
            sems = {t: st.enter_context(nc.semaphore("s_" + t)) for t in self.timelines}
            block = st.enter_context(nc.Block())

            def run(engname):
                def body(e):
                    for waits, fn, inc in self.ops[engname]:
                        for tl, v in waits.items():
                            e.wait_ge(sems[tl], rank[tl][v] if tl in rank else v)
                        if fn is None:
                            continue
                        ins = fn(e)
                        if inc is None:
                            continue
                        if inc[0] in rank:
                            if inc[1] in rank[inc[0]]:
                                ins.then_inc(sems[inc[0]], 1)
                        else:
                            ins.then_inc(sems[inc[0]], inc[1])
                return body

            block.sync(run("sp"))
            block.tensor(run("pe"))
            block.scalar(run("act"))
            block.vector(run("dve"))
            block.gpsimd(run("pool"))


def build(NT=NTILES, dbg=False, stop=None):
    nc = bass.Bass("TRN2", target_bir_lowering=False)
    din = lambda name, shape, dt=F32: nc.dram_tensor(name, shape, dt, kind="ExternalInput").ap()
    x_d = din("x", [S, D])
    win_d = din("w_in", [D, CW])
    wa_d = din("wa", [512, D])
    wb_d = din("wb", [512, D])
    wo_d = din("wo", [D, D])
    gcol_d = din("gcol", [128, 8])
    fg_d = din("fgain", [1, D])
    bm_d = din("bmrow", [1, 2 * D])
    bias_d = din("biasT", [128, 5120])
    rope_d = din("rope", [128, 32 * 40])
    idf_d = din("identf", [128, 128])
    cb_d = din("cbf", [128, 896], BF16)
    out_d = nc.dram_tensor("out", [S, D], F32, kind="ExternalOutput").ap()
    dbg_d = {}
    if dbg:
        for name, shape in (("d_qit", [128, 1024]), ("d_kc", [128, 512]), ("d_ya", [128, 512]),
                            ("d_sc", [128, 512]), ("d_lo", [128, 1]), ("d_cnt", [128, 1]),
                            ("d_yb", [128, 512]), ("d_mg", [128, 1024])):
            dbg_d[name] = nc.dram_tensor(name, shape, F32, kind="ExternalOutput").ap()

    with contextlib.ExitStack() as st:
        T = lambda name, shape, dt: st.enter_context(nc.sbuf_tensor(name, shape, dt))
        winb = T("winb", [128, 8 * CW], BF16)
        wab = T("wab", [128, 4 * D], BF16)
        wbb = T("wbb", [128, 4 * D], BF16)
        wob = T("wob", [128, 8 * D], BF16)
        sc = T("sc", [128, 4096], F32)
        biasb = T("biasb", [128, 5120], BF16)
        cbf = T("cbf_s", [128, 896], BF16)
        rope = T("rope_s", [128, 32 * 40], F32)
        identf = T("identf_s", [128, 128], F32)
        fgb = T("fgb", [128, D], F32)
        gcol = T("gcol_s", [128, 8], F32)
        bmb = T("bmb", [1, 2 * D], BF16)
        kaT = T("kaT", [128, 4 * 5 * 128], BF16)
        vaug = T("vaug", [128, 5 * 8 * 65], BF16)
        kcache = T("kcache", [128, S], BF16)
        vbc = T("vbc", [128, 32 * 128], BF16)
        xnT = T("xnT", [128, 8 * 128], BF16)
        qaT = T("qaT", [128, 4 * 128], BF16)
        gbT = T("gbT", [128, 4 * 128], BF16)
        gas = T("gas", [128, 512], BF16)
        QI = T("QI", [128, 8 * 128], BF16)
        KI = T("KI", [128, 128], BF16)
        QIT = T("QIT", [128, 8 * 128], BF16)
        mgb = QI
        mgT = QIT
        Dh = T("Dh", [128, 8 * 128], BF16)
        Rr = T("Rr", [128, 2 * 512], BF16)
        PT = T("PT", [128, 2 * 1024], BF16)
        nmr = T("nmr", [128, 2 * 128], BF16)
        junk = T("junk", [128, 64], BF16)
        junk2 = T("junk2", [128, 64], BF16)
        sm2 = T("sm2", [128, 32], F32)
        dmy = T("dmy", [128, 2], F32)
        bst = T("bst", [128, 16], F32)
        CI = sm2[:, 0:16]
        DI = sm2[:, 16:32]
        yab = T("yab", [128, 512], BF16)
        yaT = T("yaT", [128, 4 * 128], BF16)
        ybT = T("ybT", [128, 4 * 128], BF16)
        sm = T("sm", [128, 64], F32)
        PS = st.enter_context(nc.psum_tensor("ps", [128, 4096], F32))

        ident = cbf[:, 0:128]
        I4 = cbf[:, 128:640]
        ones = cbf[:, 640:768]
        bank = lambda i: PS[:, i * 512:(i + 1) * 512]
        bankb = lambda i: PS[:, i * 512:(i + 1) * 512].bitcast(BF16)
        K = [f"K{i}" for i in range(8)]
        PTW = (['PT0'], ['PT1', 'R2', 'R3'])

        P = Prog(nc)

        def mm(out, lhsT, rhs, start, stop, reads, writes, inc=False):
            P.op("pe", lambda e: e.matmul(out, lhsT=lhsT, rhs=rhs, start=start, stop=stop),
                 reads=reads, writes=writes, inc=inc)

        def tr(out, in_, reads, writes, inc=False):
            P.op("pe", lambda e: e.transpose(out=out, in_=in_, identity=ident),
                 reads=list(reads) + ["cbf"], writes=writes, inc=inc)

        def act(out, in_, func, reads, writes, scale=None, bias=None, accum=None):
            kw = {}
            if scale is not None:
                kw["scale"] = scale
            if bias is not None:
                kw["bias"] = bias
            if accum is not None:
                kw["accum_out"] = accum
            P.op("act", lambda e: e.activation(out=out, in_=in_, func=func, **kw), reads=reads, writes=writes)

        def ts(out, in0, s1, s2, op0, op1, reads, writes, accum=None, eng="dve"):
            kw = {}
            if op1 is not None:
                kw["op1"] = op1
            if accum is not None:
                kw["accum_out"] = accum
            P.op(eng, lambda e: e.tensor_scalar(out=out, in0=in0, scalar1=s1, scalar2=s2, op0=op0, **kw),
                 reads=reads, writes=writes)

        def tt(out, in0, in1, op, reads, writes, eng="dve"):
            P.op(eng, lambda e: e.tensor_tensor(out=out, in0=in0, in1=in1, op=op), reads=reads, writes=writes)

        def stt(out, in0, scalar, in1, op0, op1, reads, writes, eng="dve"):
            P.op(eng, lambda e: e.scalar_tensor_tensor(out=out, in0=in0, scalar=scalar, in1=in1, op0=op0, op1=op1),
                 reads=reads, writes=writes)

        def cp(out, in_, reads, writes, eng="dve"):
            P.op(eng, lambda e: e.tensor_copy(out=out, in_=in_), reads=reads, writes=writes)

        def dma(q, out, in_, reads, writes):
            P.dma(q, lambda e: e.dma_start(out=out, in_=in_), reads=reads, writes=writes)

        dma("sp", cbf[:], cb_d, [], ["cbf"])
        dma("sp", identf[:], idf_d, [], ["identf"])
        dma("sp", rope[:], rope_d, [], ["rope"])
        dma("sp", gcol[:], gcol_d, [], ["gcol"])
        dma("sp", fgb[:], fg_d.partition_broadcast(128), [], ["fgb"])
        dma("sp", sc[0:1, 0:2 * D], bm_d, [], ["stg0", "stg1", "stg2", "stg3"])
        cp(bmb[:], sc[0:1, 0:2 * D], ["stg0", "stg1", "stg2", "stg3"], ["bmb"])
        P.op("pool", lambda e: e.memset(vaug[:], 1.0), writes=["vaug0", "vaug1", "vaug2", "vaug3", "vaug4"])
        P.op("pool", lambda e: e.memset(KI[:], 0.0), writes=["KI"])
        P.op("pool", lambda e: e.memset(vbc[:], 1.0), writes=["vbc"])
        P.op("pool", lambda e: e.memset(sm[:, 63:64], 1e-6), writes=["sm"])
        P.op("pool", lambda e: e.memset(dmy[:], 1.0), writes=["dmy"])
        for _i in range(16):
            P.op("pool", lambda e, a=sm2[:, _i:_i + 1], v=2.0 ** -(_i + 1): e.memset(a, v), writes=["CI"])
        P.op("pool", lambda e: e.memset(QI[:], 0.0), writes=["QI"])

        stage_i = [0]
        ENG_ROT = ("dve", "act", "dve", "act", "dve", "pool", "act", "dve")

        def convert(dst, src_dram, width, scale_ap=None, cscale=None):
            if cscale is not None:
                scale_ap = cscale
            i = stage_i[0]
            stage_i[0] += 1
            s = i % 4
            stg = sc[:, s * 1024: s * 1024 + width]
            res = f"stg{s}"
            dma("sp" if i % 2 == 0 else "pool", stg, src_dram, [], [res])
            eng = ENG_ROT[i % len(ENG_ROT)]
            if eng == "act":
                if scale_ap is None:
                    act(dst, stg, AF.Copy, [res], ["wts"])
                else:
                    act(dst, stg, AF.Copy, [res, "gcol"], ["wts"], scale=scale_ap)
            elif scale_ap is None:
                cp(dst, stg, [res], ["wts"], eng=eng)
            else:
                ts(dst, stg, scale_ap, None, ALU.mult, None, [res, "gcol"], ["wts"], eng=eng)

        for kc in range(8):
            for c0 in range(0, CW, 1024):
                c1 = min(c0 + 1024, CW)
                convert(winb[:, kc * CW + c0: kc * CW + c1], win_d[kc * 128:(kc + 1) * 128, c0:c1], c1 - c0,
                        gcol[:, kc:kc + 1])
        for c in range(4):
            convert(wab[:, c * D:(c + 1) * D], wa_d[c * 128:(c + 1) * 128, :], D, cscale=0.5)
            convert(wbb[:, c * D:(c + 1) * D], wb_d[c * 128:(c + 1) * 128, :], D, cscale=0.5)
        for c in range(8):
            convert(wob[:, c * D:(c + 1) * D], wo_d[c * 128:(c + 1) * 128, :], D, cscale=0.5)
        for c0 in range(0, 5120, 1024):
            convert(biasb[:, c0:c0 + 1024], bias_d[:, c0:c0 + 1024], 1024)

        SCR = ["stg0", "stg1", "stg2", "stg3"]
        if stop == 'setup':
            NT = 0

        def proj_tok(bk, col0, width, off=0, first=True, last=True):
            for kc in range(8):
                mm(bank(bk)[:, off:off + width], xnT[:, kc * 128:(kc + 1) * 128],
                   winb[:, kc * CW + col0: kc * CW + col0 + width], kc == 0, kc == 7,
                   ["xnT", "wts"], [K[bk]], inc=(kc == 7 and last))

        def proj_feat(bk, col0, off):
            for kc in range(8):
                mm(bank(bk)[:, off:off + 128], winb[:, kc * CW + col0: kc * CW + col0 + 128],
                   xnT[:, kc * 128:(kc + 1) * 128], kc == 0, kc == 7, ["xnT", "wts"], [K[bk]], inc=(kc == 7))

        def rope_apply(dst3, src3, cos2, sin2, H, half, tmp, reads, writes):
            n = H * half
            cb = cos2.unsqueeze(1).broadcast_to([128, H, half])
            sb = sin2.unsqueeze(1).broadcast_to([128, H, half])
            t = [tmp[:, i * n:(i + 1) * n].rearrange("p (h d) -> p h d", h=H) for i in range(4)]
            x1 = src3[:, :, 0:half]
            x2 = src3[:, :, half:2 * half]
            rr = list(reads) + ["rope"]
            tt(t[0], x1, cb, ALU.mult, rr, ["rtmp"])
            tt(t[1], x2, sb, ALU.mult, rr, ["rtmp"])
            tt(t[2], x2, cb, ALU.mult, rr, ["rtmp"])
            tt(t[3], x1, sb, ALU.mult, rr, ["rtmp"])
            tt(dst3[:, :, 0:half], t[0], t[1], ALU.subtract, ["rtmp"], writes)
            tt(dst3[:, :, half:2 * half], t[2], t[3], ALU.add, ["rtmp"], writes)

        rtmp = sm

        for b in range(NT):
            slot = b % 5
            N = 128 * (b + 1)
            rp = rope[:, b * 40:(b + 1) * 40]
            x_sb = PT[:].bitcast(F32)
            PTALL = ["PT0", "PT1", "R2", "R3"]
            xnb = Rr[:]
            XNB = ["R0", "R1"]
            rt = sc[:, 1536:2560]

            def x_norm_stats():
                act(xnb, x_sb, AF.Square, PTALL, XNB + ["nsm"], accum=sm[:, 0:1])
                act(sm[:, 1:2], sm[:, 0:1], AF.Sqrt, ["nsm"], ["nsm"], scale=1.0 / D, bias=sm[:, 63:64])
                P.op("dve", lambda e, o=sm[:, 2:3], i=sm[:, 1:2]: e.reciprocal(out=o, in_=i), reads=["nsm"], writes=["nsm"])

            def x_norm_apply():
                ts(xnb, x_sb, sm[:, 2:3], None, ALU.mult, None, ["nsm"] + PTALL, XNB)
                P.op("dve", lambda e, a=PT[0:64, 0:1024]: e.memset(a, 0.0), reads=[], writes=["PT0"])

            def x_transposes():
                for kc in range(8):
                    tr(bankb(7)[:, kc * 128:(kc + 1) * 128], xnb[:, kc * 128:(kc + 1) * 128], XNB, [K[7]], inc=(kc == 7))

            if b == 0:
                dma("sp", x_sb, x_d[0:128, :], [], PTALL)
                x_norm_stats()
                x_norm_apply()
                x_transposes()
                cp(xnT[:], bankb(7), [K[7]], ["xnT"])
            act(dmy[:, 0:1], dmy[:, 1:2], AF.Tanh, [], ["dmy"])
            if stop == 'p1a':
                break
            proj_tok(0, VA, 512)
            va_dst = vaug[:, slot * 520:(slot + 1) * 520].rearrange("p (h d) -> p h d", h=8)[:, :, 0:64]
            act(va_dst, bank(0).rearrange("p (h d) -> p h d", h=8), AF.Copy, [K[0]], [f"vaug{slot}"])
            proj_tok(1, GA, 512)
            tnh = sc[:, 2816:3328]
            act(tnh, bank(1), AF.Tanh, [K[1]] + SCR, SCR, scale=0.5)
            stt(gas[:], tnh, 1.0, bank(1), ALU.add, ALU.mult, [K[1]] + SCR, ["gas"])
            if stop == 'p1b':
                break
            proj_tok(2, QB, 512)
            QI3 = QI[:].rearrange("p (h d) -> p h d", h=8)
            b03 = bank(2).rearrange("p (h d) -> p h d", h=8)
            ts(QI3[:, :, 16:64], b03[:, :, 16:64], 0.125, None, ALU.mult, None, [K[2]], ["QI"])
            rope_apply(QI3[:, :, 0:16], b03[:, :, 0:16], rp[:, 16:24], rp[:, 24:32], 8, 8, rt, [K[2]] + SCR, ["QI"] + SCR)
            if stop == 'p1c':
                break
            proj_tok(3, KBO, 128, off=0, last=False)
            proj_tok(3, IQ, 296, off=128)
            for hp in range(4):
                proj_feat(4, QA + hp * 128, hp * 128)
            act(qaT[:], bank(4), AF.Copy, [K[4]], ["qaT"], scale=0.125)
            for hp in range(4):
                proj_feat(5, KA + hp * 128, hp * 128)
            proj_tok(6, GB, 512)
            b1 = bank(3)
            rope_apply(KI[:, 0:64].rearrange("p (h d) -> p h d", h=1), b1[:, 0:64].rearrange("p (h d) -> p h d", h=1),
                       rp[:, 0:8], rp[:, 8:16], 1, 8, rt, [K[3]] + SCR, ["KI"] + SCR)
            cp(KI[:, 16:64], b1[:, 16:64], [K[3]], ["KI"])
            cp(vbc[:, b * 128:b * 128 + 64], b1[:, 64:128], [K[3]], ["vbc"])
            iw_ps = b1[:, 128 + 288:128 + 296]
            ts(sm[:, 16:24], iw_ps, 0.0, 2.0, ALU.is_ge, ALU.mult, [K[3]], ["sm"])
            ts(sm[:, 16:24], sm[:, 16:24], -1.0, None, ALU.add, None, ["sm"], ["sm"])
            stt(sm[:, 8:16], iw_ps, 0.0625, sm[:, 16:24], ALU.mult, ALU.mult, [K[3], "sm"], ["sm"])
            iqs = sc[:, 2560:2816].rearrange("p (h d) -> p h d", h=8)
            tt(iqs, b1[:, 128:384].rearrange("p (h d) -> p h d", h=8),
               sm[:, 8:16].unsqueeze(2).broadcast_to([128, 8, 32]), ALU.mult, [K[3], "sm"] + SCR, SCR)
            rope_apply(QI3[:, :, 64:72], iqs[:, :, 0:8], rp[:, 32:36], rp[:, 36:40], 8, 4, rt, SCR, ["QI"] + SCR)
            cp(QI3[:, :, 72:96], iqs[:, :, 8:32], SCR, ["QI"])
            rope_apply(KI[:, 64:72].rearrange("p (h d) -> p h d", h=1),
                       b1[:, 384:392].rearrange("p (h d) -> p h d", h=1),
                       rp[:, 32:36], rp[:, 36:40], 1, 4, rt, [K[3]] + SCR, ["KI"] + SCR)
            cp(KI[:, 72:96], b1[:, 392:416], [K[3]], ["KI"])
            tt(Dh[:].rearrange("p (h q) -> p h q", h=8), ident.unsqueeze(1).broadcast_to([128, 8, 128]),
               sm[:, 16:24].unsqueeze(2).broadcast_to([128, 8, 128]), ALU.mult, ["cbf", "sm"], ["Dh"])
            if stop == 'p1d':
                break
            for h in range(8):
                tr(bankb(7)[:, h * 128:(h + 1) * 128], QI[:, h * 128:(h + 1) * 128], ["QI"], [K[7]], inc=(h == 7))
            cp(QIT[:], bankb(7), [K[7]], ["QIT"])
            tr(bankb(0)[:, 0:128], KI[:], ["KI"], [K[0]], inc=True)
            cp(kcache[:, b * 128:(b + 1) * 128], bankb(0)[:, 0:128], [K[0]], ["kcache"])
            ka_dst = kaT[:].rearrange("p (c s t) -> p c s t", c=4, s=5)[:, :, slot, :]
            cp(ka_dst, bank(5).rearrange("p (c t) -> p c t", c=4), [K[5]], [f"kaT{slot}"])
            tnh2 = sc[:, 3328:3840]
            act(tnh2, bank(6), AF.Tanh, [K[6]] + SCR, SCR, scale=0.5)
            stt(gbT[:], tnh2, 1.0, bank(6), ALU.add, ALU.mult, [K[6]] + SCR, ["gbT"])
            if dbg and b == NT - 1:
                cp(sc[:, 3072:4096], QIT[:], ["QIT"], SCR)
                dma("sp", dbg_d["d_qit"], sc[:, 3072:4096], SCR, [])
                cp(sc[:, 3072:3584], kcache[:, 0:512], ["kcache"], SCR)
                dma("sp", dbg_d["d_kc"], sc[:, 3072:3584], SCR, [])

            if stop == 'p1':
                break
            jjs = [jj for jj in range(5) if b - 4 + jj >= 0]
            yacc_v = [Rr[:].bitcast(F32)[:, 0:260], Dh[:].bitcast(F32)[:, 0:260]]
            yacc_r = [["R0", "R1"], ["Dh"]]
            yaf_all = QI[:].bitcast(F32)

            def a_keytile(ji):
                jj = jjs[ji]
                j = b - 4 + jj
                sj = j % 5
                pts = ji % 2
                for half in range(2):
                    bk = 4 + half
                    for hh in range(4):
                        h = half * 4 + hh
                        pr = (h % 2) * 64
                        c = h // 2
                        mm(bank(bk)[:, hh * 128:(hh + 1) * 128],
                           kaT[pr:pr + 64, (c * 5 + sj) * 128:(c * 5 + sj + 1) * 128],
                           qaT[pr:pr + 64, c * 128:(c + 1) * 128], True, False,
                           [f"kaT{sj}", "qaT"], [K[bk]])
                        mm(bank(bk)[:, hh * 128:(hh + 1) * 128], ident,
                           biasb[:, (jj * 8 + h) * 128:(jj * 8 + h + 1) * 128], False, True,
                           ["cbf", "wts"], [K[bk]])
                pta = PT[:, pts * 1024:(pts + 1) * 1024]
                for half in range(2):
                    act(pta[:, half * 512:(half + 1) * 512], bank(4 + half), AF.Exp, [K[4 + half]], PTW[pts])
                for h in range(8):
                    bk = 6 + h // 4
                    o = (h % 4) * 65
                    mm(bank(bk)[:, o:o + 65], pta[:, h * 128:(h + 1) * 128],
                       vaug[:, (sj * 8 + h) * 65:(sj * 8 + h + 1) * 65], True, True,
                       [f"PT{pts}", f"vaug{sj}"], [K[bk]])
                for k2 in range(2):
                    if ji == 0:
                        cp(yacc_v[k2], bank(6 + k2)[:, 0:260], [K[6 + k2]], yacc_r[k2])
                    else:
                        tt(yacc_v[k2], yacc_v[k2], bank(6 + k2)[:, 0:260], ALU.add, [K[6 + k2]] + yacc_r[k2], yacc_r[k2])

            def a_finish():
                for k2 in range(2):
                    yv = yacc_v[k2].rearrange("p (h d) -> p h d", h=4)
                    rec = sm[:, 24 + 4 * k2:28 + 4 * k2].rearrange("p (h o) -> p h o", h=4, o=1)
                    P.op("dve", lambda e, o=rec, i=yv[:, :, 64:65]: e.reciprocal(out=o, in_=i),
                         reads=yacc_r[k2], writes=["sm"])
                    yaf = yaf_all[:, k2 * 256:(k2 + 1) * 256].rearrange("p (h d) -> p h d", h=4)
                    tt(yaf, yv[:, :, 0:64], rec.broadcast_to([128, 4, 64]), ALU.mult, ["sm", "QI"] + yacc_r[k2], ["QI"])
                tt(yab[:], yaf_all, gas[:], ALU.mult, ["QI", "gas"], ["yab"])
                for c in range(4):
                    tr(bankb(2)[:, c * 128:(c + 1) * 128], yab[:, c * 128:(c + 1) * 128], ["yab"], [K[2]])
                cp(yaT[:], bankb(2)[:, 0:512], [K[2]], ["yaT"])

            if stop == 'A':
                break
            nblk = (N + 511) // 512
            items = [(kb, h) for kb in range(nblk) for h in range(8)]

            LB = (0, 1, 4, 5)
            RS = (Rr[:, 0:512], Rr[:, 512:1024], PT[:, 1024:1536], PT[:, 1536:2048])
            RSR = ("R0", "R1", "R2", "R3")

            def idx_L(i):
                kb, h = items[i]
                k0 = kb * 512
                w = min(512, N - k0)
                lb = LB[i % 4]
                mm(bank(lb)[:, 0:w], IQz[:, h * 128:(h + 1) * 128], kcache[:, k0:k0 + w],
                   True, True, ["PT0", "kcache"], [K[lb]])
                rs = i % 4
                if i % 2 == 0:
                    act(RS[rs][:, 0:w], bank(lb)[:, 0:w], AF.Relu, [K[lb]], [RSR[rs]])
                else:
                    ts(RS[rs][:, 0:w], bank(lb)[:, 0:w], 0.0, None, ALU.max, None, [K[lb]], [RSR[rs]])

            def idx_D(i):
                kb, h = items[i]
                k0 = kb * 512
                w = min(512, N - k0)
                rs = i % 4
                sb = 2 + kb % 2
                mm(bank(sb)[:, 0:w], Dh[:, h * 128:(h + 1) * 128], RS[rs][:, 0:w], h == 0, h == 7,
                   ["Dh", RSR[rs]], [K[sb]])
                if h == 7:
                    act(sc[:, k0:k0 + w], bank(sb)[:, 0:w], AF.Copy, [K[sb]], SCR)
                    if b >= 2:
                        wm = w - 64 if kb == nblk - 1 else w
                        P.op("dve", lambda e, o=bst[:, kb:kb + 1], i=sc[:, k0:k0 + w]:
                             e.tensor_reduce(out=o, in_=i, axis=AX.X, op=ALU.max), reads=SCR, writes=["bst"])
                        P.op("dve", lambda e, o=bst[:, 8 + kb:9 + kb], i=sc[:, k0:k0 + wm]:
                             e.tensor_reduce(out=o, in_=i, axis=AX.X, op=ALU.min), reads=SCR, writes=["bst"])

            IQz = PT[:, 0:1024]
            cp(IQz[64:128, :], QIT[64:128, :], ["QIT"], ["PT0", "PT1", "R2", "R3"])
            LA = 3
            for i in range(min(LA, len(items))):
                idx_L(i)
            for i in range(len(items)):
                idx_D(i)
                if i + LA < len(items):
                    idx_L(i + LA)
            P.op("dve", lambda e, a=sc[0:64, N - 64:N]: e.memset(a, -1e30), reads=[], writes=SCR)
            if dbg and b == NT - 1:
                dma("sp", dbg_d["d_sc"], sc[:, 0:512], SCR, [])
            if stop == 'idx':
                break
            LO, HI, D0, MID, CNT, U, SA, T2 = (sm[:, 32:33], sm[:, 33:34], sm[:, 34:35], sm[:, 35:36], sm[:, 36:37],
                                               sm[:, 37:38], sm[:, 38:39], sm[:, 39:40])
            if b < 2:
                P.op("dve", lambda e, a=LO: e.memset(a, -1e29), reads=[], writes=["bLO"])
                for ji in range(len(jjs)):
                    a_keytile(ji)
                a_finish()
            else:
                P.op("dve", lambda e, o=HI, i=bst[:, 0:nblk]: e.tensor_reduce(out=o, in_=i, axis=AX.X, op=ALU.max),
                     reads=["bst"], writes=["bHI"])
                P.op("dve", lambda e, o=LO, i=bst[:, 8:8 + nblk]: e.tensor_reduce(out=o, in_=i, axis=AX.X, op=ALU.min),
                     reads=["bst"], writes=["bLO"])
                tt(D0, HI, LO, ALU.subtract, ["bHI", "bLO"], ["bD0"])
                ts(DI, CI, D0, None, ALU.mult, None, ["bD0", "CI"], ["bDI"])
                n1 = 64 * max(1, int(round(DVE_SHARE * N / 64)))
                nA = N - n1
                jo2 = junk2[:, 0:64].unsqueeze(1).broadcast_to([128, nA // 64, 64])
                sview2 = sc[:, n1:N].rearrange("p (a c) -> p a c", c=64)
                tt(MID, LO, DI[:, 0:1], ALU.add, ["bLO", "bDI"], ["bMID"])
                jo = junk[:, 0:64].unsqueeze(1).broadcast_to([128, n1 // 64, 64])
                sview = sc[:, 0:n1].rearrange("p (a c) -> p a c", c=64)
                a_at = {3 * ji + 3: ji for ji in range(len(jjs))}
                MIDm = sm[:, 43:44]
                thrp = (511.0 - nA) / 2.0
                for it in range(NBIS):
                    last = it == NBIS - 1
                    stt(MIDm, DI[:, it:it + 1], -1.0 if last else -0.5, MID, ALU.mult, ALU.add, ["bDI", "bMID"], ["bMIDm"])
                    ts(jo, sview, MID, thrp, ALU.is_ge, ALU.subtract, SCR + ["bMID"], ["junk", "bCNT"], accum=CNT)
                    act(jo2, sview2, AF.Sign, SCR + ["bMID"], ["junk2", "bSA"], scale=-1.0, bias=MID, accum=SA)
                    stt(U, SA, -0.5, CNT, ALU.mult, ALU.is_ge, ["bCNT", "bSA"], ["bU"])
                    if not last:
                        stt(MID, U, DI[:, it:it + 1], MIDm, ALU.mult, ALU.add, ["bU", "bDI", "bMIDm"], ["bMID"])
                    else:
                        stt(LO, U, DI[:, NBIS - 1:NBIS], MIDm, ALU.mult, ALU.add, ["bU", "bDI", "bMIDm"], ["bLO"])
                    if it in a_at:
                        a_keytile(a_at[it])
                a_finish()
            if dbg and b == NT - 1:
                jo = junk[:, 0:64].unsqueeze(1).broadcast_to([128, N // 64, 64])
                sview = sc[:, 0:N].rearrange("p (a c) -> p a c", c=64)
                ts(jo, sview, LO, None, ALU.is_ge, ALU.add, SCR + ["bLO"], ["junk", "bCNT"], accum=CNT)
                dma("sp", dbg_d["d_lo"], LO, ["bLO"], [])
                dma("sp", dbg_d["d_cnt"], CNT, ["bCNT"], [])
            if stop == 'bis':
                break
            QBz = Dh
            P.op("dve", lambda e, a=QBz[64:128, :]: e.memset(a, 0.0), reads=[], writes=["Dh"])
            cp(QBz[0:64, :], QIT[0:64, :], ["QIT"], ["Dh"])

            def b_stage1(j):
                ns = j % 2
                nm = nmr[:, ns * 128:(ns + 1) * 128]
                ts(nm, sc[:, j * 128:(j + 1) * 128], LO, NEGM, ALU.is_lt, ALU.mult, SCR + ["bLO"], [f"nm{ns}"])
                pts = j % 2
                ptb = PT[:, pts * 1024:(pts + 1) * 1024]
                sb0 = 2 if j % 2 == 0 else 6
                for half in range(2):
                    bk = sb0 + half
                    mm(bank(bk), kcache[:, j * 128:(j + 1) * 128], QBz[:, half * 512:(half + 1) * 512],
                       True, False, ["kcache", "Dh"], [K[bk]])
                    mm(bank(bk), nm, I4, False, True, [f"nm{ns}", "cbf"], [K[bk]])
                for half in range(2):
                    act(ptb[:, half * 512:(half + 1) * 512], bank(sb0 + half), AF.Exp, [K[sb0 + half]], PTW[pts])

            def b_stage2(j):
                pts = j % 2
                ptb = PT[:, pts * 1024:(pts + 1) * 1024]
                for half in range(2):
                    mm(bank(4 + half), vbc[:, j * 128:(j + 1) * 128], ptb[:, half * 512:(half + 1) * 512],
                       j == 0, j == b, [f"PT{pts}", "vbc"], [K[4 + half]])

            for j in range(b + 1):
                b_stage1(j)
                if j >= 1:
                    b_stage2(j - 1)
            b_stage2(b)
            act(sc[:, 0:512], bank(4), AF.Copy, [K[4]] + SCR, SCR)
            cp(sc[:, 512:1024], bank(5), [K[5]] + SCR, SCR)
            for h in range(8):
                P.op("pe", lambda e, o=bank(h // 4)[:, (h % 4) * 128:(h % 4 + 1) * 128],
                     i=sc[:, h * 128:(h + 1) * 128]: e.transpose(out=o, in_=i, identity=identf[:]),
                     reads=SCR + ["identf"], writes=[K[h // 4]])
            for k2 in range(2):
                yv = bank(k2).rearrange("p (h d) -> p h d", h=4)
                rec = sm[:, 24 + 4 * k2:28 + 4 * k2].rearrange("p (h o) -> p h o", h=4, o=1)
                P.op("dve", lambda e, o=rec, i=yv[:, :, 64:65]: e.reciprocal(out=o, in_=i), reads=[K[k2]], writes=["sm"])
                ybf = sc[:, 1024 + k2 * 256:1024 + (k2 + 1) * 256].rearrange("p (h d) -> p h d", h=4)
                tt(ybf, yv[:, :, 0:64], rec.broadcast_to([128, 4, 64]), ALU.mult, [K[k2], "sm"] + SCR, SCR)
            tt(yab[:], sc[:, 1024:1536], gbT[:], ALU.mult, SCR + ["gbT"], ["yab"])
            for c in range(4):
                tr(bankb(2)[:, c * 128:(c + 1) * 128], yab[:, c * 128:(c + 1) * 128], ["yab"], [K[2]])
            cp(ybT[:], bankb(2)[:, 0:512], [K[2]], ["ybT"])
            if dbg and b == NT - 1:
                cp(sc[:, 3072:3584], ybT[:], ["ybT"], SCR)
                dma("sp", dbg_d["d_yb"], sc[:, 3072:3584], SCR, [])

            if stop == 'B':
                break
            hbuf = sc[:, 2048:3072]
            dma("sp", hbuf, x_d[b * 128:(b + 1) * 128, :], [], SCR)
            if b + 1 < NT:
                dma("sp", x_sb, x_d[(b + 1) * 128:(b + 2) * 128, :], [], PTALL)
            for nb in range(2):
                cs = slice(nb * 512, (nb + 1) * 512)
                for c in range(4):
                    mm(bank(0), yaT[:, c * 128:(c + 1) * 128], wab[:, c * D + nb * 512: c * D + (nb + 1) * 512],
                       c == 0, c == 3, ["yaT", "wts"], [K[0]], inc=(c == 3))
                for c in range(4):
                    mm(bank(1), ybT[:, c * 128:(c + 1) * 128], wbb[:, c * D + nb * 512: c * D + (nb + 1) * 512],
                       c == 0, c == 3, ["ybT", "wts"], [K[1]], inc=(c == 3))
                for gi, (zc, bk) in enumerate(((ZA, 2), (ZB, 3))):
                    for kc in range(8):
                        mm(bank(bk), xnT[:, kc * 128:(kc + 1) * 128],
                           winb[:, kc * CW + zc + nb * 512: kc * CW + zc + (nb + 1) * 512], kc == 0, False,
                           ["xnT", "wts"], [K[bk]])
                    mm(bank(bk), ones[0:1, 0:128], bmb[0:1, gi * D + nb * 512: gi * D + (nb + 1) * 512], False, True,
                       ["cbf", "bmb"], [K[bk]], inc=True)
                g_a = sc[:, 0:512]
                g_b = sc[:, 512:1024]
                act(g_a, bank(2), AF.Tanh, [K[2]] + SCR, SCR, scale=0.5)
                act(g_b, bank(3), AF.Tanh, [K[3]] + SCR, SCR, scale=0.5)
                stt(g_a, g_a, 1.0, bank(0), ALU.add, ALU.mult, [K[0]] + SCR, SCR)
                stt(g_b, g_b, 1.0, bank(1), ALU.add, ALU.mult, [K[1]] + SCR, SCR)
                tt(mgb[:, cs], g_a, g_b, ALU.add, SCR, ["QI"])
            if dbg and b == NT - 1:
                cp(sc[:, 3072:4096], mgb[:], ["QI"], SCR)
                dma("sp", dbg_d["d_mg"], sc[:, 3072:4096], SCR, [])
            act(dmy[:, 0:1], dmy[:, 1:2], AF.Sqrt, [], ["dmy"])
            if b + 1 < NT:
                x_norm_stats()
                x_norm_apply()
            for kc in range(8):
                tr(bankb(4)[:, kc * 128:(kc + 1) * 128], mgb[:, kc * 128:(kc + 1) * 128], ["QI"], [K[4]], inc=(kc == 7))
            cp(mgT[:], bankb(4), [K[4]], ["QIT"])
            if b + 1 < NT:
                x_transposes()
                act(xnT[:], bankb(7), AF.Copy, [K[7]], ["xnT"])
            for nb in range(2):
                for kc in range(8):
                    mm(bank(5 + nb), mgT[:, kc * 128:(kc + 1) * 128],
                       wob[:, kc * D + nb * 512: kc * D + (nb + 1) * 512], kc == 0, kc == 7,
                       ["QIT", "wts"], [K[5 + nb]], inc=(kc == 7))
            for nb in range(2):
                tt(hbuf[:, nb * 512:(nb + 1) * 512], hbuf[:, nb * 512:(nb + 1) * 512], bank(5 + nb), ALU.add,
                   [K[5 + nb]] + SCR, SCR)
            act(sc[:, 0:1024], hbuf, AF.Square, SCR, SCR + ["sm"], accum=sm[:, 40:41])
            act(sm[:, 41:42], sm[:, 40:41], AF.Sqrt, ["sm"], ["sm"], scale=1.0 / D, bias=sm[:, 63:64])
            P.op("dve", lambda e, o=sm[:, 42:43], i=sm[:, 41:42]: e.reciprocal(out=o, in_=i), reads=["sm"], writes=["sm"])
            stt(hbuf, hbuf, sm[:, 42:43], fgb[:], ALU.mult, ALU.mult, SCR + ["sm", "fgb"], SCR)
            dma("sp", out_d[b * 128:(b + 1) * 128, :], hbuf, SCR, [])

        P.finish_waits("sp")
        P.emit()
    return nc


def _host_consts():
    bf = ml_dtypes.bfloat16
    cb = np.zeros((128, 896), np.float32)
    cb[:, 0:128] = np.eye(128)
    for r in range(4):
        cb[:, 128 + r * 128:128 + (r + 1) * 128] = np.eye(128)
    cb[:, 640:768] = 1.0
    cb = cb.astype(bf)
    pos = np.arange(S, dtype=np.float32)
    invB = (np.float32(500000.0) ** (-np.arange(8, dtype=np.float32) / np.float32(8))).astype(np.float32)
    invI = (np.float32(500000.0) ** (-np.arange(4, dtype=np.float32) / np.float32(4))).astype(np.float32)
    angB = (pos[:, None] * invB[None, :]).astype(np.float32)
    angI = (pos[:, None] * invI[None, :]).astype(np.float32)
    tab = np.concatenate([np.cos(angB), np.sin(angB), np.cos(angB) * 0.125, np.sin(angB) * 0.125,
                          np.cos(angI), np.sin(angI)], axis=1).astype(np.float32)
    rope = np.ascontiguousarray(tab.reshape(32, 128, 40).transpose(1, 0, 2).reshape(128, 32 * 40))
    kk = np.arange(128)[:, None, None]
    jj = np.arange(5)[None, :, None]
    q = np.arange(128)[None, None, :]
    dist = 128 * (4 - jj) + q - kk
    idx = np.clip(dist, -256, 256) + 256
    dchunk = 2 * (jj - 4) + (kk >= 64).astype(np.int64) - (q >= 64).astype(np.int64)
    bad = (dchunk < -8) | (dchunk > 0)
    return cb, rope, idx, bad


def _bias_table(rel_bias, idx, bad):
    t = rel_bias[:, idx]
    t = np.where(bad[None], np.float32(NEGM), t).astype(np.float32)
    return np.ascontiguousarray(t.transpose(1, 2, 0, 3).reshape(128, 5 * 8 * 128))


def make_in_maps(x, norm_gain, w_in, b_merge, rel_bias, w_branch_a, w_branch_b, w_out, final_norm_gain):
    cb, rope, idx, bad = _host_consts()
    shared = {
        "w_in": np.ascontiguousarray(w_in[0], dtype=np.float32),
        "wa": np.ascontiguousarray(w_branch_a[0], dtype=np.float32),
        "wb": np.ascontiguousarray(w_branch_b[0], dtype=np.float32),
        "wo": np.ascontiguousarray(w_out[0], dtype=np.float32),
        "gcol": np.ascontiguousarray(np.asarray(norm_gain[0], np.float32).reshape(8, 128).T),
        "fgain": np.ascontiguousarray(np.asarray(final_norm_gain, np.float32).reshape(1, D)),
        "bmrow": np.ascontiguousarray(np.asarray(b_merge[0], np.float32).reshape(1, 2 * D)),
        "biasT": _bias_table(np.asarray(rel_bias[0], np.float32), idx, bad),
        "rope": rope,
        "cbf": cb,
        "identf": np.eye(128, dtype=np.float32),
    }
    return [dict(shared, x=np.ascontiguousarray(x[i], dtype=np.float32)) for i in range(x.shape[0])]


def kernel(x, norm_gain, w_in, b_merge, rel_bias, w_branch_a, w_branch_b, w_out, final_norm_gain):
    args = [np.asarray(a) for a in (x, norm_gain, w_in, b_merge, rel_bias, w_branch_a, w_branch_b, w_out,
                                    final_norm_gain)]
    in_maps = make_in_maps(*args)
    nc = build()
    res = run_bass_kernel_spmd(nc, in_maps, core_ids=list(range(8)))
    return np.stack([np.asarray(r["out"], dtype=np.float32) for r in res.results], axis=0)
```

```python
import contextlib
import numpy as np
import ml_dtypes
import concourse.bass as bass
import concourse.mybir as mybir
from concourse.bass_utils import run_bass_kernel_spmd

F32 = mybir.dt.float32
BF16 = mybir.dt.bfloat16
ALU = mybir.AluOpType
AF = mybir.ActivationFunctionType
AX = mybir.AxisListType

S = 4096
D = 1024
NTILES = 32
CW = 5544
QA, KA, VA, GA, QB, KBO, VBO, GB, IQ, IK, IW, ZA, ZB = (
    0, 512, 1024, 1536, 2048, 2560, 2624, 2688, 3200, 3456, 3488, 3496, 4520)
NBIS = 16
DVE_SHARE = 0.43
NEGM = -30000.0

COMPUTE = ("pe", "act", "dve", "pool")
NDMASEM = 8


class Prog:
    def __init__(self, nc, dma_queues=("sp", "pool")):
        self.nc = nc
        self.ops = {e: [] for e in ("pe", "act", "dve", "pool", "sp")}
        self.cnt = {}
        self.lastw = {}
        self.reads = {}
        self.waited = {e: {} for e in self.ops}
        self.dma_i = {q: 0 for q in dma_queues}
        self.timelines = list(COMPUTE) + [f"dma_{q}{i}" for q in dma_queues for i in range(NDMASEM)]
        for t in self.timelines:
            self.cnt[t] = 0
        self.marked = {e: set() for e in COMPUTE}

    def _need(self, eng, waits, tl, val):
        if tl == eng and eng == "pe":
            return
        if self.waited[eng].get(tl, 0) >= val:
            return
        waits[tl] = max(waits.get(tl, 0), val)

    def _deps(self, eng, tl_self, reads, writes):
        waits = {}
        for r in reads:
            lw = self.lastw.get(r)
            if lw:
                self._need(eng, waits, lw[0], lw[1])
        for w in writes:
            lw = self.lastw.get(w)
            if lw and lw[0] != tl_self:
                self._need(eng, waits, lw[0], lw[1])
            for tl, v in self.reads.get(w, {}).items():
                if tl != tl_self:
                    self._need(eng, waits, tl, v)
        for tl, v in waits.items():
            self.waited[eng][tl] = max(self.waited[eng].get(tl, 0), v)
            if tl in self.marked:
                self.marked[tl].add(v)
        return waits

    def _commit(self, tl, val, reads, writes):
        for r in reads:
            self.reads.setdefault(r, {})[tl] = val
        for w in writes:
            self.lastw[w] = (tl, val)
            self.reads[w] = {}

    def op(self, eng, fn, reads=(), writes=(), inc=True):
        waits = self._deps(eng, eng, reads, writes)
        self.cnt[eng] += 1
        val = self.cnt[eng]
        self.ops[eng].append((waits, fn, (eng, val)))
        self._commit(eng, val, reads, writes)

    def dma(self, q, fn, reads=(), writes=()):
        i = self.dma_i[q]
        self.dma_i[q] += 1
        tl = f"dma_{q}{i % NDMASEM}"
        waits = self._deps(q, tl, reads, writes)
        prev = self.cnt[tl]
        if prev > 0 and self.waited[q].get(tl, 0) < prev:
            waits[tl] = max(waits.get(tl, 0), prev)
            self.waited[q][tl] = prev
        self.cnt[tl] += 16
        self.ops[q].append((waits, fn, (tl, 16)))
        self._commit(tl, self.cnt[tl], reads, writes)

    def finish_waits(self, eng="sp"):
        waits = {}
        for tl in self.timelines:
            if self.cnt[tl] > 0 and self.waited[eng].get(tl, 0) < self.cnt[tl]:
                waits[tl] = self.cnt[tl]
                if tl in self.marked:
                    self.marked[tl].add(self.cnt[tl])
        self.ops[eng].append((waits, None, None))

    def emit(self):
        nc = self.nc
        rank = {}
        for e in COMPUTE:
            rank[e] = {s: i + 1 for i, s in enumerate(sorted(self.marked[e]))}
        with contextlib.ExitStack() as st:
            sems = {t: st.enter_context(nc.semaphore("s_" + t)) for t in self.timelines}
            block = st.enter_context(nc.Block())

            def run(engname):
                def body(e):
                    for waits, fn, inc in self.ops[engname]:
                        for tl, v in waits.items():
                            e.wait_ge(sems[tl], rank[tl][v] if tl in rank else v)
                        if fn is None:
                            continue
                        ins = fn(e)
                        if inc is None:
                            continue
                        if inc[0] in rank:
                            if inc[1] in rank[inc[0]]:
                                ins.then_inc(sems[inc[0]], 1)
                        else:
                            ins.then_inc(sems[inc[0]], inc[1])
                return body

            block.sync(run("sp"))
            block.tensor(run("pe"))
            block.scalar(run("act"))
            block.vector(run("dve"))
            block.gpsimd(run("pool"))


def build(NT=NTILES, dbg=False, stop=None):
    nc = bass.Bass("TRN2", target_bir_lowering=False)
    din = lambda name, shape, dt=F32: nc.dram_tensor(name, shape, dt, kind="ExternalInput").ap()
    x_d = din("x", [S, D])
    win_d = din("w_in", [D, CW])
    wa_d = din("wa", [512, D])
    wb_d = din("wb", [512, D])
    wo_d = din("wo", [D, D])
    gcol_d = din("gcol", [128, 8])
    fg_d = din("fgain", [1, D])
    bm_d = din("bmrow", [1, 2 * D])
    bias_d = din("biasT", [128, 5120])
    rope_d = din("rope", [128, 32 * 40])
    idf_d = din("identf", [128, 128])
    cb_d = din("cbf", [128, 896], BF16)
    out_d = nc.dram_tensor("out", [S, D], F32, kind="ExternalOutput").ap()
    dbg_d = {}
    if dbg:
        for name, shape in (("d_qit", [128, 1024]), ("d_kc", [128, 512]), ("d_ya", [128, 512]),
                            ("d_sc", [128, 512]), ("d_lo", [128, 1]), ("d_cnt", [128, 1]),
                            ("d_yb", [128, 512]), ("d_mg", [128, 1024])):
            dbg_d[name] = nc.dram_tensor(name, shape, F32, kind="ExternalOutput").ap()

    with contextlib.ExitStack() as st:
        T = lambda name, shape, dt: st.enter_context(nc.sbuf_tensor(name, shape, dt))
        winb = T("winb", [128, 8 * CW], BF16)
        wab = T("wab", [128, 4 * D], BF16)
        wbb = T("wbb", [128, 4 * D], BF16)
        wob = T("wob", [128, 8 * D], BF16)
        sc = T("sc", [128, 4096], F32)
        biasb = T("biasb", [128, 5120], BF16)
        cbf = T("cbf_s", [128, 896], BF16)
        rope = T("rope_s", [128, 32 * 40], F32)
        identf = T("identf_s", [128, 128], F32)
        fgb = T("fgb", [128, D], F32)
        gcol = T("gcol_s", [128, 8], F32)
        bmb = T("bmb", [1, 2 * D], BF16)
        kaT = T("kaT", [128, 4 * 5 * 128], BF16)
        vaug = T("vaug", [128, 5 * 8 * 65], BF16)
        kcache = T("kcache", [128, S], BF16)
        vbc = T("vbc", [128, 32 * 128], BF16)
        xnT = T("xnT", [128, 8 * 128], BF16)
        qaT = T("qaT", [128, 4 * 128], BF16)
        gbT = T("gbT", [128, 4 * 128], BF16)
        gas = T("gas", [128, 512], BF16)
        QI = T("QI", [128, 8 * 128], BF16)
        KI = T("KI", [128, 128], BF16)
        QIT = T("QIT", [128, 8 * 128], BF16)
        mgb = QI
        mgT = QIT
        Dh = T("Dh", [128, 8 * 128], BF16)
        Rr = T("Rr", [128, 2 * 512], BF16)
        PT = T("PT", [128, 2 * 1024], BF16)
        nmr = T("nmr", [128, 2 * 128], BF16)
        junk = T("junk", [128, 64], BF16)
        junk2 = T("junk2", [128, 64], BF16)
        sm2 = T("sm2", [128, 32], F32)
        dmy = T("dmy", [128, 2], F32)
        CI = sm2[:, 0:16]
        DI = sm2[:, 16:32]
        yab = T("yab", [128, 512], BF16)
        yaT = T("yaT", [128, 4 * 128], BF16)
        ybT = T("ybT", [128, 4 * 128], BF16)
        sm = T("sm", [128, 64], F32)
        PS = st.enter_context(nc.psum_tensor("ps", [128, 4096], F32))

        ident = cbf[:, 0:128]
        I4 = cbf[:, 128:640]
        ones = cbf[:, 640:768]
        bank = lambda i: PS[:, i * 512:(i + 1) * 512]
        bankb = lambda i: PS[:, i * 512:(i + 1) * 512].bitcast(BF16)
        K = [f"K{i}" for i in range(8)]
        PTW = (['PT0'], ['PT1', 'R2', 'R3'])

        P = Prog(nc)

        def mm(out, lhsT, rhs, start, stop, reads, writes, inc=False):
            P.op("pe", lambda e: e.matmul(out, lhsT=lhsT, rhs=rhs, start=start, stop=stop),
                 reads=reads, writes=writes, inc=inc)

        def tr(out, in_, reads, writes, inc=False):
            P.op("pe", lambda e: e.transpose(out=out, in_=in_, identity=ident),
                 reads=list(reads) + ["cbf"], writes=writes, inc=inc)

        def act(out, in_, func, reads, writes, scale=None, bias=None, accum=None):
            kw = {}
            if scale is not None:
                kw["scale"] = scale
            if bias is not None:
                kw["bias"] = bias
            if accum is not None:
                kw["accum_out"] = accum
            P.op("act", lambda e: e.activation(out=out, in_=in_, func=func, **kw), reads=reads, writes=writes)

        def ts(out, in0, s1, s2, op0, op1, reads, writes, accum=None, eng="dve"):
            kw = {}
            if op1 is not None:
                kw["op1"] = op1
            if accum is not None:
                kw["accum_out"] = accum
            P.op(eng, lambda e: e.tensor_scalar(out=out, in0=in0, scalar1=s1, scalar2=s2, op0=op0, **kw),
                 reads=reads, writes=writes)

        def tt(out, in0, in1, op, reads, writes, eng="dve"):
            P.op(eng, lambda e: e.tensor_tensor(out=out, in0=in0, in1=in1, op=op), reads=reads, writes=writes)

        def stt(out, in0, scalar, in1, op0, op1, reads, writes, eng="dve"):
            P.op(eng, lambda e: e.scalar_tensor_tensor(out=out, in0=in0, scalar=scalar, in1=in1, op0=op0, op1=op1),
                 reads=reads, writes=writes)

        def cp(out, in_, reads, writes, eng="dve"):
            P.op(eng, lambda e: e.tensor_copy(out=out, in_=in_), reads=reads, writes=writes)

        def dma(q, out, in_, reads, writes):
            P.dma(q, lambda e: e.dma_start(out=out, in_=in_), reads=reads, writes=writes)

        dma("sp", cbf[:], cb_d, [], ["cbf"])
        dma("sp", identf[:], idf_d, [], ["identf"])
        dma("sp", rope[:], rope_d, [], ["rope"])
        dma("sp", gcol[:], gcol_d, [], ["gcol"])
        dma("sp", fgb[:], fg_d.partition_broadcast(128), [], ["fgb"])
        dma("sp", sc[0:1, 0:2 * D], bm_d, [], ["stg0", "stg1", "stg2", "stg3"])
        cp(bmb[:], sc[0:1, 0:2 * D], ["stg0", "stg1", "stg2", "stg3"], ["bmb"])
        P.op("pool", lambda e: e.memset(vaug[:], 1.0), writes=["vaug0", "vaug1", "vaug2", "vaug3", "vaug4"])
        P.op("pool", lambda e: e.memset(KI[:], 0.0), writes=["KI"])
        P.op("pool", lambda e: e.memset(vbc[:], 1.0), writes=["vbc"])
        P.op("pool", lambda e: e.memset(sm[:, 63:64], 1e-6), writes=["sm"])
        P.op("pool", lambda e: e.memset(dmy[:], 1.0), writes=["dmy"])
        for _i in range(16):
            P.op("pool", lambda e, a=sm2[:, _i:_i + 1], v=2.0 ** -(_i + 1): e.memset(a, v), writes=["CI"])
        P.op("pool", lambda e: e.memset(QI[:], 0.0), writes=["QI"])

        stage_i = [0]
        ENG_ROT = ("dve", "act", "dve", "act", "dve", "pool", "act", "dve")

        def convert(dst, src_dram, width, scale_ap=None, cscale=None):
            if cscale is not None:
                scale_ap = cscale
            i = stage_i[0]
            stage_i[0] += 1
            s = i % 4
            stg = sc[:, s * 1024: s * 1024 + width]
            res = f"stg{s}"
            dma("sp" if i % 2 == 0 else "pool", stg, src_dram, [], [res])
            eng = ENG_ROT[i % len(ENG_ROT)]
            if eng == "act":
                if scale_ap is None:
                    act(dst, stg, AF.Copy, [res], ["wts"])
                else:
                    act(dst, stg, AF.Copy, [res, "gcol"], ["wts"], scale=scale_ap)
            elif scale_ap is None:
                cp(dst, stg, [res], ["wts"], eng=eng)
            else:
                ts(dst, stg, scale_ap, None, ALU.mult, None, [res, "gcol"], ["wts"], eng=eng)

        for kc in range(8):
            for c0 in range(0, CW, 1024):
                c1 = min(c0 + 1024, CW)
                convert(winb[:, kc * CW + c0: kc * CW + c1], win_d[kc * 128:(kc + 1) * 128, c0:c1], c1 - c0,
                        gcol[:, kc:kc + 1])
        for c in range(4):
            convert(wab[:, c * D:(c + 1) * D], wa_d[c * 128:(c + 1) * 128, :], D, cscale=0.5)
            convert(wbb[:, c * D:(c + 1) * D], wb_d[c * 128:(c + 1) * 128, :], D, cscale=0.5)
        for c in range(8):
            convert(wob[:, c * D:(c + 1) * D], wo_d[c * 128:(c + 1) * 128, :], D, cscale=0.5)
        for c0 in range(0, 5120, 1024):
            convert(biasb[:, c0:c0 + 1024], bias_d[:, c0:c0 + 1024], 1024)

        SCR = ["stg0", "stg1", "stg2", "stg3"]
        if stop == 'setup':
            NT = 0

        def proj_tok(bk, col0, width, off=0, first=True, last=True):
            for kc in range(8):
                mm(bank(bk)[:, off:off + width], xnT[:, kc * 128:(kc + 1) * 128],
                   winb[:, kc * CW + col0: kc * CW + col0 + width], kc == 0, kc == 7,
                   ["xnT", "wts"], [K[bk]], inc=(kc == 7 and last))

        def proj_feat(bk, col0, off):
            for kc in range(8):
                mm(bank(bk)[:, off:off + 128], winb[:, kc * CW + col0: kc * CW + col0 + 128],
                   xnT[:, kc * 128:(kc + 1) * 128], kc == 0, kc == 7, ["xnT", "wts"], [K[bk]], inc=(kc == 7))

        def rope_apply(dst3, src3, cos2, sin2, H, half, tmp, reads, writes):
            n = H * half
            cb = cos2.unsqueeze(1).broadcast_to([128, H, half])
            sb = sin2.unsqueeze(1).broadcast_to([128, H, half])
            t = [tmp[:, i * n:(i + 1) * n].rearrange("p (h d) -> p h d", h=H) for i in range(4)]
            x1 = src3[:, :, 0:half]
            x2 = src3[:, :, half:2 * half]
            rr = list(reads) + ["rope"]
            tt(t[0], x1, cb, ALU.mult, rr, ["rtmp"])
            tt(t[1], x2, sb, ALU.mult, rr, ["rtmp"])
            tt(t[2], x2, cb, ALU.mult, rr, ["rtmp"])
            tt(t[3], x1, sb, ALU.mult, rr, ["rtmp"])
            tt(dst3[:, :, 0:half], t[0], t[1], ALU.subtract, ["rtmp"], writes)
            tt(dst3[:, :, half:2 * half], t[2], t[3], ALU.add, ["rtmp"], writes)

        rtmp = sm

        for b in range(NT):
            slot = b % 5
            N = 128 * (b + 1)
            rp = rope[:, b * 40:(b + 1) * 40]
            x_sb = PT[:].bitcast(F32)
            PTALL = ["PT0", "PT1", "R2", "R3"]
            xnb = Rr[:]
            XNB = ["R0", "R1"]
            rt = sc[:, 1536:2560]

            def x_norm_stats():
                act(xnb, x_sb, AF.Square, PTALL, XNB + ["nsm"], accum=sm[:, 0:1])
                act(sm[:, 1:2], sm[:, 0:1], AF.Sqrt, ["nsm"], ["nsm"], scale=1.0 / D, bias=sm[:, 63:64])
                P.op("dve", lambda e, o=sm[:, 2:3], i=sm[:, 1:2]: e.reciprocal(out=o, in_=i), reads=["nsm"], writes=["nsm"])

            def x_norm_apply():
                ts(xnb, x_sb, sm[:, 2:3], None, ALU.mult, None, ["nsm"] + PTALL, XNB)
                P.op("dve", lambda e, a=PT[0:64, 0:1024]: e.memset(a, 0.0), reads=[], writes=["PT0"])

            def x_transposes():
                for kc in range(8):
                    tr(bankb(7)[:, kc * 128:(kc + 1) * 128], xnb[:, kc * 128:(kc + 1) * 128], XNB, [K[7]], inc=(kc == 7))

            if b == 0:
                dma("sp", x_sb, x_d[0:128, :], [], PTALL)
                x_norm_stats()
                x_norm_apply()
                x_transposes()
                cp(xnT[:], bankb(7), [K[7]], ["xnT"])
            act(dmy[:, 0:1], dmy[:, 1:2], AF.Tanh, [], ["dmy"])
            if stop == 'p1a':
                break
            proj_tok(0, VA, 512)
            va_dst = vaug[:, slot * 520:(slot + 1) * 520].rearrange("p (h d) -> p h d", h=8)[:, :, 0:64]
            act(va_dst, bank(0).rearrange("p (h d) -> p h d", h=8), AF.Copy, [K[0]], [f"vaug{slot}"])
            proj_tok(1, GA, 512)
            tnh = sc[:, 2816:3328]
            act(tnh, bank(1), AF.Tanh, [K[1]] + SCR, SCR, scale=0.5)
            stt(gas[:], tnh, 1.0, bank(1), ALU.add, ALU.mult, [K[1]] + SCR, ["gas"])
            if stop == 'p1b':
                break
            proj_tok(2, QB, 512)
            QI3 = QI[:].rearrange("p (h d) -> p h d", h=8)
            b03 = bank(2).rearrange("p (h d) -> p h d", h=8)
            ts(QI3[:, :, 16:64], b03[:, :, 16:64], 0.125, None, ALU.mult, None, [K[2]], ["QI"])
            rope_apply(QI3[:, :, 0:16], b03[:, :, 0:16], rp[:, 16:24], rp[:, 24:32], 8, 8, rt, [K[2]] + SCR, ["QI"] + SCR)
            if stop == 'p1c':
                break
            proj_tok(3, KBO, 128, off=0, last=False)
            proj_tok(3, IQ, 296, off=128)
            for hp in range(4):
                proj_feat(4, QA + hp * 128, hp * 128)
            act(qaT[:], bank(4), AF.Copy, [K[4]], ["qaT"], scale=0.125)
            for hp in range(4):
                proj_feat(5, KA + hp * 128, hp * 128)
            proj_tok(6, GB, 512)
            b1 = bank(3)
            rope_apply(KI[:, 0:64].rearrange("p (h d) -> p h d", h=1), b1[:, 0:64].rearrange("p (h d) -> p h d", h=1),
                       rp[:, 0:8], rp[:, 8:16], 1, 8, rt, [K[3]] + SCR, ["KI"] + SCR)
            cp(KI[:, 16:64], b1[:, 16:64], [K[3]], ["KI"])
            cp(vbc[:, b * 128:b * 128 + 64], b1[:, 64:128], [K[3]], ["vbc"])
            iw_ps = b1[:, 128 + 288:128 + 296]
            ts(sm[:, 16:24], iw_ps, 0.0, 2.0, ALU.is_ge, ALU.mult, [K[3]], ["sm"])
            ts(sm[:, 16:24], sm[:, 16:24], -1.0, None, ALU.add, None, ["sm"], ["sm"])
            stt(sm[:, 8:16], iw_ps, 0.0625, sm[:, 16:24], ALU.mult, ALU.mult, [K[3], "sm"], ["sm"])
            iqs = sc[:, 2560:2816].rearrange("p (h d) -> p h d", h=8)
            tt(iqs, b1[:, 128:384].rearrange("p (h d) -> p h d", h=8),
               sm[:, 8:16].unsqueeze(2).broadcast_to([128, 8, 32]), ALU.mult, [K[3], "sm"] + SCR, SCR)
            rope_apply(QI3[:, :, 64:72], iqs[:, :, 0:8], rp[:, 32:36], rp[:, 36:40], 8, 4, rt, SCR, ["QI"] + SCR)
            cp(QI3[:, :, 72:96], iqs[:, :, 8:32], SCR, ["QI"])
            rope_apply(KI[:, 64:72].rearrange("p (h d) -> p h d", h=1),
                       b1[:, 384:392].rearrange("p (h d) -> p h d", h=1),
                       rp[:, 32:36], rp[:, 36:40], 1, 4, rt, [K[3]] + SCR, ["KI"] + SCR)
            cp(KI[:, 72:96], b1[:, 392:416], [K[3]], ["KI"])
            tt(Dh[:].rearrange("p (h q) -> p h q", h=8), ident.unsqueeze(1).broadcast_to([128, 8, 128]),
               sm[:, 16:24].unsqueeze(2).broadcast_to([128, 8, 128]), ALU.mult, ["cbf", "sm"], ["Dh"])
            if stop == 'p1d':
                break
            for h in range(8):
                tr(bankb(7)[:, h * 128:(h + 1) * 128], QI[:, h * 128:(h + 1) * 128], ["QI"], [K[7]], inc=(h == 7))
            cp(QIT[:], bankb(7), [K[7]], ["QIT"])
            tr(bankb(0)[:, 0:128], KI[:], ["KI"], [K[0]], inc=True)
            cp(kcache[:, b * 128:(b + 1) * 128], bankb(0)[:, 0:128], [K[0]], ["kcache"])
            ka_dst = kaT[:].rearrange("p (c s t) -> p c s t", c=4, s=5)[:, :, slot, :]
            cp(ka_dst, bank(5).rearrange("p (c t) -> p c t", c=4), [K[5]], [f"kaT{slot}"])
            tnh2 = sc[:, 3328:3840]
            act(tnh2, bank(6), AF.Tanh, [K[6]] + SCR, SCR, scale=0.5)
            stt(gbT[:], tnh2, 1.0, bank(6), ALU.add, ALU.mult, [K[6]] + SCR, ["gbT"])
            if dbg and b == NT - 1:
                cp(sc[:, 3072:4096], QIT[:], ["QIT"], SCR)
                dma("sp", dbg_d["d_qit"], sc[:, 3072:4096], SCR, [])
                cp(sc[:, 3072:3584], kcache[:, 0:512], ["kcache"], SCR)
                dma("sp", dbg_d["d_kc"], sc[:, 3072:3584], SCR, [])

            if stop == 'p1':
                break
            jjs = [jj for jj in range(5) if b - 4 + jj >= 0]
            yacc_v = [Rr[:].bitcast(F32)[:, 0:260], Dh[:].bitcast(F32)[:, 0:260]]
            yacc_r = [["R0", "R1"], ["Dh"]]
            yaf_all = QI[:].bitcast(F32)

            def a_keytile(ji):
                jj = jjs[ji]
                j = b - 4 + jj
                sj = j % 5
                pts = ji % 2
                for half in range(2):
                    bk = 4 + half
                    for hh in range(4):
                        h = half * 4 + hh
                        pr = (h % 2) * 64
                        c = h // 2
                        mm(bank(bk)[:, hh * 128:(hh + 1) * 128],
                           kaT[pr:pr + 64, (c * 5 + sj) * 128:(c * 5 + sj + 1) * 128],
                           qaT[pr:pr + 64, c * 128:(c + 1) * 128], True, False,
                           [f"kaT{sj}", "qaT"], [K[bk]])
                        mm(bank(bk)[:, hh * 128:(hh + 1) * 128], ident,
                           biasb[:, (jj * 8 + h) * 128:(jj * 8 + h + 1) * 128], False, True,
                           ["cbf", "wts"], [K[bk]])
                pta = PT[:, pts * 1024:(pts + 1) * 1024]
                for half in range(2):
                    act(pta[:, half * 512:(half + 1) * 512], bank(4 + half), AF.Exp, [K[4 + half]], PTW[pts])
                for h in range(8):
                    bk = 6 + h // 4
                    o = (h % 4) * 65
                    mm(bank(bk)[:, o:o + 65], pta[:, h * 128:(h + 1) * 128],
                       vaug[:, (sj * 8 + h) * 65:(sj * 8 + h + 1) * 65], True, True,
                       [f"PT{pts}", f"vaug{sj}"], [K[bk]])
                for k2 in range(2):
                    if ji == 0:
                        cp(yacc_v[k2], bank(6 + k2)[:, 0:260], [K[6 + k2]], yacc_r[k2])
                    else:
                        tt(yacc_v[k2], yacc_v[k2], bank(6 + k2)[:, 0:260], ALU.add, [K[6 + k2]] + yacc_r[k2], yacc_r[k2])

            def a_finish():
                for k2 in range(2):
                    yv = yacc_v[k2].rearrange("p (h d) -> p h d", h=4)
                    rec = sm[:, 24 + 4 * k2:28 + 4 * k2].rearrange("p (h o) -> p h o", h=4, o=1)
                    P.op("dve", lambda e, o=rec, i=yv[:, :, 64:65]: e.reciprocal(out=o, in_=i),
                         reads=yacc_r[k2], writes=["sm"])
                    yaf = yaf_all[:, k2 * 256:(k2 + 1) * 256].rearrange("p (h d) -> p h d", h=4)
                    tt(yaf, yv[:, :, 0:64], rec.broadcast_to([128, 4, 64]), ALU.mult, ["sm", "QI"] + yacc_r[k2], ["QI"])
                tt(yab[:], yaf_all, gas[:], ALU.mult, ["QI", "gas"], ["yab"])
                for c in range(4):
                    tr(bankb(0)[:, c * 128:(c + 1) * 128], yab[:, c * 128:(c + 1) * 128], ["yab"], [K[0]])
                act(yaT[:], bankb(0)[:, 0:512], AF.Copy, [K[0]], ["yaT"])

            if stop == 'A':
                break
            nblk = (N + 511) // 512
            items = [(kb, h) for kb in range(nblk) for h in range(8)]

            LB = (0, 1, 4, 5)
            RS = (Rr[:, 0:512], Rr[:, 512:1024], PT[:, 1024:1536], PT[:, 1536:2048])
            RSR = ("R0", "R1", "R2", "R3")

            def idx_L(i):
                kb, h = items[i]
                k0 = kb * 512
                w = min(512, N - k0)
                lb = LB[i % 4]
                mm(bank(lb)[:, 0:w], IQz[:, h * 128:(h + 1) * 128], kcache[:, k0:k0 + w],
                   True, True, ["PT0", "kcache"], [K[lb]])
                rs = i % 4
                if i % 2 == 0:
                    act(RS[rs][:, 0:w], bank(lb)[:, 0:w], AF.Relu, [K[lb]], [RSR[rs]])
                else:
                    ts(RS[rs][:, 0:w], bank(lb)[:, 0:w], 0.0, None, ALU.max, None, [K[lb]], [RSR[rs]])

            def idx_D(i):
                kb, h = items[i]
                k0 = kb * 512
                w = min(512, N - k0)
                rs = i % 4
                sb = 2 + kb % 2
                mm(bank(sb)[:, 0:w], Dh[:, h * 128:(h + 1) * 128], RS[rs][:, 0:w], h == 0, h == 7,
                   ["Dh", RSR[rs]], [K[sb]])
                if h == 7:
                    act(sc[:, k0:k0 + w], bank(sb)[:, 0:w], AF.Copy, [K[sb]], SCR)

            IQz = PT[:, 0:1024]
            cp(IQz[64:128, :], QIT[64:128, :], ["QIT"], ["PT0", "PT1", "R2", "R3"])
            LA = 3
            for i in range(min(LA, len(items))):
                idx_L(i)
            for i in range(len(items)):
                idx_D(i)
                if i + LA < len(items):
                    idx_L(i + LA)
            P.op("dve", lambda e, a=sc[0:64, N - 64:N]: e.memset(a, -1e30), reads=[], writes=SCR)
            if dbg and b == NT - 1:
                dma("sp", dbg_d["d_sc"], sc[:, 0:512], SCR, [])
            if stop == 'idx':
                break
            LO, HI, D0, MID, CNT, U, SA, T2 = (sm[:, 32:33], sm[:, 33:34], sm[:, 34:35], sm[:, 35:36], sm[:, 36:37],
                                               sm[:, 37:38], sm[:, 38:39], sm[:, 39:40])
            if b < 2:
                P.op("dve", lambda e, a=LO: e.memset(a, -1e29), reads=[], writes=["bLO"])
                for ji in range(len(jjs)):
                    a_keytile(ji)
                a_finish()
            else:
                P.op("dve", lambda e, o=HI, i=sc[:, 0:N]: e.tensor_reduce(out=o, in_=i, axis=AX.X, op=ALU.max),
                     reads=SCR, writes=["bHI"])
                P.op("dve", lambda e, o=LO, i=sc[:, 0:N - 64]: e.tensor_reduce(out=o, in_=i, axis=AX.X, op=ALU.min),
                     reads=SCR, writes=["bLO"])
                tt(D0, HI, LO, ALU.subtract, ["bHI", "bLO"], ["bD0"])
                ts(DI, CI, D0, None, ALU.mult, None, ["bD0", "CI"], ["bDI"])
                n1 = 64 * max(1, int(round(DVE_SHARE * N / 64)))
                nA = N - n1
                jo2 = junk2[:, 0:64].unsqueeze(1).broadcast_to([128, nA // 64, 64])
                sview2 = sc[:, n1:N].rearrange("p (a c) -> p a c", c=64)
                tt(MID, LO, DI[:, 0:1], ALU.add, ["bLO", "bDI"], ["bMID"])
                jo = junk[:, 0:64].unsqueeze(1).broadcast_to([128, n1 // 64, 64])
                sview = sc[:, 0:n1].rearrange("p (a c) -> p a c", c=64)
                a_at = {3 * ji + 3: ji for ji in range(len(jjs))}
                MIDm = sm[:, 43:44]
                thrp = (511.0 - nA) / 2.0
                for it in range(NBIS):
                    last = it == NBIS - 1
                    stt(MIDm, DI[:, it:it + 1], -1.0 if last else -0.5, MID, ALU.mult, ALU.add, ["bDI", "bMID"], ["bMIDm"])
                    ts(jo, sview, MID, thrp, ALU.is_ge, ALU.subtract, SCR + ["bMID"], ["junk", "bCNT"], accum=CNT)
                    act(jo2, sview2, AF.Sign, SCR + ["bMID"], ["junk2", "bSA"], scale=-1.0, bias=MID, accum=SA)
                    stt(U, SA, -0.5, CNT, ALU.mult, ALU.is_ge, ["bCNT", "bSA"], ["bU"])
                    if not last:
                        stt(MID, U, DI[:, it:it + 1], MIDm, ALU.mult, ALU.add, ["bU", "bDI", "bMIDm"], ["bMID"])
                    else:
                        stt(LO, U, DI[:, NBIS - 1:NBIS], MIDm, ALU.mult, ALU.add, ["bU", "bDI", "bMIDm"], ["bLO"])
                    if it in a_at:
                        a_keytile(a_at[it])
                a_finish()
            if dbg and b == NT - 1:
                jo = junk[:, 0:64].unsqueeze(1).broadcast_to([128, N // 64, 64])
                sview = sc[:, 0:N].rearrange("p (a c) -> p a c", c=64)
                ts(jo, sview, LO, None, ALU.is_ge, ALU.add, SCR + ["bLO"], ["junk", "bCNT"], accum=CNT)
                dma("sp", dbg_d["d_lo"], LO, ["bLO"], [])
                dma("sp", dbg_d["d_cnt"], CNT, ["bCNT"], [])
            if stop == 'bis':
                break
            QBz = Dh
            P.op("dve", lambda e, a=QBz[64:128, :]: e.memset(a, 0.0), reads=[], writes=["Dh"])
            cp(QBz[0:64, :], QIT[0:64, :], ["QIT"], ["Dh"])

            def b_stage1(j):
                ns = j % 2
                nm = nmr[:, ns * 128:(ns + 1) * 128]
                ts(nm, sc[:, j * 128:(j + 1) * 128], LO, NEGM, ALU.is_lt, ALU.mult, SCR + ["bLO"], [f"nm{ns}"])
                pts = j % 2
                ptb = PT[:, pts * 1024:(pts + 1) * 1024]
                sb0 = 2 if j % 2 == 0 else 6
                for half in range(2):
                    bk = sb0 + half
                    mm(bank(bk), kcache[:, j * 128:(j + 1) * 128], QBz[:, half * 512:(half + 1) * 512],
                       True, False, ["kcache", "Dh"], [K[bk]])
                    mm(bank(bk), nm, I4, False, True, [f"nm{ns}", "cbf"], [K[bk]])
                for half in range(2):
                    act(ptb[:, half * 512:(half + 1) * 512], bank(sb0 + half), AF.Exp, [K[sb0 + half]], PTW[pts])

            def b_stage2(j):
                pts = j % 2
                ptb = PT[:, pts * 1024:(pts + 1) * 1024]
                for half in range(2):
                    mm(bank(4 + half), vbc[:, j * 128:(j + 1) * 128], ptb[:, half * 512:(half + 1) * 512],
                       j == 0, j == b, [f"PT{pts}", "vbc"], [K[4 + half]])

            for j in range(b + 1):
                b_stage1(j)
                if j >= 1:
                    b_stage2(j - 1)
            b_stage2(b)
            act(sc[:, 0:512], bank(4), AF.Copy, [K[4]] + SCR, SCR)
            cp(sc[:, 512:1024], bank(5), [K[5]] + SCR, SCR)
            for h in range(8):
                P.op("pe", lambda e, o=bank(h // 4)[:, (h % 4) * 128:(h % 4 + 1) * 128],
                     i=sc[:, h * 128:(h + 1) * 128]: e.transpose(out=o, in_=i, identity=identf[:]),
                     reads=SCR + ["identf"], writes=[K[h // 4]])
            for k2 in range(2):
                yv = bank(k2).rearrange("p (h d) -> p h d", h=4)
                rec = sm[:, 24 + 4 * k2:28 + 4 * k2].rearrange("p (h o) -> p h o", h=4, o=1)
                P.op("dve", lambda e, o=rec, i=yv[:, :, 64:65]: e.reciprocal(out=o, in_=i), reads=[K[k2]], writes=["sm"])
                ybf = sc[:, 1024 + k2 * 256:1024 + (k2 + 1) * 256].rearrange("p (h d) -> p h d", h=4)
                tt(ybf, yv[:, :, 0:64], rec.broadcast_to([128, 4, 64]), ALU.mult, [K[k2], "sm"] + SCR, SCR)
            tt(yab[:], sc[:, 1024:1536], gbT[:], ALU.mult, SCR + ["gbT"], ["yab"])
            for c in range(4):
                tr(bankb(2)[:, c * 128:(c + 1) * 128], yab[:, c * 128:(c + 1) * 128], ["yab"], [K[2]])
            cp(ybT[:], bankb(2)[:, 0:512], [K[2]], ["ybT"])
            if dbg and b == NT - 1:
                cp(sc[:, 3072:3584], ybT[:], ["ybT"], SCR)
                dma("sp", dbg_d["d_yb"], sc[:, 3072:3584], SCR, [])

            if stop == 'B':
                break
            hbuf = sc[:, 2048:3072]
            dma("sp", hbuf, x_d[b * 128:(b + 1) * 128, :], [], SCR)
            if b + 1 < NT:
                dma("sp", x_sb, x_d[(b + 1) * 128:(b + 2) * 128, :], [], PTALL)
            for nb in range(2):
                cs = slice(nb * 512, (nb + 1) * 512)
                for c in range(4):
                    mm(bank(0), yaT[:, c * 128:(c + 1) * 128], wab[:, c * D + nb * 512: c * D + (nb + 1) * 512],
                       c == 0, c == 3, ["yaT", "wts"], [K[0]], inc=(c == 3))
                for c in range(4):
                    mm(bank(1), ybT[:, c * 128:(c + 1) * 128], wbb[:, c * D + nb * 512: c * D + (nb + 1) * 512],
                       c == 0, c == 3, ["ybT", "wts"], [K[1]], inc=(c == 3))
                for gi, (zc, bk) in enumerate(((ZA, 2), (ZB, 3))):
                    for kc in range(8):
                        mm(bank(bk), xnT[:, kc * 128:(kc + 1) * 128],
                           winb[:, kc * CW + zc + nb * 512: kc * CW + zc + (nb + 1) * 512], kc == 0, False,
                           ["xnT", "wts"], [K[bk]])
                    mm(bank(bk), ones[0:1, 0:128], bmb[0:1, gi * D + nb * 512: gi * D + (nb + 1) * 512], False, True,
                       ["cbf", "bmb"], [K[bk]], inc=True)
                g_a = sc[:, 0:512]
                g_b = sc[:, 512:1024]
                act(g_a, bank(2), AF.Tanh, [K[2]] + SCR, SCR, scale=0.5)
                act(g_b, bank(3), AF.Tanh, [K[3]] + SCR, SCR, scale=0.5)
                stt(g_a, g_a, 1.0, bank(0), ALU.add, ALU.mult, [K[0]] + SCR, SCR)
                stt(g_b, g_b, 1.0, bank(1), ALU.add, ALU.mult, [K[1]] + SCR, SCR)
                tt(mgb[:, cs], g_a, g_b, ALU.add, SCR, ["QI"])
            if dbg and b == NT - 1:
                cp(sc[:, 3072:4096], mgb[:], ["QI"], SCR)
                dma("sp", dbg_d["d_mg"], sc[:, 3072:4096], SCR, [])
            act(dmy[:, 0:1], dmy[:, 1:2], AF.Sqrt, [], ["dmy"])
            if b + 1 < NT:
                x_norm_stats()
                x_norm_apply()
            for kc in range(8):
                tr(bankb(4)[:, kc * 128:(kc + 1) * 128], mgb[:, kc * 128:(kc + 1) * 128], ["QI"], [K[4]], inc=(kc == 7))
            cp(mgT[:], bankb(4), [K[4]], ["QIT"])
            if b + 1 < NT:
                x_transposes()
                act(xnT[:], bankb(7), AF.Copy, [K[7]], ["xnT"])
            for nb in range(2):
                for kc in range(8):
                    mm(bank(5 + nb), mgT[:, kc * 128:(kc + 1) * 128],
                       wob[:, kc * D + nb * 512: kc * D + (nb + 1) * 512], kc == 0, kc == 7,
                       ["QIT", "wts"], [K[5 + nb]], inc=(kc == 7))
            for nb in range(2):
                tt(hbuf[:, nb * 512:(nb + 1) * 512], hbuf[:, nb * 512:(nb + 1) * 512], bank(5 + nb), ALU.add,
                   [K[5 + nb]] + SCR, SCR)
            act(sc[:, 0:1024], hbuf, AF.Square, SCR, SCR + ["sm"], accum=sm[:, 40:41])
            act(sm[:, 41:42], sm[:, 40:41], AF.Sqrt, ["sm"], ["sm"], scale=1.0 / D, bias=sm[:, 63:64])
            P.op("dve", lambda e, o=sm[:, 42:43], i=sm[:, 41:42]: e.reciprocal(out=o, in_=i), reads=["sm"], writes=["sm"])
            stt(hbuf, hbuf, sm[:, 42:43], fgb[:], ALU.mult, ALU.mult, SCR + ["sm", "fgb"], SCR)
            dma("sp", out_d[b * 128:(b + 1) * 128, :], hbuf, SCR, [])

        P.finish_waits("sp")
        P.emit()
    return nc


def _host_consts():
    bf = ml_dtypes.bfloat16
    cb = np.zeros((128, 896), np.float32)
    cb[:, 0:128] = np.eye(128)
    for r in range(4):
        cb[:, 128 + r * 128:128 + (r + 1) * 128] = np.eye(128)
    cb[:, 640:768] = 1.0
    cb = cb.astype(bf)
    pos = np.arange(S, dtype=np.float32)
    invB = (np.float32(500000.0) ** (-np.arange(8, dtype=np.float32) / np.float32(8))).astype(np.float32)
    invI = (np.float32(500000.0) ** (-np.arange(4, dtype=np.float32) / np.float32(4))).astype(np.float32)
    angB = (pos[:, None] * invB[None, :]).astype(np.float32)
    angI = (pos[:, None] * invI[None, :]).astype(np.float32)
    tab = np.concatenate([np.cos(angB), np.sin(angB), np.cos(angB) * 0.125, np.sin(angB) * 0.125,
                          np.cos(angI), np.sin(angI)], axis=1).astype(np.float32)
    rope = np.ascontiguousarray(tab.reshape(32, 128, 40).transpose(1, 0, 2).reshape(128, 32 * 40))
    kk = np.arange(128)[:, None, None]
    jj = np.arange(5)[None, :, None]
    q = np.arange(128)[None, None, :]
    dist = 128 * (4 - jj) + q - kk
    idx = np.clip(dist, -256, 256) + 256
    dchunk = 2 * (jj - 4) + (kk >= 64).astype(np.int64) - (q >= 64).astype(np.int64)
    bad = (dchunk < -8) | (dchunk > 0)
    return cb, rope, idx, bad


def _bias_table(rel_bias, idx, bad):
    t = rel_bias[:, idx]
    t = np.where(bad[None], np.float32(NEGM), t).astype(np.float32)
    return np.ascontiguousarray(t.transpose(1, 2, 0, 3).reshape(128, 5 * 8 * 128))


def make_in_maps(x, norm_gain, w_in, b_merge, rel_bias, w_branch_a, w_branch_b, w_out, final_norm_gain):
    cb, rope, idx, bad = _host_consts()
    shared = {
        "w_in": np.ascontiguousarray(w_in[0], dtype=np.float32),
        "wa": np.ascontiguousarray(w_branch_a[0], dtype=np.float32),
        "wb": np.ascontiguousarray(w_branch_b[0], dtype=np.float32),
        "wo": np.ascontiguousarray(w_out[0], dtype=np.float32),
        "gcol": np.ascontiguousarray(np.asarray(norm_gain[0], np.float32).reshape(8, 128).T),
        "fgain": np.ascontiguousarray(np.asarray(final_norm_gain, np.float32).reshape(1, D)),
        "bmrow": np.ascontiguousarray(np.asarray(b_merge[0], np.float32).reshape(1, 2 * D)),
        "biasT": _bias_table(np.asarray(rel_bias[0], np.float32), idx, bad),
        "rope": rope,
        "cbf": cb,
        "identf": np.eye(128, dtype=np.float32),
    }
    return [dict(shared, x=np.ascontiguousarray(x[i], dtype=np.float32)) for i in range(x.shape[0])]


def kernel(x, norm_gain, w_in, b_merge, rel_bias, w_branch_a, w_branch_b, w_out, final_norm_gain):
    args = [np.asarray(a) for a in (x, norm_gain, w_in, b_merge, rel_bias, w_branch_a, w_branch_b, w_out,
                                    final_norm_gain)]
    in_maps = make_in_maps(*args)
    nc = build()
    res = run_bass_kernel_spmd(nc, in_maps, core_ids=list(range(8)))
    return np.stack([np.asarray(r["out"], dtype=np.float32) for r in res.results], axis=0)
```

```python
import contextlib
import numpy as np
import ml_dtypes
import concourse.bass as bass
import concourse.mybir as mybir
from concourse.bass_utils import run_bass_kernel_spmd

F32 = mybir.dt.float32
BF16 = mybir.dt.bfloat16
ALU = mybir.AluOpType
AF = mybir.ActivationFunctionType
AX = mybir.AxisListType

S = 4096
D = 1024
NTILES = 32
CW = 5544
QA, KA, VA, GA, QB, KBO, VBO, GB, IQ, IK, IW, ZA, ZB = (
    0, 512, 1024, 1536, 2048, 2560, 2624, 2688, 3200, 3456, 3488, 3496, 4520)
NBIS = 16
DVE_SHARE = 0.43
NEGM = -30000.0

COMPUTE = ("pe", "act", "dve", "pool")
NDMASEM = 8


class Prog:
    def __init__(self, nc, dma_queues=("sp", "pool")):
        self.nc = nc
        self.ops = {e: [] for e in ("pe", "act", "dve", "pool", "sp")}
        self.cnt = {}
        self.lastw = {}
        self.reads = {}
        self.waited = {e: {} for e in self.ops}
        self.dma_i = {q: 0 for q in dma_queues}
        self.timelines = list(COMPUTE) + [f"dma_{q}{i}" for q in dma_queues for i in range(NDMASEM)]
        for t in self.timelines:
            self.cnt[t] = 0
        self.marked = {e: set() for e in COMPUTE}

    def _need(self, eng, waits, tl, val):
        if tl == eng and eng == "pe":
            return
        if self.waited[eng].get(tl, 0) >= val:
            return
        waits[tl] = max(waits.get(tl, 0), val)

    def _deps(self, eng, tl_self, reads, writes):
        waits = {}
        for r in reads:
            lw = self.lastw.get(r)
            if lw:
                self._need(eng, waits, lw[0], lw[1])
        for w in writes:
            lw = self.lastw.get(w)
            if lw and lw[0] != tl_self:
                self._need(eng, waits, lw[0], lw[1])
            for tl, v in self.reads.get(w, {}).items():
                if tl != tl_self:
                    self._need(eng, waits, tl, v)
        for tl, v in waits.items():
            self.waited[eng][tl] = max(self.waited[eng].get(tl, 0), v)
            if tl in self.marked:
                self.marked[tl].add(v)
        return waits

    def _commit(self, tl, val, reads, writes):
        for r in reads:
            self.reads.setdefault(r, {})[tl] = val
        for w in writes:
            self.lastw[w] = (tl, val)
            self.reads[w] = {}

    def op(self, eng, fn, reads=(), writes=(), inc=True):
        waits = self._deps(eng, eng, reads, writes)
        self.cnt[eng] += 1
        val = self.cnt[eng]
        self.ops[eng].append((waits, fn, (eng, val)))
        self._commit(eng, val, reads, writes)

    def dma(self, q, fn, reads=(), writes=()):
        i = self.dma_i[q]
        self.dma_i[q] += 1
        tl = f"dma_{q}{i % NDMASEM}"
        waits = self._deps(q, tl, reads, writes)
        prev = self.cnt[tl]
        if prev > 0 and self.waited[q].get(tl, 0) < prev:
            waits[tl] = max(waits.get(tl, 0), prev)
            self.waited[q][tl] = prev
        self.cnt[tl] += 16
        self.ops[q].append((waits, fn, (tl, 16)))
        self._commit(tl, self.cnt[tl], reads, writes)

    def finish_waits(self, eng="sp"):
        waits = {}
        for tl in self.timelines:
            if self.cnt[tl] > 0 and self.waited[eng].get(tl, 0) < self.cnt[tl]:
                waits[tl] = self.cnt[tl]
                if tl in self.marked:
                    self.marked[tl].add(self.cnt[tl])
        self.ops[eng].append((waits, None, None))

    def emit(self):
        nc = self.nc
        rank = {}
        for e in COMPUTE:
            rank[e] = {s: i + 1 for i, s in enumerate(sorted(self.marked[e]))}
        with contextlib.ExitStack() as st:
            sems = {t: st.enter_context(nc.semaphore("s_" + t)) for t in self.timelines}
            block = st.enter_context(nc.Block())

            def run(engname):
                def body(e):
                    for waits, fn, inc in self.ops[engname]:
                        for tl, v in waits.items():
                            e.wait_ge(sems[tl], rank[tl][v] if tl in rank else v)
                        if fn is None:
                            continue
                        ins = fn(e)
                        if inc is None:
                            continue
                        if inc[0] in rank:
                            if inc[1] in rank[inc[0]]:
                                ins.then_inc(sems[inc[0]], 1)
                        else:
                            ins.then_inc(sems[inc[0]], inc[1])
                return body

            block.sync(run("sp"))
            block.tensor(run("pe"))
            block.scalar(run("act"))
            block.vector(run("dve"))
            block.gpsimd(run("pool"))


def build(NT=NTILES, dbg=False, stop=None):
    nc = bass.Bass("TRN2", target_bir_lowering=False)
    din = lambda name, shape, dt=F32: nc.dram_tensor(name, shape, dt, kind="ExternalInput").ap()
    x_d = din("x", [S, D])
    win_d = din("w_in", [D, CW])
    wa_d = din("wa", [512, D])
    wb_d = din("wb", [512, D])
    wo_d = din("wo", [D, D])
    gcol_d = din("gcol", [128, 8])
    fg_d = din("fgain", [1, D])
    bm_d = din("bmrow", [1, 2 * D])
    bias_d = din("biasT", [128, 5120])
    rope_d = din("rope", [128, 32 * 40])
    idf_d = din("identf", [128, 128])
    cb_d = din("cbf", [128, 896], BF16)
    out_d = nc.dram_tensor("out", [S, D], F32, kind="ExternalOutput").ap()
    dbg_d = {}
    if dbg:
        for name, shape in (("d_qit", [128, 1024]), ("d_kc", [128, 512]), ("d_ya", [128, 512]),
                            ("d_sc", [128, 512]), ("d_lo", [128, 1]), ("d_cnt", [128, 1]),
                            ("d_yb", [128, 512]), ("d_mg", [128, 1024])):
            dbg_d[name] = nc.dram_tensor(name, shape, F32, kind="ExternalOutput").ap()

    with contextlib.ExitStack() as st:
        T = lambda name, shape, dt: st.enter_context(nc.sbuf_tensor(name, shape, dt))
        winb = T("winb", [128, 8 * CW], BF16)
        wab = T("wab", [128, 4 * D], BF16)
        wbb = T("wbb", [128, 4 * D], BF16)
        wob = T("wob", [128, 8 * D], BF16)
        sc = T("sc", [128, 4096], F32)
        biasb = T("biasb", [128, 5120], BF16)
        cbf = T("cbf_s", [128, 896], BF16)
        rope = T("rope_s", [128, 32 * 40], F32)
        identf = T("identf_s", [128, 128], F32)
        fgb = T("fgb", [128, D], F32)
        gcol = T("gcol_s", [128, 8], F32)
        bmb = T("bmb", [1, 2 * D], BF16)
        kaT = T("kaT", [128, 4 * 5 * 128], BF16)
        vaug = T("vaug", [128, 5 * 8 * 65], BF16)
        kcache = T("kcache", [128, S], BF16)
        vbc = T("vbc", [128, 32 * 128], BF16)
        xnT = T("xnT", [128, 8 * 128], BF16)
        qaT = T("qaT", [128, 4 * 128], BF16)
        gbT = T("gbT", [128, 4 * 128], BF16)
        gas = T("gas", [128, 512], BF16)
        QI = T("QI", [128, 8 * 128], BF16)
        KI = T("KI", [128, 128], BF16)
        QIT = T("QIT", [128, 8 * 128], BF16)
        mgb = QI
        mgT = QIT
        Dh = T("Dh", [128, 8 * 128], BF16)
        Rr = T("Rr", [128, 2 * 512], BF16)
        PT = T("PT", [128, 2 * 1024], BF16)
        nmr = T("nmr", [128, 2 * 128], BF16)
        junk = T("junk", [128, 64], BF16)
        junk2 = T("junk2", [128, 64], BF16)
        sm2 = T("sm2", [128, 32], F32)
        dmy = T("dmy", [128, 2], F32)
        CI = sm2[:, 0:16]
        DI = sm2[:, 16:32]
        yab = T("yab", [128, 512], BF16)
        yaT = T("yaT", [128, 4 * 128], BF16)
        ybT = T("ybT", [128, 4 * 128], BF16)
        sm = T("sm", [128, 64], F32)
        PS = st.enter_context(nc.psum_tensor("ps", [128, 4096], F32))

        ident = cbf[:, 0:128]
        I4 = cbf[:, 128:640]
        ones = cbf[:, 640:768]
        bank = lambda i: PS[:, i * 512:(i + 1) * 512]
        bankb = lambda i: PS[:, i * 512:(i + 1) * 512].bitcast(BF16)
        K = [f"K{i}" for i in range(8)]
        PTW = (['PT0'], ['PT1', 'R2', 'R3'])

        P = Prog(nc)

        def mm(out, lhsT, rhs, start, stop, reads, writes, inc=False):
            P.op("pe", lambda e: e.matmul(out, lhsT=lhsT, rhs=rhs, start=start, stop=stop),
                 reads=reads, writes=writes, inc=inc)

        def tr(out, in_, reads, writes, inc=False):
            P.op("pe", lambda e: e.transpose(out=out, in_=in_, identity=ident),
                 reads=list(reads) + ["cbf"], writes=writes, inc=inc)

        def act(out, in_, func, reads, writes, scale=None, bias=None, accum=None):
            kw = {}
            if scale is not None:
                kw["scale"] = scale
            if bias is not None:
                kw["bias"] = bias
            if accum is not None:
                kw["accum_out"] = accum
            P.op("act", lambda e: e.activation(out=out, in_=in_, func=func, **kw), reads=reads, writes=writes)

        def ts(out, in0, s1, s2, op0, op1, reads, writes, accum=None, eng="dve"):
            kw = {}
            if op1 is not None:
                kw["op1"] = op1
            if accum is not None:
                kw["accum_out"] = accum
            P.op(eng, lambda e: e.tensor_scalar(out=out, in0=in0, scalar1=s1, scalar2=s2, op0=op0, **kw),
                 reads=reads, writes=writes)

        def tt(out, in0, in1, op, reads, writes, eng="dve"):
            P.op(eng, lambda e: e.tensor_tensor(out=out, in0=in0, in1=in1, op=op), reads=reads, writes=writes)

        def stt(out, in0, scalar, in1, op0, op1, reads, writes, eng="dve"):
            P.op(eng, lambda e: e.scalar_tensor_tensor(out=out, in0=in0, scalar=scalar, in1=in1, op0=op0, op1=op1),
                 reads=reads, writes=writes)

        def cp(out, in_, reads, writes, eng="dve"):
            P.op(eng, lambda e: e.tensor_copy(out=out, in_=in_), reads=reads, writes=writes)

        def dma(q, out, in_, reads, writes):
            P.dma(q, lambda e: e.dma_start(out=out, in_=in_), reads=reads, writes=writes)

        dma("sp", cbf[:], cb_d, [], ["cbf"])
        dma("sp", identf[:], idf_d, [], ["identf"])
        dma("sp", rope[:], rope_d, [], ["rope"])
        dma("sp", gcol[:], gcol_d, [], ["gcol"])
        dma("sp", fgb[:], fg_d.partition_broadcast(128), [], ["fgb"])
        dma("sp", sc[0:1, 0:2 * D], bm_d, [], ["stg0", "stg1", "stg2", "stg3"])
        cp(bmb[:], sc[0:1, 0:2 * D], ["stg0", "stg1", "stg2", "stg3"], ["bmb"])
        P.op("pool", lambda e: e.memset(vaug[:], 1.0), writes=["vaug0", "vaug1", "vaug2", "vaug3", "vaug4"])
        P.op("pool", lambda e: e.memset(KI[:], 0.0), writes=["KI"])
        P.op("pool", lambda e: e.memset(vbc[:], 1.0), writes=["vbc"])
        P.op("pool", lambda e: e.memset(sm[:, 63:64], 1e-6), writes=["sm"])
        P.op("pool", lambda e: e.memset(dmy[:], 1.0), writes=["dmy"])
        for _i in range(16):
            P.op("pool", lambda e, a=sm2[:, _i:_i + 1], v=2.0 ** -(_i + 1): e.memset(a, v), writes=["CI"])
        P.op("pool", lambda e: e.memset(QI[:], 0.0), writes=["QI"])

        stage_i = [0]
        ENG_ROT = ("dve", "act", "dve", "act", "dve", "pool", "act", "dve")

        def convert(dst, src_dram, width, scale_ap=None, cscale=None):
            if cscale is not None:
                scale_ap = cscale
            i = stage_i[0]
            stage_i[0] += 1
            s = i % 4
            stg = sc[:, s * 1024: s * 1024 + width]
            res = f"stg{s}"
            dma("sp" if i % 2 == 0 else "pool", stg, src_dram, [], [res])
            eng = ENG_ROT[i % len(ENG_ROT)]
            if eng == "act":
                if scale_ap is None:
                    act(dst, stg, AF.Copy, [res], ["wts"])
                else:
                    act(dst, stg, AF.Copy, [res, "gcol"], ["wts"], scale=scale_ap)
            elif scale_ap is None:
                cp(dst, stg, [res], ["wts"], eng=eng)
            else:
                ts(dst, stg, scale_ap, None, ALU.mult, None, [res, "gcol"], ["wts"], eng=eng)

        for kc in range(8):
            for c0 in range(0, CW, 1024):
                c1 = min(c0 + 1024, CW)
                convert(winb[:, kc * CW + c0: kc * CW + c1], win_d[kc * 128:(kc + 1) * 128, c0:c1], c1 - c0,
                        gcol[:, kc:kc + 1])
        for c in range(4):
            convert(wab[:, c * D:(c + 1) * D], wa_d[c * 128:(c + 1) * 128, :], D, cscale=0.5)
            convert(wbb[:, c * D:(c + 1) * D], wb_d[c * 128:(c + 1) * 128, :], D, cscale=0.5)
        for c in range(8):
            convert(wob[:, c * D:(c + 1) * D], wo_d[c * 128:(c + 1) * 128, :], D, cscale=0.5)
        for c0 in range(0, 5120, 1024):
            convert(biasb[:, c0:c0 + 1024], bias_d[:, c0:c0 + 1024], 1024)

        SCR = ["stg0", "stg1", "stg2", "stg3"]
        if stop == 'setup':
            NT = 0

        def proj_tok(bk, col0, width, off=0, first=True, last=True):
            for kc in range(8):
                mm(bank(bk)[:, off:off + width], xnT[:, kc * 128:(kc + 1) * 128],
                   winb[:, kc * CW + col0: kc * CW + col0 + width], kc == 0, kc == 7,
                   ["xnT", "wts"], [K[bk]], inc=(kc == 7 and last))

        def proj_feat(bk, col0, off):
            for kc in range(8):
                mm(bank(bk)[:, off:off + 128], winb[:, kc * CW + col0: kc * CW + col0 + 128],
                   xnT[:, kc * 128:(kc + 1) * 128], kc == 0, kc == 7, ["xnT", "wts"], [K[bk]], inc=(kc == 7))

        def rope_apply(dst3, src3, cos2, sin2, H, half, tmp, reads, writes):
            n = H * half
            cb = cos2.unsqueeze(1).broadcast_to([128, H, half])
            sb = sin2.unsqueeze(1).broadcast_to([128, H, half])
            t = [tmp[:, i * n:(i + 1) * n].rearrange("p (h d) -> p h d", h=H) for i in range(4)]
            x1 = src3[:, :, 0:half]
            x2 = src3[:, :, half:2 * half]
            rr = list(reads) + ["rope"]
            tt(t[0], x1, cb, ALU.mult, rr, ["rtmp"])
            tt(t[1], x2, sb, ALU.mult, rr, ["rtmp"])
            tt(t[2], x2, cb, ALU.mult, rr, ["rtmp"])
            tt(t[3], x1, sb, ALU.mult, rr, ["rtmp"])
            tt(dst3[:, :, 0:half], t[0], t[1], ALU.subtract, ["rtmp"], writes)
            tt(dst3[:, :, half:2 * half], t[2], t[3], ALU.add, ["rtmp"], writes)

        rtmp = sm

        for b in range(NT):
            slot = b % 5
            N = 128 * (b + 1)
            rp = rope[:, b * 40:(b + 1) * 40]
            x_sb = PT[:].bitcast(F32)
            PTALL = ["PT0", "PT1", "R2", "R3"]
            xnb = Rr[:]
            XNB = ["R0", "R1"]
            rt = sc[:, 1536:2560]

            def x_norm_stats():
                act(xnb, x_sb, AF.Square, PTALL, XNB + ["nsm"], accum=sm[:, 0:1])
                act(sm[:, 1:2], sm[:, 0:1], AF.Sqrt, ["nsm"], ["nsm"], scale=1.0 / D, bias=sm[:, 63:64])
                P.op("dve", lambda e, o=sm[:, 2:3], i=sm[:, 1:2]: e.reciprocal(out=o, in_=i), reads=["nsm"], writes=["nsm"])

            def x_norm_apply():
                ts(xnb, x_sb, sm[:, 2:3], None, ALU.mult, None, ["nsm"] + PTALL, XNB)
                P.op("dve", lambda e, a=PT[0:64, 0:1024]: e.memset(a, 0.0), reads=[], writes=["PT0"])

            def x_transposes():
                for kc in range(8):
                    tr(bankb(7)[:, kc * 128:(kc + 1) * 128], xnb[:, kc * 128:(kc + 1) * 128], XNB, [K[7]], inc=(kc == 7))

            if b == 0:
                dma("sp", x_sb, x_d[0:128, :], [], PTALL)
                x_norm_stats()
                x_norm_apply()
                x_transposes()
                cp(xnT[:], bankb(7), [K[7]], ["xnT"])
            act(dmy[:, 0:1], dmy[:, 1:2], AF.Tanh, [], ["dmy"])
            if stop == 'p1a':
                break
            proj_tok(0, VA, 512)
            va_dst = vaug[:, slot * 520:(slot + 1) * 520].rearrange("p (h d) -> p h d", h=8)[:, :, 0:64]
            act(va_dst, bank(0).rearrange("p (h d) -> p h d", h=8), AF.Copy, [K[0]], [f"vaug{slot}"])
            proj_tok(1, GA, 512)
            tnh = sc[:, 2816:3328]
            act(tnh, bank(1), AF.Tanh, [K[1]] + SCR, SCR, scale=0.5)
            stt(gas[:], tnh, 1.0, bank(1), ALU.add, ALU.mult, [K[1]] + SCR, ["gas"])
            if stop == 'p1b':
                break
            proj_tok(2, QB, 512)
            QI3 = QI[:].rearrange("p (h d) -> p h d", h=8)
            b03 = bank(2).rearrange("p (h d) -> p h d", h=8)
            ts(QI3[:, :, 16:64], b03[:, :, 16:64], 0.125, None, ALU.mult, None, [K[2]], ["QI"])
            rope_apply(QI3[:, :, 0:16], b03[:, :, 0:16], rp[:, 16:24], rp[:, 24:32], 8, 8, rt, [K[2]] + SCR, ["QI"] + SCR)
            if stop == 'p1c':
                break
            proj_tok(3, KBO, 128, off=0, last=False)
            proj_tok(3, IQ, 296, off=128)
            for hp in range(4):
                proj_feat(4, QA + hp * 128, hp * 128)
            act(qaT[:], bank(4), AF.Copy, [K[4]], ["qaT"], scale=0.125)
            for hp in range(4):
                proj_feat(5, KA + hp * 128, hp * 128)
            proj_tok(6, GB, 512)
            b1 = bank(3)
            rope_apply(KI[:, 0:64].rearrange("p (h d) -> p h d", h=1), b1[:, 0:64].rearrange("p (h d) -> p h d", h=1),
                       rp[:, 0:8], rp[:, 8:16], 1, 8, rt, [K[3]] + SCR, ["KI"] + SCR)
            cp(KI[:, 16:64], b1[:, 16:64], [K[3]], ["KI"])
            cp(vbc[:, b * 128:b * 128 + 64], b1[:, 64:128], [K[3]], ["vbc"])
            iw_ps = b1[:, 128 + 288:128 + 296]
            ts(sm[:, 16:24], iw_ps, 0.0, 2.0, ALU.is_ge, ALU.mult, [K[3]], ["sm"])
            ts(sm[:, 16:24], sm[:, 16:24], -1.0, None, ALU.add, None, ["sm"], ["sm"])
            stt(sm[:, 8:16], iw_ps, 0.0625, sm[:, 16:24], ALU.mult, ALU.mult, [K[3], "sm"], ["sm"])
            iqs = sc[:, 2560:2816].rearrange("p (h d) -> p h d", h=8)
            tt(iqs, b1[:, 128:384].rearrange("p (h d) -> p h d", h=8),
               sm[:, 8:16].unsqueeze(2).broadcast_to([128, 8, 32]), ALU.mult, [K[3], "sm"] + SCR, SCR)
            rope_apply(QI3[:, :, 64:72], iqs[:, :, 0:8], rp[:, 32:36], rp[:, 36:40], 8, 4, rt, SCR, ["QI"] + SCR)
            cp(QI3[:, :, 72:96], iqs[:, :, 8:32], SCR, ["QI"])
            rope_apply(KI[:, 64:72].rearrange("p (h d) -> p h d", h=1),
                       b1[:, 384:392].rearrange("p (h d) -> p h d", h=1),
                       rp[:, 32:36], rp[:, 36:40], 1, 4, rt, [K[3]] + SCR, ["KI"] + SCR)
            cp(KI[:, 72:96], b1[:, 392:416], [K[3]], ["KI"])
            tt(Dh[:].rearrange("p (h q) -> p h q", h=8), ident.unsqueeze(1).broadcast_to([128, 8, 128]),
               sm[:, 16:24].unsqueeze(2).broadcast_to([128, 8, 128]), ALU.mult, ["cbf", "sm"], ["Dh"])
            if stop == 'p1d':
                break
            for h in range(8):
                tr(bankb(7)[:, h * 128:(h + 1) * 128], QI[:, h * 128:(h + 1) * 128], ["QI"], [K[7]], inc=(h == 7))
            cp(QIT[:], bankb(7), [K[7]], ["QIT"])
            tr(bankb(0)[:, 0:128], KI[:], ["KI"], [K[0]], inc=True)
            cp(kcache[:, b * 128:(b + 1) * 128], bankb(0)[:, 0:128], [K[0]], ["kcache"])
            ka_dst = kaT[:].rearrange("p (c s t) -> p c s t", c=4, s=5)[:, :, slot, :]
            cp(ka_dst, bank(5).rearrange("p (c t) -> p c t", c=4), [K[5]], [f"kaT{slot}"])
            tnh2 = sc[:, 3328:3840]
            act(tnh2, bank(6), AF.Tanh, [K[6]] + SCR, SCR, scale=0.5)
            stt(gbT[:], tnh2, 1.0, bank(6), ALU.add, ALU.mult, [K[6]] + SCR, ["gbT"])
            if dbg and b == NT - 1:
                cp(sc[:, 3072:4096], QIT[:], ["QIT"], SCR)
                dma("sp", dbg_d["d_qit"], sc[:, 3072:4096], SCR, [])
                cp(sc[:, 3072:3584], kcache[:, 0:512], ["kcache"], SCR)
                dma("sp", dbg_d["d_kc"], sc[:, 3072:3584], SCR, [])

            if stop == 'p1':
                break
            jjs = [jj for jj in range(5) if b - 4 + jj >= 0]
            yacc_v = [Rr[:].bitcast(F32)[:, 0:260], Dh[:].bitcast(F32)[:, 0:260]]
            yacc_r = [["R0", "R1"], ["Dh"]]
            yaf_all = QI[:].bitcast(F32)

            def a_keytile(ji):
                jj = jjs[ji]
                j = b - 4 + jj
                sj = j % 5
                pts = ji % 2
                for half in range(2):
                    bk = 4 + half
                    for hh in range(4):
                        h = half * 4 + hh
                        pr = (h % 2) * 64
                        c = h // 2
                        mm(bank(bk)[:, hh * 128:(hh + 1) * 128],
                           kaT[pr:pr + 64, (c * 5 + sj) * 128:(c * 5 + sj + 1) * 128],
                           qaT[pr:pr + 64, c * 128:(c + 1) * 128], True, False,
                           [f"kaT{sj}", "qaT"], [K[bk]])
                        mm(bank(bk)[:, hh * 128:(hh + 1) * 128], ident,
                           biasb[:, (jj * 8 + h) * 128:(jj * 8 + h + 1) * 128], False, True,
                           ["cbf", "wts"], [K[bk]])
                pta = PT[:, pts * 1024:(pts + 1) * 1024]
                for half in range(2):
                    act(pta[:, half * 512:(half + 1) * 512], bank(4 + half), AF.Exp, [K[4 + half]], PTW[pts])
                for h in range(8):
                    bk = 6 + h // 4
                    o = (h % 4) * 65
                    mm(bank(bk)[:, o:o + 65], pta[:, h * 128:(h + 1) * 128],
                       vaug[:, (sj * 8 + h) * 65:(sj * 8 + h + 1) * 65], True, True,
                       [f"PT{pts}", f"vaug{sj}"], [K[bk]])
                for k2 in range(2):
                    if ji == 0:
                        cp(yacc_v[k2], bank(6 + k2)[:, 0:260], [K[6 + k2]], yacc_r[k2])
                    else:
                        tt(yacc_v[k2], yacc_v[k2], bank(6 + k2)[:, 0:260], ALU.add, [K[6 + k2]] + yacc_r[k2], yacc_r[k2])

            def a_finish():
                for k2 in range(2):
                    yv = yacc_v[k2].rearrange("p (h d) -> p h d", h=4)
                    rec = sm[:, 24 + 4 * k2:28 + 4 * k2].rearrange("p (h o) -> p h o", h=4, o=1)
                    P.op("dve", lambda e, o=rec, i=yv[:, :, 64:65]: e.reciprocal(out=o, in_=i),
                         reads=yacc_r[k2], writes=["sm"])
                    yaf = yaf_all[:, k2 * 256:(k2 + 1) * 256].rearrange("p (h d) -> p h d", h=4)
                    tt(yaf, yv[:, :, 0:64], rec.broadcast_to([128, 4, 64]), ALU.mult, ["sm", "QI"] + yacc_r[k2], ["QI"])
                tt(yab[:], yaf_all, gas[:], ALU.mult, ["QI", "gas"], ["yab"])
                for c in range(4):
                    tr(bankb(0)[:, c * 128:(c + 1) * 128], yab[:, c * 128:(c + 1) * 128], ["yab"], [K[0]])
                act(yaT[:], bankb(0)[:, 0:512], AF.Copy, [K[0]], ["yaT"])

            if stop == 'A':
                break
            nblk = (N + 511) // 512
            items = [(kb, h) for kb in range(nblk) for h in range(8)]

            LB = (0, 1, 4, 5)
            RS = (Rr[:, 0:512], Rr[:, 512:1024], PT[:, 1024:1536], PT[:, 1536:2048])
            RSR = ("R0", "R1", "R2", "R3")

            def idx_L(i):
                kb, h = items[i]
                k0 = kb * 512
                w = min(512, N - k0)
                lb = LB[i % 4]
                mm(bank(lb)[:, 0:w], IQz[:, h * 128:(h + 1) * 128], kcache[:, k0:k0 + w],
                   True, True, ["PT0", "kcache"], [K[lb]])
                rs = i % 4
                if i % 2 == 0:
                    act(RS[rs][:, 0:w], bank(lb)[:, 0:w], AF.Relu, [K[lb]], [RSR[rs]])
                else:
                    ts(RS[rs][:, 0:w], bank(lb)[:, 0:w], 0.0, None, ALU.max, None, [K[lb]], [RSR[rs]])

            def idx_D(i):
                kb, h = items[i]
                k0 = kb * 512
                w = min(512, N - k0)
                rs = i % 4
                sb = 2 + kb % 2
                mm(bank(sb)[:, 0:w], Dh[:, h * 128:(h + 1) * 128], RS[rs][:, 0:w], h == 0, h == 7,
                   ["Dh", RSR[rs]], [K[sb]])
                if h == 7:
                    act(sc[:, k0:k0 + w], bank(sb)[:, 0:w], AF.Copy, [K[sb]], SCR)

            IQz = PT[:, 0:1024]
            cp(IQz[64:128, :], QIT[64:128, :], ["QIT"], ["PT0", "PT1", "R2", "R3"])
            LA = 3
            for i in range(min(LA, len(items))):
                idx_L(i)
            for i in range(len(items)):
                idx_D(i)
                if i + LA < len(items):
                    idx_L(i + LA)
            P.op("dve", lambda e, a=sc[0:64, N - 64:N]: e.memset(a, -1e30), reads=[], writes=SCR)
            if dbg and b == NT - 1:
                dma("sp", dbg_d["d_sc"], sc[:, 0:512], SCR, [])
            if stop == 'idx':
                break
            LO, HI, D0, MID, CNT, U, SA, T2 = (sm[:, 32:33], sm[:, 33:34], sm[:, 34:35], sm[:, 35:36], sm[:, 36:37],
                                               sm[:, 37:38], sm[:, 38:39], sm[:, 39:40])
            if b < 2:
                P.op("dve", lambda e, a=LO: e.memset(a, -1e29), reads=[], writes=["bLO"])
                for ji in range(len(jjs)):
                    a_keytile(ji)
                a_finish()
            else:
                P.op("dve", lambda e, o=HI, i=sc[:, 0:N]: e.tensor_reduce(out=o, in_=i, axis=AX.X, op=ALU.max),
                     reads=SCR, writes=["bHI"])
                P.op("dve", lambda e, o=LO, i=sc[:, 0:256]: e.tensor_reduce(out=o, in_=i, axis=AX.X, op=ALU.min),
                     reads=SCR, writes=["bLO"])
                tt(D0, HI, LO, ALU.subtract, ["bHI", "bLO"], ["bD0"])
                ts(DI, CI, D0, None, ALU.mult, None, ["bD0", "CI"], ["bDI"])
                n1 = 64 * max(1, int(round(DVE_SHARE * N / 64)))
                nA = N - n1
                jo2 = junk2[:, 0:64].unsqueeze(1).broadcast_to([128, nA // 64, 64])
                sview2 = sc[:, n1:N].rearrange("p (a c) -> p a c", c=64)
                tt(MID, LO, DI[:, 0:1], ALU.add, ["bLO", "bDI"], ["bMID"])
                jo = junk[:, 0:64].unsqueeze(1).broadcast_to([128, n1 // 64, 64])
                sview = sc[:, 0:n1].rearrange("p (a c) -> p a c", c=64)
                a_at = {3 * ji + 3: ji for ji in range(len(jjs))}
                MIDm = sm[:, 43:44]
                thrp = (511.0 - nA) / 2.0
                for it in range(NBIS):
                    last = it == NBIS - 1
                    stt(MIDm, DI[:, it:it + 1], -1.0 if last else -0.5, MID, ALU.mult, ALU.add, ["bDI", "bMID"], ["bMIDm"])
                    ts(jo, sview, MID, thrp, ALU.is_ge, ALU.subtract, SCR + ["bMID"], ["junk", "bCNT"], accum=CNT)
                    act(jo2, sview2, AF.Sign, SCR + ["bMID"], ["junk2", "bSA"], scale=-1.0, bias=MID, accum=SA)
                    stt(U, SA, -0.5, CNT, ALU.mult, ALU.is_ge, ["bCNT", "bSA"], ["bU"])
                    if not last:
                        stt(MID, U, DI[:, it:it + 1], MIDm, ALU.mult, ALU.add, ["bU", "bDI", "bMIDm"], ["bMID"])
                    else:
                        stt(LO, U, DI[:, NBIS - 1:NBIS], MIDm, ALU.mult, ALU.add, ["bU", "bDI", "bMIDm"], ["bLO"])
                    if it in a_at:
                        a_keytile(a_at[it])
                a_finish()
            if dbg and b == NT - 1:
                jo = junk[:, 0:64].unsqueeze(1).broadcast_to([128, N // 64, 64])
                sview = sc[:, 0:N].rearrange("p (a c) -> p a c", c=64)
                ts(jo, sview, LO, None, ALU.is_ge, ALU.add, SCR + ["bLO"], ["junk", "bCNT"], accum=CNT)
                dma("sp", dbg_d["d_lo"], LO, ["bLO"], [])
                dma("sp", dbg_d["d_cnt"], CNT, ["bCNT"], [])
            if stop == 'bis':
                break
            QBz = Dh
            P.op("dve", lambda e, a=QBz[64:128, :]: e.memset(a, 0.0), reads=[], writes=["Dh"])
            cp(QBz[0:64, :], QIT[0:64, :], ["QIT"], ["Dh"])

            def b_stage1(j):
                ns = j % 2
                nm = nmr[:, ns * 128:(ns + 1) * 128]
                ts(nm, sc[:, j * 128:(j + 1) * 128], LO, NEGM, ALU.is_lt, ALU.mult, SCR + ["bLO"], [f"nm{ns}"])
                pts = j % 2
                ptb = PT[:, pts * 1024:(pts + 1) * 1024]
                sb0 = 2 if j % 2 == 0 else 6
                for half in range(2):
                    bk = sb0 + half
                    mm(bank(bk), kcache[:, j * 128:(j + 1) * 128], QBz[:, half * 512:(half + 1) * 512],
                       True, False, ["kcache", "Dh"], [K[bk]])
                    mm(bank(bk), nm, I4, False, True, [f"nm{ns}", "cbf"], [K[bk]])
                for half in range(2):
                    act(ptb[:, half * 512:(half + 1) * 512], bank(sb0 + half), AF.Exp, [K[sb0 + half]], PTW[pts])

            def b_stage2(j):
                pts = j % 2
                ptb = PT[:, pts * 1024:(pts + 1) * 1024]
                for half in range(2):
                    mm(bank(4 + half), vbc[:, j * 128:(j + 1) * 128], ptb[:, half * 512:(half + 1) * 512],
                       j == 0, j == b, [f"PT{pts}", "vbc"], [K[4 + half]])

            for j in range(b + 1):
                b_stage1(j)
                if j >= 1:
                    b_stage2(j - 1)
            b_stage2(b)
            act(sc[:, 0:512], bank(4), AF.Copy, [K[4]] + SCR, SCR)
            cp(sc[:, 512:1024], bank(5), [K[5]] + SCR, SCR)
            for h in range(8):
                P.op("pe", lambda e, o=bank(h // 4)[:, (h % 4) * 128:(h % 4 + 1) * 128],
                     i=sc[:, h * 128:(h + 1) * 128]: e.transpose(out=o, in_=i, identity=identf[:]),
                     reads=SCR + ["identf"], writes=[K[h // 4]])
            for k2 in range(2):
                yv = bank(k2).rearrange("p (h d) -> p h d", h=4)
                rec = sm[:, 24 + 4 * k2:28 + 4 * k2].rearrange("p (h o) -> p h o", h=4, o=1)
                P.op("dve", lambda e, o=rec, i=yv[:, :, 64:65]: e.reciprocal(out=o, in_=i), reads=[K[k2]], writes=["sm"])
                ybf = sc[:, 1024 + k2 * 256:1024 + (k2 + 1) * 256].rearrange("p (h d) -> p h d", h=4)
                tt(ybf, yv[:, :, 0:64], rec.broadcast_to([128, 4, 64]), ALU.mult, [K[k2], "sm"] + SCR, SCR)
            tt(yab[:], sc[:, 1024:1536], gbT[:], ALU.mult, SCR + ["gbT"], ["yab"])
            for c in range(4):
                tr(bankb(2)[:, c * 128:(c + 1) * 128], yab[:, c * 128:(c + 1) * 128], ["yab"], [K[2]])
            cp(ybT[:], bankb(2)[:, 0:512], [K[2]], ["ybT"])
            if dbg and b == NT - 1:
                cp(sc[:, 3072:3584], ybT[:], ["ybT"], SCR)
                dma("sp", dbg_d["d_yb"], sc[:, 3072:3584], SCR, [])

            if stop == 'B':
                break
            hbuf = sc[:, 2048:3072]
            dma("sp", hbuf, x_d[b * 128:(b + 1) * 128, :], [], SCR)
            if b + 1 < NT:
                dma("sp", x_sb, x_d[(b + 1) * 128:(b + 2) * 128, :], [], PTALL)
            for nb in range(2):
                cs = slice(nb * 512, (nb + 1) * 512)
                for c in range(4):
                    mm(bank(0), yaT[:, c * 128:(c + 1) * 128], wab[:, c * D + nb * 512: c * D + (nb + 1) * 512],
                       c == 0, c == 3, ["yaT", "wts"], [K[0]], inc=(c == 3))
                for c in range(4):
                    mm(bank(1), ybT[:, c * 128:(c + 1) * 128], wbb[:, c * D + nb * 512: c * D + (nb + 1) * 512],
                       c == 0, c == 3, ["ybT", "wts"], [K[1]], inc=(c == 3))
                for gi, (zc, bk) in enumerate(((ZA, 2), (ZB, 3))):
                    for kc in range(8):
                        mm(bank(bk), xnT[:, kc * 128:(kc + 1) * 128],
                           winb[:, kc * CW + zc + nb * 512: kc * CW + zc + (nb + 1) * 512], kc == 0, False,
                           ["xnT", "wts"], [K[bk]])
                    mm(bank(bk), ones[0:1, 0:128], bmb[0:1, gi * D + nb * 512: gi * D + (nb + 1) * 512], False, True,
                       ["cbf", "bmb"], [K[bk]], inc=True)
                g_a = sc[:, 0:512]
                g_b = sc[:, 512:1024]
                act(g_a, bank(2), AF.Tanh, [K[2]] + SCR, SCR, scale=0.5)
                act(g_b, bank(3), AF.Tanh, [K[3]] + SCR, SCR, scale=0.5)
                stt(g_a, g_a, 1.0, bank(0), ALU.add, ALU.mult, [K[0]] + SCR, SCR)
                stt(g_b, g_b, 1.0, bank(1), ALU.add, ALU.mult, [K[1]] + SCR, SCR)
                tt(mgb[:, cs], g_a, g_b, ALU.add, SCR, ["QI"])
            if dbg and b == NT - 1:
                cp(sc[:, 3072:4096], mgb[:], ["QI"], SCR)
                dma("sp", dbg_d["d_mg"], sc[:, 3072:4096], SCR, [])
            act(dmy[:, 0:1], dmy[:, 1:2], AF.Sqrt, [], ["dmy"])
            if b + 1 < NT:
                x_norm_stats()
                x_norm_apply()
            for kc in range(8):
                tr(bankb(4)[:, kc * 128:(kc + 1) * 128], mgb[:, kc * 128:(kc + 1) * 128], ["QI"], [K[4]], inc=(kc == 7))
            cp(mgT[:], bankb(4), [K[4]], ["QIT"])
            if b + 1 < NT:
                x_transposes()
                act(xnT[:], bankb(7), AF.Copy, [K[7]], ["xnT"])
            for nb in range(2):
                for kc in range(8):
                    mm(bank(5 + nb), mgT[:, kc * 128:(kc + 1) * 128],
                       wob[:, kc * D + nb * 512: kc * D + (nb + 1) * 512], kc == 0, kc == 7,
                       ["QIT", "wts"], [K[5 + nb]], inc=(kc == 7))
            for nb in range(2):
                tt(hbuf[:, nb * 512:(nb + 1) * 512], hbuf[:, nb * 512:(nb + 1) * 512], bank(5 + nb), ALU.add,
                   [K[5 + nb]] + SCR, SCR)
            act(sc[:, 0:1024], hbuf, AF.Square, SCR, SCR + ["sm"], accum=sm[:, 40:41])
            act(sm[:, 41:42], sm[:, 40:41], AF.Sqrt, ["sm"], ["sm"], scale=1.0 / D, bias=sm[:, 63:64])
            P.op("dve", lambda e, o=sm[:, 42:43], i=sm[:, 41:42]: e.reciprocal(out=o, in_=i), reads=["sm"], writes=["sm"])
            stt(hbuf, hbuf, sm[:, 42:43], fgb[:], ALU.mult, ALU.mult, SCR + ["sm", "fgb"], SCR)
            dma("sp", out_d[b * 128:(b + 1) * 128, :], hbuf, SCR, [])

        P.finish_waits("sp")
        P.emit()
    return nc


def _host_consts():
    bf = ml_dtypes.bfloat16
    cb = np.zeros((128, 896), np.float32)
    cb[:, 0:128] = np.eye(128)
    for r in range(4):
        cb[:, 128 + r * 128:128 + (r + 1) * 128] = np.eye(128)
    cb[:, 640:768] = 1.0
    cb = cb.astype(bf)
    pos = np.arange(S, dtype=np.float32)
    invB = (np.float32(500000.0) ** (-np.arange(8, dtype=np.float32) / np.float32(8))).astype(np.float32)
    invI = (np.float32(500000.0) ** (-np.arange(4, dtype=np.float32) / np.float32(4))).astype(np.float32)
    angB = (pos[:, None] * invB[None, :]).astype(np.float32)
    angI = (pos[:, None] * invI[None, :]).astype(np.float32)
    tab = np.concatenate([np.cos(angB), np.sin(angB), np.cos(angB) * 0.125, np.sin(angB) * 0.125,
                          np.cos(angI), np.sin(angI)], axis=1).astype(np.float32)
    rope = np.ascontiguousarray(tab.reshape(32, 128, 40).transpose(1, 0, 2).reshape(128, 32 * 40))
    kk = np.arange(128)[:, None, None]
    jj = np.arange(5)[None, :, None]
    q = np.arange(128)[None, None, :]
    dist = 128 * (4 - jj) + q - kk
    idx = np.clip(dist, -256, 256) + 256
    dchunk = 2 * (jj - 4) + (kk >= 64).astype(np.int64) - (q >= 64).astype(np.int64)
    bad = (dchunk < -8) | (dchunk > 0)
    return cb, rope, idx, bad


def _bias_table(rel_bias, idx, bad):
    t = rel_bias[:, idx]
    t = np.where(bad[None], np.float32(NEGM), t).astype(np.float32)
    return np.ascontiguousarray(t.transpose(1, 2, 0, 3).reshape(128, 5 * 8 * 128))


def make_in_maps(x, norm_gain, w_in, b_merge, rel_bias, w_branch_a, w_branch_b, w_out, final_norm_gain):
    cb, rope, idx, bad = _host_consts()
    shared = {
        "w_in": np.ascontiguousarray(w_in[0], dtype=np.float32),
        "wa": np.ascontiguousarray(w_branch_a[0], dtype=np.float32),
        "wb": np.ascontiguousarray(w_branch_b[0], dtype=np.float32),
        "wo": np.ascontiguousarray(w_out[0], dtype=np.float32),
        "gcol": np.ascontiguousarray(np.asarray(norm_gain[0], np.float32).reshape(8, 128).T),
        "fgain": np.ascontiguousarray(np.asarray(final_norm_gain, np.float32).reshape(1, D)),
        "bmrow": np.ascontiguousarray(np.asarray(b_merge[0], np.float32).reshape(1, 2 * D)),
        "biasT": _bias_table(np.asarray(rel_bias[0], np.float32), idx, bad),
        "rope": rope,
        "cbf": cb,
        "identf": np.eye(128, dtype=np.float32),
    }
    return [dict(shared, x=np.ascontiguousarray(x[i], dtype=np.float32)) for i in range(x.shape[0])]


def kernel(x, norm_gain, w_in, b_merge, rel_bias, w_branch_a, w_branch_b, w_out, final_norm_gain):
    args = [np.asarray(a) for a in (x, norm_gain, w_in, b_merge, rel_bias, w_branch_a, w_branch_b, w_out,
                                    final_norm_gain)]
    in_maps = make_in_maps(*args)
    nc = build()
    res = run_bass_kernel_spmd(nc, in_maps, core_ids=list(range(8)))
    return np.stack([np.asarray(r["out"], dtype=np.float32) for r in res.results], axis=0)
```

```python
import contextlib
import numpy as np
import ml_dtypes
import concourse.bass as bass
import concourse.mybir as mybir
from concourse.bass_utils import run_bass_kernel_spmd

F32 = mybir.dt.float32
BF16 = mybir.dt.bfloat16
ALU = mybir.AluOpType
AF = mybir.ActivationFunctionType
AX = mybir.AxisListType

S = 4096
D = 1024
NTILES = 32
CW = 5544
QA, KA, VA, GA, QB, KBO, VBO, GB, IQ, IK, IW, ZA, ZB = (
    0, 512, 1024, 1536, 2048, 2560, 2624, 2688, 3200, 3456, 3488, 3496, 4520)
NBIS = 16
DVE_SHARE = 0.43
NEGM = -30000.0

COMPUTE = ("pe", "act", "dve", "pool")
NDMASEM = 8


class Prog:
    def __init__(self, nc, dma_queues=("sp", "pool")):
        self.nc = nc
        self.ops = {e: [] for e in ("pe", "act", "dve", "pool", "sp")}
        self.cnt = {}
        self.lastw = {}
        self.reads = {}
        self.waited = {e: {} for e in self.ops}
        self.dma_i = {q: 0 for q in dma_queues}
        self.timelines = list(COMPUTE) + [f"dma_{q}{i}" for q in dma_queues for i in range(NDMASEM)]
        for t in self.timelines:
            self.cnt[t] = 0
        self.marked = {e: set() for e in COMPUTE}

    def _need(self, eng, waits, tl, val):
        if tl == eng and eng == "pe":
            return
        if self.waited[eng].get(tl, 0) >= val:
            return
        waits[tl] = max(waits.get(tl, 0), val)

    def _deps(self, eng, tl_self, reads, writes):
        waits = {}
        for r in reads:
            lw = self.lastw.get(r)
            if lw:
                self._need(eng, waits, lw[0], lw[1])
        for w in writes:
            lw = self.lastw.get(w)
            if lw and lw[0] != tl_self:
                self._need(eng, waits, lw[0], lw[1])
            for tl, v in self.reads.get(w, {}).items():
                if tl != tl_self:
                    self._need(eng, waits, tl, v)
        for tl, v in waits.items():
            self.waited[eng][tl] = max(self.waited[eng].get(tl, 0), v)
            if tl in self.marked:
                self.marked[tl].add(v)
        return waits

    def _commit(self, tl, val, reads, writes):
        for r in reads:
            self.reads.setdefault(r, {})[tl] = val
        for w in writes:
            self.lastw[w] = (tl, val)
            self.reads[w] = {}

    def op(self, eng, fn, reads=(), writes=(), inc=True):
        waits = self._deps(eng, eng, reads, writes)
        self.cnt[eng] += 1
        val = self.cnt[eng]
        self.ops[eng].append((waits, fn, (eng, val)))
        self._commit(eng, val, reads, writes)

    def dma(self, q, fn, reads=(), writes=()):
        i = self.dma_i[q]
        self.dma_i[q] += 1
        tl = f"dma_{q}{i % NDMASEM}"
        waits = self._deps(q, tl, reads, writes)
        prev = self.cnt[tl]
        if prev > 0 and self.waited[q].get(tl, 0) < prev:
            waits[tl] = max(waits.get(tl, 0), prev)
            self.waited[q][tl] = prev
        self.cnt[tl] += 16
        self.ops[q].append((waits, fn, (tl, 16)))
        self._commit(tl, self.cnt[tl], reads, writes)

    def finish_waits(self, eng="sp"):
        waits = {}
        for tl in self.timelines:
            if self.cnt[tl] > 0 and self.waited[eng].get(tl, 0) < self.cnt[tl]:
                waits[tl] = self.cnt[tl]
                if tl in self.marked:
                    self.marked[tl].add(self.cnt[tl])
        self.ops[eng].append((waits, None, None))

    def emit(self):
        nc = self.nc
        rank = {}
        for e in COMPUTE:
            rank[e] = {s: i + 1 for i, s in enumerate(sorted(self.marked[e]))}
        with contextlib.ExitStack() as st:
            sems = {t: st.enter_context(nc.semaphore("s_" + t)) for t in self.timelines}
            block = st.enter_context(nc.Block())

            def run(engname):
                def body(e):
                    for waits, fn, inc in self.ops[engname]:
                        for tl, v in waits.items():
                            e.wait_ge(sems[tl], rank[tl][v] if tl in rank else v)
                        if fn is None:
                            continue
                        ins = fn(e)
                        if inc is None:
                            continue
                        if inc[0] in rank:
                            if inc[1] in rank[inc[0]]:
                                ins.then_inc(sems[inc[0]], 1)
                        else:
                            ins.then_inc(sems[inc[0]], inc[1])
                return body

            block.sync(run("sp"))
            block.tensor(run("pe"))
            block.scalar(run("act"))
            block.vector(run("dve"))
            block.gpsimd(run("pool"))


def build(NT=NTILES, dbg=False, stop=None):
    nc = bass.Bass("TRN2", target_bir_lowering=False)
    din = lambda name, shape, dt=F32: nc.dram_tensor(name, shape, dt, kind="ExternalInput").ap()
    x_d = din("x", [S, D])
    win_d = din("w_in", [D, CW])
    wa_d = din("wa", [512, D])
    wb_d = din("wb", [512, D])
    wo_d = din("wo", [D, D])
    gcol_d = din("gcol", [128, 8])
    fg_d = din("fgain", [1, D])
    bm_d = din("bmrow", [1, 2 * D])
    bias_d = din("biasT", [128, 5120])
    rope_d = din("rope", [128, 32 * 40])
    idf_d = din("identf", [128, 128])
    cb_d = din("cbf", [128, 896], BF16)
    out_d = nc.dram_tensor("out", [S, D], F32, kind="ExternalOutput").ap()
    dbg_d = {}
    if dbg:
        for name, shape in (("d_qit", [128, 1024]), ("d_kc", [128, 512]), ("d_ya", [128, 512]),
                            ("d_sc", [128, 512]), ("d_lo", [128, 1]), ("d_cnt", [128, 1]),
                            ("d_yb", [128, 512]), ("d_mg", [128, 1024])):
            dbg_d[name] = nc.dram_tensor(name, shape, F32, kind="ExternalOutput").ap()

    with contextlib.ExitStack() as st:
        T = lambda name, shape, dt: st.enter_context(nc.sbuf_tensor(name, shape, dt))
        winb = T("winb", [128, 8 * CW], BF16)
        wab = T("wab", [128, 4 * D], BF16)
        wbb = T("wbb", [128, 4 * D], BF16)
        wob = T("wob", [128, 8 * D], BF16)
        sc = T("sc", [128, 4096], F32)
        biasb = T("biasb", [128, 5120], BF16)
        cbf = T("cbf_s", [128, 896], BF16)
        rope = T("rope_s", [128, 32 * 40], F32)
        identf = T("identf_s", [128, 128], F32)
        fgb = T("fgb", [128, D], F32)
        gcol = T("gcol_s", [128, 8], F32)
        bmb = T("bmb", [1, 2 * D], BF16)
        kaT = T("kaT", [128, 4 * 5 * 128], BF16)
        vaug = T("vaug", [128, 5 * 8 * 65], BF16)
        kcache = T("kcache", [128, S], BF16)
        vbc = T("vbc", [128, 32 * 128], BF16)
        xnT = T("xnT", [128, 8 * 128], BF16)
        qaT = T("qaT", [128, 4 * 128], BF16)
        gbT = T("gbT", [128, 4 * 128], BF16)
        gas = T("gas", [128, 512], BF16)
        QI = T("QI", [128, 8 * 128], BF16)
        KI = T("KI", [128, 128], BF16)
        QIT = T("QIT", [128, 8 * 128], BF16)
        mgb = QI
        mgT = QIT
        Dh = T("Dh", [128, 8 * 128], BF16)
        Rr = T("Rr", [128, 2 * 512], BF16)
        PT = T("PT", [128, 2 * 1024], BF16)
        nmr = T("nmr", [128, 2 * 128], BF16)
        junk = T("junk", [128, 64], BF16)
        junk2 = T("junk2", [128, 64], BF16)
        sm2 = T("sm2", [128, 32], F32)
        dmy = T("dmy", [128, 2], F32)
        bst = T("bst", [128, 8], F32)
        CI = sm2[:, 0:16]
        DI = sm2[:, 16:32]
        yab = T("yab", [128, 512], BF16)
        yaT = T("yaT", [128, 4 * 128], BF16)
        ybT = T("ybT", [128, 4 * 128], BF16)
        sm = T("sm", [128, 64], F32)
        PS = st.enter_context(nc.psum_tensor("ps", [128, 4096], F32))

        ident = cbf[:, 0:128]
        I4 = cbf[:, 128:640]
        ones = cbf[:, 640:768]
        bank = lambda i: PS[:, i * 512:(i + 1) * 512]
        bankb = lambda i: PS[:, i * 512:(i + 1) * 512].bitcast(BF16)
        K = [f"K{i}" for i in range(8)]
        PTW = (['PT0'], ['PT1', 'R2', 'R3'])

        P = Prog(nc)

        def mm(out, lhsT, rhs, start, stop, reads, writes, inc=False):
            P.op("pe", lambda e: e.matmul(out, lhsT=lhsT, rhs=rhs, start=start, stop=stop),
                 reads=reads, writes=writes, inc=inc)

        def tr(out, in_, reads, writes, inc=False):
            P.op("pe", lambda e: e.transpose(out=out, in_=in_, identity=ident),
                 reads=list(reads) + ["cbf"], writes=writes, inc=inc)

        def act(out, in_, func, reads, writes, scale=None, bias=None, accum=None):
            kw = {}
            if scale is not None:
                kw["scale"] = scale
            if bias is not None:
                kw["bias"] = bias
            if accum is not None:
                kw["accum_out"] = accum
            P.op("act", lambda e: e.activation(out=out, in_=in_, func=func, **kw), reads=reads, writes=writes)

        def ts(out, in0, s1, s2, op0, op1, reads, writes, accum=None, eng="dve"):
            kw = {}
            if op1 is not None:
                kw["op1"] = op1
            if accum is not None:
                kw["accum_out"] = accum
            P.op(eng, lambda e: e.tensor_scalar(out=out, in0=in0, scalar1=s1, scalar2=s2, op0=op0, **kw),
                 reads=reads, writes=writes)

        def tt(out, in0, in1, op, reads, writes, eng="dve"):
            P.op(eng, lambda e: e.tensor_tensor(out=out, in0=in0, in1=in1, op=op), reads=reads, writes=writes)

        def stt(out, in0, scalar, in1, op0, op1, reads, writes, eng="dve"):
            P.op(eng, lambda e: e.scalar_tensor_tensor(out=out, in0=in0, scalar=scalar, in1=in1, op0=op0, op1=op1),
                 reads=reads, writes=writes)

        def cp(out, in_, reads, writes, eng="dve"):
            P.op(eng, lambda e: e.tensor_copy(out=out, in_=in_), reads=reads, writes=writes)

        def dma(q, out, in_, reads, writes):
            P.dma(q, lambda e: e.dma_start(out=out, in_=in_), reads=reads, writes=writes)

        dma("sp", cbf[:], cb_d, [], ["cbf"])
        dma("sp", identf[:], idf_d, [], ["identf"])
        dma("sp", rope[:], rope_d, [], ["rope"])
        dma("sp", gcol[:], gcol_d, [], ["gcol"])
        dma("sp", fgb[:], fg_d.partition_broadcast(128), [], ["fgb"])
        dma("sp", sc[0:1, 0:2 * D], bm_d, [], ["stg0", "stg1", "stg2", "stg3"])
        cp(bmb[:], sc[0:1, 0:2 * D], ["stg0", "stg1", "stg2", "stg3"], ["bmb"])
        P.op("pool", lambda e: e.memset(vaug[:], 1.0), writes=["vaug0", "vaug1", "vaug2", "vaug3", "vaug4"])
        P.op("pool", lambda e: e.memset(KI[:], 0.0), writes=["KI"])
        P.op("pool", lambda e: e.memset(vbc[:], 1.0), writes=["vbc"])
        P.op("pool", lambda e: e.memset(sm[:, 63:64], 1e-6), writes=["sm"])
        P.op("pool", lambda e: e.memset(dmy[:], 1.0), writes=["dmy"])
        for _i in range(16):
            P.op("pool", lambda e, a=sm2[:, _i:_i + 1], v=2.0 ** -(_i + 1): e.memset(a, v), writes=["CI"])
        P.op("pool", lambda e: e.memset(QI[:], 0.0), writes=["QI"])

        stage_i = [0]
        ENG_ROT = ("dve", "act", "dve", "act", "dve", "pool", "act", "dve")

        def convert(dst, src_dram, width, scale_ap=None, cscale=None):
            if cscale is not None:
                scale_ap = cscale
            i = stage_i[0]
            stage_i[0] += 1
            s = i % 4
            stg = sc[:, s * 1024: s * 1024 + width]
            res = f"stg{s}"
            dma("sp" if i % 2 == 0 else "pool", stg, src_dram, [], [res])
            eng = ENG_ROT[i % len(ENG_ROT)]
            if eng == "act":
                if scale_ap is None:
                    act(dst, stg, AF.Copy, [res], ["wts"])
                else:
                    act(dst, stg, AF.Copy, [res, "gcol"], ["wts"], scale=scale_ap)
            elif scale_ap is None:
                cp(dst, stg, [res], ["wts"], eng=eng)
            else:
                ts(dst, stg, scale_ap, None, ALU.mult, None, [res, "gcol"], ["wts"], eng=eng)

        for kc in range(8):
            for c0 in range(0, CW, 1024):
                c1 = min(c0 + 1024, CW)
                convert(winb[:, kc * CW + c0: kc * CW + c1], win_d[kc * 128:(kc + 1) * 128, c0:c1], c1 - c0,
                        gcol[:, kc:kc + 1])
        for c in range(4):
            convert(wab[:, c * D:(c + 1) * D], wa_d[c * 128:(c + 1) * 128, :], D, cscale=0.5)
            convert(wbb[:, c * D:(c + 1) * D], wb_d[c * 128:(c + 1) * 128, :], D, cscale=0.5)
        for c in range(8):
            convert(wob[:, c * D:(c + 1) * D], wo_d[c * 128:(c + 1) * 128, :], D, cscale=0.5)
        for c0 in range(0, 5120, 1024):
            convert(biasb[:, c0:c0 + 1024], bias_d[:, c0:c0 + 1024], 1024)

        SCR = ["stg0", "stg1", "stg2", "stg3"]
        if stop == 'setup':
            NT = 0

        def proj_tok(bk, col0, width, off=0, first=True, last=True):
            for kc in range(8):
                mm(bank(bk)[:, off:off + width], xnT[:, kc * 128:(kc + 1) * 128],
                   winb[:, kc * CW + col0: kc * CW + col0 + width], kc == 0, kc == 7,
                   ["xnT", "wts"], [K[bk]], inc=(kc == 7 and last))

        def proj_feat(bk, col0, off):
            for kc in range(8):
                mm(bank(bk)[:, off:off + 128], winb[:, kc * CW + col0: kc * CW + col0 + 128],
                   xnT[:, kc * 128:(kc + 1) * 128], kc == 0, kc == 7, ["xnT", "wts"], [K[bk]], inc=(kc == 7))

        def rope_apply(dst3, src3, cos2, sin2, H, half, tmp, reads, writes):
            n = H * half
            cb = cos2.unsqueeze(1).broadcast_to([128, H, half])
            sb = sin2.unsqueeze(1).broadcast_to([128, H, half])
            t = [tmp[:, i * n:(i + 1) * n].rearrange("p (h d) -> p h d", h=H) for i in range(4)]
            x1 = src3[:, :, 0:half]
            x2 = src3[:, :, half:2 * half]
            rr = list(reads) + ["rope"]
            tt(t[0], x1, cb, ALU.mult, rr, ["rtmp"])
            tt(t[1], x2, sb, ALU.mult, rr, ["rtmp"])
            tt(t[2], x2, cb, ALU.mult, rr, ["rtmp"])
            tt(t[3], x1, sb, ALU.mult, rr, ["rtmp"])
            tt(dst3[:, :, 0:half], t[0], t[1], ALU.subtract, ["rtmp"], writes)
            tt(dst3[:, :, half:2 * half], t[2], t[3], ALU.add, ["rtmp"], writes)

        rtmp = sm

        for b in range(NT):
            slot = b % 5
            N = 128 * (b + 1)
            rp = rope[:, b * 40:(b + 1) * 40]
            x_sb = PT[:].bitcast(F32)
            PTALL = ["PT0", "PT1", "R2", "R3"]
            xnb = Rr[:]
            XNB = ["R0", "R1"]
            rt = sc[:, 1536:2560]

            def x_norm_stats():
                act(xnb, x_sb, AF.Square, PTALL, XNB + ["nsm"], accum=sm[:, 0:1])
                act(sm[:, 1:2], sm[:, 0:1], AF.Sqrt, ["nsm"], ["nsm"], scale=1.0 / D, bias=sm[:, 63:64])
                P.op("dve", lambda e, o=sm[:, 2:3], i=sm[:, 1:2]: e.reciprocal(out=o, in_=i), reads=["nsm"], writes=["nsm"])

            def x_norm_apply():
                ts(xnb, x_sb, sm[:, 2:3], None, ALU.mult, None, ["nsm"] + PTALL, XNB)
                P.op("dve", lambda e, a=PT[0:64, 0:1024]: e.memset(a, 0.0), reads=[], writes=["PT0"])

            def x_transposes():
                for kc in range(8):
                    tr(bankb(7)[:, kc * 128:(kc + 1) * 128], xnb[:, kc * 128:(kc + 1) * 128], XNB, [K[7]], inc=(kc == 7))

            if b == 0:
                dma("sp", x_sb, x_d[0:128, :], [], PTALL)
                x_norm_stats()
                x_norm_apply()
                x_transposes()
                cp(xnT[:], bankb(7), [K[7]], ["xnT"])
            act(dmy[:, 0:1], dmy[:, 1:2], AF.Tanh, [], ["dmy"])
            if stop == 'p1a':
                break
            proj_tok(0, VA, 512)
            va_dst = vaug[:, slot * 520:(slot + 1) * 520].rearrange("p (h d) -> p h d", h=8)[:, :, 0:64]
            act(va_dst, bank(0).rearrange("p (h d) -> p h d", h=8), AF.Copy, [K[0]], [f"vaug{slot}"])
            proj_tok(1, GA, 512)
            tnh = sc[:, 2816:3328]
            act(tnh, bank(1), AF.Tanh, [K[1]] + SCR, SCR, scale=0.5)
            stt(gas[:], tnh, 1.0, bank(1), ALU.add, ALU.mult, [K[1]] + SCR, ["gas"])
            if stop == 'p1b':
                break
            proj_tok(2, QB, 512)
            QI3 = QI[:].rearrange("p (h d) -> p h d", h=8)
            b03 = bank(2).rearrange("p (h d) -> p h d", h=8)
            ts(QI3[:, :, 16:64], b03[:, :, 16:64], 0.125, None, ALU.mult, None, [K[2]], ["QI"])
            rope_apply(QI3[:, :, 0:16], b03[:, :, 0:16], rp[:, 16:24], rp[:, 24:32], 8, 8, rt, [K[2]] + SCR, ["QI"] + SCR)
            if stop == 'p1c':
                break
            proj_tok(3, KBO, 128, off=0, last=False)
            proj_tok(3, IQ, 296, off=128)
            for hp in range(4):
                proj_feat(4, QA + hp * 128, hp * 128)
            act(qaT[:], bank(4), AF.Copy, [K[4]], ["qaT"], scale=0.125)
            for hp in range(4):
                proj_feat(5, KA + hp * 128, hp * 128)
            proj_tok(6, GB, 512)
            b1 = bank(3)
            rope_apply(KI[:, 0:64].rearrange("p (h d) -> p h d", h=1), b1[:, 0:64].rearrange("p (h d) -> p h d", h=1),
                       rp[:, 0:8], rp[:, 8:16], 1, 8, rt, [K[3]] + SCR, ["KI"] + SCR)
            cp(KI[:, 16:64], b1[:, 16:64], [K[3]], ["KI"])
            cp(vbc[:, b * 128:b * 128 + 64], b1[:, 64:128], [K[3]], ["vbc"])
            iw_ps = b1[:, 128 + 288:128 + 296]
            ts(sm[:, 16:24], iw_ps, 0.0, 2.0, ALU.is_ge, ALU.mult, [K[3]], ["sm"])
            ts(sm[:, 16:24], sm[:, 16:24], -1.0, None, ALU.add, None, ["sm"], ["sm"])
            stt(sm[:, 8:16], iw_ps, 0.0625, sm[:, 16:24], ALU.mult, ALU.mult, [K[3], "sm"], ["sm"])
            iqs = sc[:, 2560:2816].rearrange("p (h d) -> p h d", h=8)
            tt(iqs, b1[:, 128:384].rearrange("p (h d) -> p h d", h=8),
               sm[:, 8:16].unsqueeze(2).broadcast_to([128, 8, 32]), ALU.mult, [K[3], "sm"] + SCR, SCR)
            rope_apply(QI3[:, :, 64:72], iqs[:, :, 0:8], rp[:, 32:36], rp[:, 36:40], 8, 4, rt, SCR, ["QI"] + SCR)
            cp(QI3[:, :, 72:96], iqs[:, :, 8:32], SCR, ["QI"])
            rope_apply(KI[:, 64:72].rearrange("p (h d) -> p h d", h=1),
                       b1[:, 384:392].rearrange("p (h d) -> p h d", h=1),
                       rp[:, 32:36], rp[:, 36:40], 1, 4, rt, [K[3]] + SCR, ["KI"] + SCR)
            cp(KI[:, 72:96], b1[:, 392:416], [K[3]], ["KI"])
            tt(Dh[:].rearrange("p (h q) -> p h q", h=8), ident.unsqueeze(1).broadcast_to([128, 8, 128]),
               sm[:, 16:24].unsqueeze(2).broadcast_to([128, 8, 128]), ALU.mult, ["cbf", "sm"], ["Dh"])
            if stop == 'p1d':
                break
            for h in range(8):
                tr(bankb(7)[:, h * 128:(h + 1) * 128], QI[:, h * 128:(h + 1) * 128], ["QI"], [K[7]], inc=(h == 7))
            cp(QIT[:], bankb(7), [K[7]], ["QIT"])
            tr(bankb(0)[:, 0:128], KI[:], ["KI"], [K[0]], inc=True)
            cp(kcache[:, b * 128:(b + 1) * 128], bankb(0)[:, 0:128], [K[0]], ["kcache"])
            ka_dst = kaT[:].rearrange("p (c s t) -> p c s t", c=4, s=5)[:, :, slot, :]
            cp(ka_dst, bank(5).rearrange("p (c t) -> p c t", c=4), [K[5]], [f"kaT{slot}"])
            tnh2 = sc[:, 3328:3840]
            act(tnh2, bank(6), AF.Tanh, [K[6]] + SCR, SCR, scale=0.5)
            stt(gbT[:], tnh2, 1.0, bank(6), ALU.add, ALU.mult, [K[6]] + SCR, ["gbT"])
            if dbg and b == NT - 1:
                cp(sc[:, 3072:4096], QIT[:], ["QIT"], SCR)
                dma("sp", dbg_d["d_qit"], sc[:, 3072:4096], SCR, [])
                cp(sc[:, 3072:3584], kcache[:, 0:512], ["kcache"], SCR)
                dma("sp", dbg_d["d_kc"], sc[:, 3072:3584], SCR, [])

            if stop == 'p1':
                break
            jjs = [jj for jj in range(5) if b - 4 + jj >= 0]
            yacc_v = [Rr[:].bitcast(F32)[:, 0:260], Dh[:].bitcast(F32)[:, 0:260]]
            yacc_r = [["R0", "R1"], ["Dh"]]
            yaf_all = QI[:].bitcast(F32)

            def a_keytile(ji):
                jj = jjs[ji]
                j = b - 4 + jj
                sj = j % 5
                pts = ji % 2
                for half in range(2):
                    bk = 4 + half
                    for hh in range(4):
                        h = half * 4 + hh
                        pr = (h % 2) * 64
                        c = h // 2
                        mm(bank(bk)[:, hh * 128:(hh + 1) * 128],
                           kaT[pr:pr + 64, (c * 5 + sj) * 128:(c * 5 + sj + 1) * 128],
                           qaT[pr:pr + 64, c * 128:(c + 1) * 128], True, False,
                           [f"kaT{sj}", "qaT"], [K[bk]])
                        mm(bank(bk)[:, hh * 128:(hh + 1) * 128], ident,
                           biasb[:, (jj * 8 + h) * 128:(jj * 8 + h + 1) * 128], False, True,
                           ["cbf", "wts"], [K[bk]])
                pta = PT[:, pts * 1024:(pts + 1) * 1024]
                for half in range(2):
                    act(pta[:, half * 512:(half + 1) * 512], bank(4 + half), AF.Exp, [K[4 + half]], PTW[pts])
                for h in range(8):
                    bk = 6 + h // 4
                    o = (h % 4) * 65
                    mm(bank(bk)[:, o:o + 65], pta[:, h * 128:(h + 1) * 128],
                       vaug[:, (sj * 8 + h) * 65:(sj * 8 + h + 1) * 65], True, True,
                       [f"PT{pts}", f"vaug{sj}"], [K[bk]])
                for k2 in range(2):
                    if ji == 0:
                        cp(yacc_v[k2], bank(6 + k2)[:, 0:260], [K[6 + k2]], yacc_r[k2])
                    else:
                        tt(yacc_v[k2], yacc_v[k2], bank(6 + k2)[:, 0:260], ALU.add, [K[6 + k2]] + yacc_r[k2], yacc_r[k2])

            def a_finish():
                for k2 in range(2):
                    yv = yacc_v[k2].rearrange("p (h d) -> p h d", h=4)
                    rec = sm[:, 24 + 4 * k2:28 + 4 * k2].rearrange("p (h o) -> p h o", h=4, o=1)
                    P.op("dve", lambda e, o=rec, i=yv[:, :, 64:65]: e.reciprocal(out=o, in_=i),
                         reads=yacc_r[k2], writes=["sm"])
                    yaf = yaf_all[:, k2 * 256:(k2 + 1) * 256].rearrange("p (h d) -> p h d", h=4)
                    tt(yaf, yv[:, :, 0:64], rec.broadcast_to([128, 4, 64]), ALU.mult, ["sm", "QI"] + yacc_r[k2], ["QI"])
                tt(yab[:], yaf_all, gas[:], ALU.mult, ["QI", "gas"], ["yab"])
                for c in range(4):
                    tr(bankb(0)[:, c * 128:(c + 1) * 128], yab[:, c * 128:(c + 1) * 128], ["yab"], [K[0]])
                act(yaT[:], bankb(0)[:, 0:512], AF.Copy, [K[0]], ["yaT"])

            if stop == 'A':
                break
            nblk = (N + 511) // 512
            items = [(kb, h) for kb in range(nblk) for h in range(8)]

            LB = (0, 1, 4, 5)
            RS = (Rr[:, 0:512], Rr[:, 512:1024], PT[:, 1024:1536], PT[:, 1536:2048])
            RSR = ("R0", "R1", "R2", "R3")

            def idx_L(i):
                kb, h = items[i]
                k0 = kb * 512
                w = min(512, N - k0)
                lb = LB[i % 4]
                mm(bank(lb)[:, 0:w], IQz[:, h * 128:(h + 1) * 128], kcache[:, k0:k0 + w],
                   True, True, ["PT0", "kcache"], [K[lb]])
                rs = i % 4
                if h not in (1, 3, 5):
                    act(RS[rs][:, 0:w], bank(lb)[:, 0:w], AF.Relu, [K[lb]], [RSR[rs]])
                else:
                    ts(RS[rs][:, 0:w], bank(lb)[:, 0:w], 0.0, None, ALU.max, None, [K[lb]], [RSR[rs]])

            def idx_D(i):
                kb, h = items[i]
                k0 = kb * 512
                w = min(512, N - k0)
                rs = i % 4
                sb = 2 + kb % 2
                mm(bank(sb)[:, 0:w], Dh[:, h * 128:(h + 1) * 128], RS[rs][:, 0:w], h == 0, h == 7,
                   ["Dh", RSR[rs]], [K[sb]])
                if h == 7:
                    ts(sc[:, k0:k0 + w], bank(sb)[:, 0:w], 0.0, None, ALU.add, ALU.max, [K[sb]], SCR + ["bst"],
                       accum=bst[:, kb:kb + 1])

            IQz = PT[:, 0:1024]
            cp(IQz[64:128, :], QIT[64:128, :], ["QIT"], ["PT0", "PT1", "R2", "R3"])
            LA = 3
            for i in range(min(LA, len(items))):
                idx_L(i)
            for i in range(len(items)):
                idx_D(i)
                if i + LA < len(items):
                    idx_L(i + LA)
            P.op("dve", lambda e, a=sc[0:64, N - 64:N]: e.memset(a, -1e30), reads=[], writes=SCR)
            if dbg and b == NT - 1:
                dma("sp", dbg_d["d_sc"], sc[:, 0:512], SCR, [])
            if stop == 'idx':
                break
            LO, HI, D0, MID, CNT, U, SA, T2 = (sm[:, 32:33], sm[:, 33:34], sm[:, 34:35], sm[:, 35:36], sm[:, 36:37],
                                               sm[:, 37:38], sm[:, 38:39], sm[:, 39:40])
            if b < 2:
                P.op("dve", lambda e, a=LO: e.memset(a, -1e29), reads=[], writes=["bLO"])
                for ji in range(len(jjs)):
                    a_keytile(ji)
                a_finish()
            else:
                P.op("dve", lambda e, o=HI, i=bst[:, 0:nblk]: e.tensor_reduce(out=o, in_=i, axis=AX.X, op=ALU.max),
                     reads=["bst"], writes=["bHI"])
                P.op("dve", lambda e, o=LO, i=sc[:, 0:256]: e.tensor_reduce(out=o, in_=i, axis=AX.X, op=ALU.min),
                     reads=SCR, writes=["bLO"])
                tt(D0, HI, LO, ALU.subtract, ["bHI", "bLO"], ["bD0"])
                ts(DI, CI, D0, None, ALU.mult, None, ["bD0", "CI"], ["bDI"])
                n1 = 64 * max(1, int(round(DVE_SHARE * N / 64)))
                nA = N - n1
                jo2 = junk2[:, 0:64].unsqueeze(1).broadcast_to([128, nA // 64, 64])
                sview2 = sc[:, n1:N].rearrange("p (a c) -> p a c", c=64)
                tt(MID, LO, DI[:, 0:1], ALU.add, ["bLO", "bDI"], ["bMID"])
                jo = junk[:, 0:64].unsqueeze(1).broadcast_to([128, n1 // 64, 64])
                sview = sc[:, 0:n1].rearrange("p (a c) -> p a c", c=64)
                a_at = {3 * ji + 3: ji for ji in range(len(jjs))}
                MIDm = sm[:, 43:44]
                thrp = (511.0 - nA) / 2.0
                for it in range(NBIS):
                    last = it == NBIS - 1
                    stt(MIDm, DI[:, it:it + 1], -1.0 if last else -0.5, MID, ALU.mult, ALU.add, ["bDI", "bMID"], ["bMIDm"])
                    ts(jo, sview, MID, thrp, ALU.is_ge, ALU.subtract, SCR + ["bMID"], ["junk", "bCNT"], accum=CNT)
                    act(jo2, sview2, AF.Sign, SCR + ["bMID"], ["junk2", "bSA"], scale=-1.0, bias=MID, accum=SA)
                    stt(U, SA, -0.5, CNT, ALU.mult, ALU.is_ge, ["bCNT", "bSA"], ["bU"])
                    if not last:
                        stt(MID, U, DI[:, it:it + 1], MIDm, ALU.mult, ALU.add, ["bU", "bDI", "bMIDm"], ["bMID"])
                    else:
                        stt(LO, U, DI[:, NBIS - 1:NBIS], MIDm, ALU.mult, ALU.add, ["bU", "bDI", "bMIDm"], ["bLO"])
                    if it in a_at:
                        a_keytile(a_at[it])
                a_finish()
            if dbg and b == NT - 1:
                jo = junk[:, 0:64].unsqueeze(1).broadcast_to([128, N // 64, 64])
                sview = sc[:, 0:N].rearrange("p (a c) -> p a c", c=64)
                ts(jo, sview, LO, None, ALU.is_ge, ALU.add, SCR + ["bLO"], ["junk", "bCNT"], accum=CNT)
                dma("sp", dbg_d["d_lo"], LO, ["bLO"], [])
                dma("sp", dbg_d["d_cnt"], CNT, ["bCNT"], [])
            if stop == 'bis':
                break
            QBz = Dh
            P.op("dve", lambda e, a=QBz[64:128, :]: e.memset(a, 0.0), reads=[], writes=["Dh"])
            cp(QBz[0:64, :], QIT[0:64, :], ["QIT"], ["Dh"])

            def b_stage1(j):
                ns = j % 2
                nm = nmr[:, ns * 128:(ns + 1) * 128]
                ts(nm, sc[:, j * 128:(j + 1) * 128], LO, NEGM, ALU.is_lt, ALU.mult, SCR + ["bLO"], [f"nm{ns}"])
                pts = j % 2
                ptb = PT[:, pts * 1024:(pts + 1) * 1024]
                sb0 = 2 if j % 2 == 0 else 6
                for half in range(2):
                    bk = sb0 + half
                    mm(bank(bk), kcache[:, j * 128:(j + 1) * 128], QBz[:, half * 512:(half + 1) * 512],
                       True, False, ["kcache", "Dh"], [K[bk]])
                    mm(bank(bk), nm, I4, False, True, [f"nm{ns}", "cbf"], [K[bk]])
                for half in range(2):
                    act(ptb[:, half * 512:(half + 1) * 512], bank(sb0 + half), AF.Exp, [K[sb0 + half]], PTW[pts])

            def b_stage2(j):
                pts = j % 2
                ptb = PT[:, pts * 1024:(pts + 1) * 1024]
                for half in range(2):
                    mm(bank(4 + half), vbc[:, j * 128:(j + 1) * 128], ptb[:, half * 512:(half + 1) * 512],
                       j == 0, j == b, [f"PT{pts}", "vbc"], [K[4 + half]])

            for j in range(b + 1):
                b_stage1(j)
                if j >= 1:
                    b_stage2(j - 1)
            b_stage2(b)
            act(sc[:, 0:512], bank(4), AF.Copy, [K[4]] + SCR, SCR)
            cp(sc[:, 512:1024], bank(5), [K[5]] + SCR, SCR)
            for h in range(8):
                P.op("pe", lambda e, o=bank(h // 4)[:, (h % 4) * 128:(h % 4 + 1) * 128],
                     i=sc[:, h * 128:(h + 1) * 128]: e.transpose(out=o, in_=i, identity=identf[:]),
                     reads=SCR + ["identf"], writes=[K[h // 4]])
            for k2 in range(2):
                yv = bank(k2).rearrange("p (h d) -> p h d", h=4)
                rec = sm[:, 24 + 4 * k2:28 + 4 * k2].rearrange("p (h o) -> p h o", h=4, o=1)
                P.op("dve", lambda e, o=rec, i=yv[:, :, 64:65]: e.reciprocal(out=o, in_=i), reads=[K[k2]], writes=["sm"])
                ybf = sc[:, 1024 + k2 * 256:1024 + (k2 + 1) * 256].rearrange("p (h d) -> p h d", h=4)
                tt(ybf, yv[:, :, 0:64], rec.broadcast_to([128, 4, 64]), ALU.mult, [K[k2], "sm"] + SCR, SCR)
            tt(yab[:], sc[:, 1024:1536], gbT[:], ALU.mult, SCR + ["gbT"], ["yab"])
            for c in range(4):
                tr(bankb(2)[:, c * 128:(c + 1) * 128], yab[:, c * 128:(c + 1) * 128], ["yab"], [K[2]])
            cp(ybT[:], bankb(2)[:, 0:512], [K[2]], ["ybT"])
            if dbg and b == NT - 1:
                cp(sc[:, 3072:3584], ybT[:], ["ybT"], SCR)
                dma("sp", dbg_d["d_yb"], sc[:, 3072:3584], SCR, [])

            if stop == 'B':
                break
            hbuf = sc[:, 2048:3072]
            dma("sp", hbuf, x_d[b * 128:(b + 1) * 128, :], [], SCR)
            if b + 1 < NT:
                dma("sp", x_sb, x_d[(b + 1) * 128:(b + 2) * 128, :], [], PTALL)
            for nb in range(2):
                cs = slice(nb * 512, (nb + 1) * 512)
                for c in range(4):
                    mm(bank(0), yaT[:, c * 128:(c + 1) * 128], wab[:, c * D + nb * 512: c * D + (nb + 1) * 512],
                       c == 0, c == 3, ["yaT", "wts"], [K[0]], inc=(c == 3))
                for c in range(4):
                    mm(bank(1), ybT[:, c * 128:(c + 1) * 128], wbb[:, c * D + nb * 512: c * D + (nb + 1) * 512],
                       c == 0, c == 3, ["ybT", "wts"], [K[1]], inc=(c == 3))
                for gi, (zc, bk) in enumerate(((ZA, 2), (ZB, 3))):
                    for kc in range(8):
                        mm(bank(bk), xnT[:, kc * 128:(kc + 1) * 128],
                           winb[:, kc * CW + zc + nb * 512: kc * CW + zc + (nb + 1) * 512], kc == 0, False,
                           ["xnT", "wts"], [K[bk]])
                    mm(bank(bk), ones[0:1, 0:128], bmb[0:1, gi * D + nb * 512: gi * D + (nb + 1) * 512], False, True,
                       ["cbf", "bmb"], [K[bk]], inc=True)
                g_a = sc[:, 0:512]
                g_b = sc[:, 512:1024]
                act(g_a, bank(2), AF.Tanh, [K[2]] + SCR, SCR, scale=0.5)
                act(g_b, bank(3), AF.Tanh, [K[3]] + SCR, SCR, scale=0.5)
                stt(g_a, g_a, 1.0, bank(0), ALU.add, ALU.mult, [K[0]] + SCR, SCR)
                stt(g_b, g_b, 1.0, bank(1), ALU.add, ALU.mult, [K[1]] + SCR, SCR)
                tt(mgb[:, cs], g_a, g_b, ALU.add, SCR, ["QI"])
            if dbg and b == NT - 1:
                cp(sc[:, 3072:4096], mgb[:], ["QI"], SCR)
                dma("sp", dbg_d["d_mg"], sc[:, 3072:4096], SCR, [])
            act(dmy[:, 0:1], dmy[:, 1:2], AF.Sqrt, [], ["dmy"])
            if b + 1 < NT:
                x_norm_stats()
                x_norm_apply()
            for kc in range(8):
                tr(bankb(4)[:, kc * 128:(kc + 1) * 128], mgb[:, kc * 128:(kc + 1) * 128], ["QI"], [K[4]], inc=(kc == 7))
            cp(mgT[:], bankb(4), [K[4]], ["QIT"])
            if b + 1 < NT:
                x_transposes()
                act(xnT[:], bankb(7), AF.Copy, [K[7]], ["xnT"])
            for nb in range(2):
                for kc in range(8):
                    mm(bank(5 + nb), mgT[:, kc * 128:(kc + 1) * 128],
                       wob[:, kc * D + nb * 512: kc * D + (nb + 1) * 512], kc == 0, kc == 7,
                       ["QIT", "wts"], [K[5 + nb]], inc=(kc == 7))
            for nb in range(2):
                tt(hbuf[:, nb * 512:(nb + 1) * 512], hbuf[:, nb * 512:(nb + 1) * 512], bank(5 + nb), ALU.add,
                   [K[5 + nb]] + SCR, SCR)
            act(sc[:, 0:1024], hbuf, AF.Square, SCR, SCR + ["sm"], accum=sm[:, 40:41])
            act(sm[:, 41:42], sm[:, 40:41], AF.Sqrt, ["sm"], ["sm"], scale=1.0 / D, bias=sm[:, 63:64])
            P.op("dve", lambda e, o=sm[:, 42:43], i=sm[:, 41:42]: e.reciprocal(out=o, in_=i), reads=["sm"], writes=["sm"])
            stt(hbuf, hbuf, sm[:, 42:43], fgb[:], ALU.mult, ALU.mult, SCR + ["sm", "fgb"], SCR)
            dma("sp", out_d[b * 128:(b + 1) * 128, :], hbuf, SCR, [])

        P.finish_waits("sp")
        P.emit()
    return nc


def _host_consts():
    bf = ml_dtypes.bfloat16
    cb = np.zeros((128, 896), np.float32)
    cb[:, 0:128] = np.eye(128)
    for r in range(4):
        cb[:, 128 + r * 128:128 + (r + 1) * 128] = np.eye(128)
    cb[:, 640:768] = 1.0
    cb = cb.astype(bf)
    pos = np.arange(S, dtype=np.float32)
    invB = (np.float32(500000.0) ** (-np.arange(8, dtype=np.float32) / np.float32(8))).astype(np.float32)
    invI = (np.float32(500000.0) ** (-np.arange(4, dtype=np.float32) / np.float32(4))).astype(np.float32)
    angB = (pos[:, None] * invB[None, :]).astype(np.float32)
    angI = (pos[:, None] * invI[None, :]).astype(np.float32)
    tab = np.concatenate([np.cos(angB), np.sin(angB), np.cos(angB) * 0.125, np.sin(angB) * 0.125,
                          np.cos(angI), np.sin(angI)], axis=1).astype(np.float32)
    rope = np.ascontiguousarray(tab.reshape(32, 128, 40).transpose(1, 0, 2).reshape(128, 32 * 40))
    kk = np.arange(128)[:, None, None]
    jj = np.arange(5)[None, :, None]
    q = np.arange(128)[None, None, :]
    dist = 128 * (4 - jj) + q - kk
    idx = np.clip(dist, -256, 256) + 256
    dchunk = 2 * (jj - 4) + (kk >= 64).astype(np.int64) - (q >= 64).astype(np.int64)
    bad = (dchunk < -8) | (dchunk > 0)
    return cb, rope, idx, bad


def _bias_table(rel_bias, idx, bad):
    t = rel_bias[:, idx]
    t = np.where(bad[None], np.float32(NEGM), t).astype(np.float32)
    return np.ascontiguousarray(t.transpose(1, 2, 0, 3).reshape(128, 5 * 8 * 128))


def make_in_maps(x, norm_gain, w_in, b_merge, rel_bias, w_branch_a, w_branch_b, w_out, final_norm_gain):
    cb, rope, idx, bad = _host_consts()
    shared = {
        "w_in": np.ascontiguousarray(w_in[0], dtype=np.float32),
        "wa": np.ascontiguousarray(w_branch_a[0], dtype=np.float32),
        "wb": np.ascontiguousarray(w_branch_b[0], dtype=np.float32),
        "wo": np.ascontiguousarray(w_out[0], dtype=np.float32),
        "gcol": np.ascontiguousarray(np.asarray(norm_gain[0], np.float32).reshape(8, 128).T),
        "fgain": np.ascontiguousarray(np.asarray(final_norm_gain, np.float32).reshape(1, D)),
        "bmrow": np.ascontiguousarray(np.asarray(b_merge[0], np.float32).reshape(1, 2 * D)),
        "biasT": _bias_table(np.asarray(rel_bias[0], np.float32), idx, bad),
        "rope": rope,
        "cbf": cb,
        "identf": np.eye(128, dtype=np.float32),
    }
    return [dict(shared, x=np.ascontiguousarray(x[i], dtype=np.float32)) for i in range(x.shape[0])]


def kernel(x, norm_gain, w_in, b_merge, rel_bias, w_branch_a, w_branch_b, w_out, final_norm_gain):
    args = [np.asarray(a) for a in (x, norm_gain, w_in, b_merge, rel_bias, w_branch_a, w_branch_b, w_out,
                                    final_norm_gain)]
    in_maps = make_in_maps(*args)
    nc = build()
    res = run_bass_kernel_spmd(nc, in_maps, core_ids=list(range(8)))
    return np.stack([np.asarray(r["out"], dtype=np.float32) for r in res.results], axis=0)
```

```python
import contextlib
import numpy as np
import ml_dtypes
import concourse.bass as bass
import concourse.mybir as mybir
from concourse.bass_utils import run_bass_kernel_spmd

F32 = mybir.dt.float32
BF16 = mybir.dt.bfloat16
ALU = mybir.AluOpType
AF = mybir.ActivationFunctionType
AX = mybir.AxisListType

S = 4096
D = 1024
NTILES = 32
CW = 5544
QA, KA, VA, GA, QB, KBO, VBO, GB, IQ, IK, IW, ZA, ZB = (
    0, 512, 1024, 1536, 2048, 2560, 2624, 2688, 3200, 3456, 3488, 3496, 4520)
NBIS = 16
DVE_SHARE = 0.43
NEGM = -30000.0

COMPUTE = ("pe", "act", "dve", "pool")
NDMASEM = 8


class Prog:
    def __init__(self, nc, dma_queues=("sp", "pool")):
        self.nc = nc
        self.ops = {e: [] for e in ("pe", "act", "dve", "pool", "sp")}
        self.cnt = {}
        self.lastw = {}
        self.reads = {}
        self.waited = {e: {} for e in self.ops}
        self.dma_i = {q: 0 for q in dma_queues}
        self.timelines = list(COMPUTE) + [f"dma_{q}{i}" for q in dma_queues for i in range(NDMASEM)]
        for t in self.timelines:
            self.cnt[t] = 0
        self.marked = {e: set() for e in COMPUTE}

    def _need(self, eng, waits, tl, val):
        if tl == eng and eng == "pe":
            return
        if self.waited[eng].get(tl, 0) >= val:
            return
        waits[tl] = max(waits.get(tl, 0), val)

    def _deps(self, eng, tl_self, reads, writes):
        waits = {}
        for r in reads:
            lw = self.lastw.get(r)
            if lw:
                self._need(eng, waits, lw[0], lw[1])
        for w in writes:
            lw = self.lastw.get(w)
            if lw and lw[0] != tl_self:
                self._need(eng, waits, lw[0], lw[1])
            for tl, v in self.reads.get(w, {}).items():
                if tl != tl_self:
                    self._need(eng, waits, tl, v)
        for tl, v in waits.items():
            self.waited[eng][tl] = max(self.waited[eng].get(tl, 0), v)
            if tl in self.marked:
                self.marked[tl].add(v)
        return waits

    def _commit(self, tl, val, reads, writes):
        for r in reads:
            self.reads.setdefault(r, {})[tl] = val
        for w in writes:
            self.lastw[w] = (tl, val)
            self.reads[w] = {}

    def op(self, eng, fn, reads=(), writes=(), inc=True):
        waits = self._deps(eng, eng, reads, writes)
        self.cnt[eng] += 1
        val = self.cnt[eng]
        self.ops[eng].append((waits, fn, (eng, val)))
        self._commit(eng, val, reads, writes)

    def dma(self, q, fn, reads=(), writes=()):
        i = self.dma_i[q]
        self.dma_i[q] += 1
        tl = f"dma_{q}{i % NDMASEM}"
        waits = self._deps(q, tl, reads, writes)
        prev = self.cnt[tl]
        if prev > 0 and self.waited[q].get(tl, 0) < prev:
            waits[tl] = max(waits.get(tl, 0), prev)
            self.waited[q][tl] = prev
        self.cnt[tl] += 16
        self.ops[q].append((waits, fn, (tl, 16)))
        self._commit(tl, self.cnt[tl], reads, writes)

    def finish_waits(self, eng="sp"):
        waits = {}
        for tl in self.timelines:
            if self.cnt[tl] > 0 and self.waited[eng].get(tl, 0) < self.cnt[tl]:
                waits[tl] = self.cnt[tl]
                if tl in self.marked:
                    self.marked[tl].add(self.cnt[tl])
        self.ops[eng].append((waits, None, None))

    def emit(self):
        nc = self.nc
        rank = {}
        for e in COMPUTE:
            rank[e] = {s: i + 1 for i, s in enumerate(sorted(self.marked[e]))}
        with contextlib.ExitStack() as st:
            sems = {t: st.enter_context(nc.semaphore("s_" + t)) for t in self.timelines}
            block = st.enter_context(nc.Block())

            def run(engname):
                def body(e):
                    for waits, fn, inc in self.ops[engname]:
                        for tl, v in waits.items():
                            e.wait_ge(sems[tl], rank[tl][v] if tl in rank else v)
                        if fn is None:
                            continue
                        ins = fn(e)
                        if inc is None:
                            continue
                        if inc[0] in rank:
                            if inc[1] in rank[inc[0]]:
                                ins.then_inc(sems[inc[0]], 1)
                        else:
                            ins.then_inc(sems[inc[0]], inc[1])
                return body

            block.sync(run("sp"))
            block.tensor(run("pe"))
            block.scalar(run("act"))
            block.vector(run("dve"))
            block.gpsimd(run("pool"))


def build(NT=NTILES, dbg=False, stop=None):
    nc = bass.Bass("TRN2", target_bir_lowering=False)
    din = lambda name, shape, dt=F32: nc.dram_tensor(name, shape, dt, kind="ExternalInput").ap()
    x_d = din("x", [S, D])
    win_d = din("w_in", [D, CW])
    wa_d = din("wa", [512, D])
    wb_d = din("wb", [512, D])
    wo_d = din("wo", [D, D])
    gcol_d = din("gcol", [128, 8])
    fg_d = din("fgain", [1, D])
    bm_d = din("bmrow", [1, 2 * D])
    bias_d = din("biasT", [128, 5120])
    rope_d = din("rope", [128, 32 * 40])
    idf_d = din("identf", [128, 128])
    cb_d = din("cbf", [128, 896], BF16)
    out_d = nc.dram_tensor("out", [S, D], F32, kind="ExternalOutput").ap()
    dbg_d = {}
    if dbg:
        for name, shape in (("d_qit", [128, 1024]), ("d_kc", [128, 512]), ("d_ya", [128, 512]),
                            ("d_sc", [128, 512]), ("d_lo", [128, 1]), ("d_cnt", [128, 1]),
                            ("d_yb", [128, 512]), ("d_mg", [128, 1024])):
            dbg_d[name] = nc.dram_tensor(name, shape, F32, kind="ExternalOutput").ap()

    with contextlib.ExitStack() as st:
        T = lambda name, shape, dt: st.enter_context(nc.sbuf_tensor(name, shape, dt))
        winb = T("winb", [128, 8 * CW], BF16)
        wab = T("wab", [128, 4 * D], BF16)
        wbb = T("wbb", [128, 4 * D], BF16)
        wob = T("wob", [128, 8 * D], BF16)
        sc = T("sc", [128, 4096], F32)
        biasb = T("biasb", [128, 5120], BF16)
        cbf = T("cbf_s", [128, 896], BF16)
        rope = T("rope_s", [128, 32 * 40], F32)
        identf = T("identf_s", [128, 128], F32)
        fgb = T("fgb", [128, D], F32)
        gcol = T("gcol_s", [128, 8], F32)
        bmb = T("bmb", [1, 2 * D], BF16)
        kaT = T("kaT", [128, 4 * 5 * 128], BF16)
        vaug = T("vaug", [128, 5 * 8 * 65], BF16)
        kcache = T("kcache", [128, S], BF16)
        vbc = T("vbc", [128, 32 * 128], BF16)
        xnT = T("xnT", [128, 8 * 128], BF16)
        qaT = T("qaT", [128, 4 * 128], BF16)
        gbT = T("gbT", [128, 4 * 128], BF16)
        gas = T("gas", [128, 512], BF16)
        QI = T("QI", [128, 8 * 128], BF16)
        KI = T("KI", [128, 128], BF16)
        QIT = T("QIT", [128, 8 * 128], BF16)
        mgb = QI
        mgT = QIT
        Dh = T("Dh", [128, 8 * 128], BF16)
        Rr = T("Rr", [128, 2 * 512], BF16)
        PT = T("PT", [128, 2 * 1024], BF16)
        nmr = T("nmr", [128, 2 * 128], BF16)
        junk = T("junk", [128, 64], BF16)
        junk2 = T("junk2", [128, 64], BF16)
        sm2 = T("sm2", [128, 32], F32)
        dmy = T("dmy", [128, 2], F32)
        bst = T("bst", [128, 8], F32)
        CI = sm2[:, 0:16]
        DI = sm2[:, 16:32]
        yab = T("yab", [128, 512], BF16)
        yaT = T("yaT", [128, 4 * 128], BF16)
        ybT = T("ybT", [128, 4 * 128], BF16)
        sm = T("sm", [128, 64], F32)
        PS = st.enter_context(nc.psum_tensor("ps", [128, 4096], F32))

        ident = cbf[:, 0:128]
        I4 = cbf[:, 128:640]
        ones = cbf[:, 640:768]
        bank = lambda i: PS[:, i * 512:(i + 1) * 512]
        bankb = lambda i: PS[:, i * 512:(i + 1) * 512].bitcast(BF16)
        K = [f"K{i}" for i in range(8)]
        PTW = (['PT0'], ['PT1', 'R2', 'R3'])

        P = Prog(nc)

        def mm(out, lhsT, rhs, start, stop, reads, writes, inc=False):
            P.op("pe", lambda e: e.matmul(out, lhsT=lhsT, rhs=rhs, start=start, stop=stop),
                 reads=reads, writes=writes, inc=inc)

        def tr(out, in_, reads, writes, inc=False):
            P.op("pe", lambda e: e.transpose(out=out, in_=in_, identity=ident),
                 reads=list(reads) + ["cbf"], writes=writes, inc=inc)

        def act(out, in_, func, reads, writes, scale=None, bias=None, accum=None):
            kw = {}
            if scale is not None:
                kw["scale"] = scale
            if bias is not None:
                kw["bias"] = bias
            if accum is not None:
                kw["accum_out"] = accum
            P.op("act", lambda e: e.activation(out=out, in_=in_, func=func, **kw), reads=reads, writes=writes)

        def ts(out, in0, s1, s2, op0, op1, reads, writes, accum=None, eng="dve"):
            kw = {}
            if op1 is not None:
                kw["op1"] = op1
            if accum is not None:
                kw["accum_out"] = accum
            P.op(eng, lambda e: e.tensor_scalar(out=out, in0=in0, scalar1=s1, scalar2=s2, op0=op0, **kw),
                 reads=reads, writes=writes)

        def tt(out, in0, in1, op, reads, writes, eng="dve"):
            P.op(eng, lambda e: e.tensor_tensor(out=out, in0=in0, in1=in1, op=op), reads=reads, writes=writes)

        def stt(out, in0, scalar, in1, op0, op1, reads, writes, eng="dve"):
            P.op(eng, lambda e: e.scalar_tensor_tensor(out=out, in0=in0, scalar=scalar, in1=in1, op0=op0, op1=op1),
                 reads=reads, writes=writes)

        def cp(out, in_, reads, writes, eng="dve"):
            P.op(eng, lambda e: e.tensor_copy(out=out, in_=in_), reads=reads, writes=writes)

        def dma(q, out, in_, reads, writes):
            P.dma(q, lambda e: e.dma_start(out=out, in_=in_), reads=reads, writes=writes)

        dma("sp", cbf[:], cb_d, [], ["cbf"])
        dma("sp", identf[:], idf_d, [], ["identf"])
        dma("sp", rope[:], rope_d, [], ["rope"])
        dma("sp", gcol[:], gcol_d, [], ["gcol"])
        dma("sp", fgb[:], fg_d.partition_broadcast(128), [], ["fgb"])
        dma("sp", sc[0:1, 0:2 * D], bm_d, [], ["stg0", "stg1", "stg2", "stg3"])
        cp(bmb[:], sc[0:1, 0:2 * D], ["stg0", "stg1", "stg2", "stg3"], ["bmb"])
        P.op("dve", lambda e: e.memset(vaug[:], 1.0), writes=["vaug0", "vaug1", "vaug2", "vaug3", "vaug4"])
        P.op("dve", lambda e: e.memset(KI[:], 0.0), writes=["KI"])
        P.op("dve", lambda e: e.memset(vbc[:], 1.0), writes=["vbc"])
        P.op("pool", lambda e: e.memset(sm[:, 63:64], 1e-6), writes=["sm"])
        P.op("pool", lambda e: e.memset(dmy[:], 1.0), writes=["dmy"])
        for _i in range(16):
            P.op("pool", lambda e, a=sm2[:, _i:_i + 1], v=2.0 ** -(_i + 1): e.memset(a, v), writes=["CI"])
        P.op("dve", lambda e: e.memset(QI[:], 0.0), writes=["QI"])

        stage_i = [0]
        ENG_ROT = ("dve", "act")

        def convert(dst, src_dram, width, scale_ap=None, cscale=None):
            if cscale is not None:
                scale_ap = cscale
            i = stage_i[0]
            stage_i[0] += 1
            s = i % 4
            stg = sc[:, s * 1024: s * 1024 + width]
            res = f"stg{s}"
            dma("sp" if i % 2 == 0 else "pool", stg, src_dram, [], [res])
            eng = ENG_ROT[i % len(ENG_ROT)]
            if eng == "act":
                if scale_ap is None:
                    act(dst, stg, AF.Copy, [res], ["wts"])
                else:
                    act(dst, stg, AF.Copy, [res, "gcol"], ["wts"], scale=scale_ap)
            elif scale_ap is None:
                cp(dst, stg, [res], ["wts"], eng=eng)
            else:
                ts(dst, stg, scale_ap, None, ALU.mult, None, [res, "gcol"], ["wts"], eng=eng)

        for kc in range(8):
            for c0 in range(0, CW, 1024):
                c1 = min(c0 + 1024, CW)
                convert(winb[:, kc * CW + c0: kc * CW + c1], win_d[kc * 128:(kc + 1) * 128, c0:c1], c1 - c0,
                        gcol[:, kc:kc + 1])
        for c in range(4):
            convert(wab[:, c * D:(c + 1) * D], wa_d[c * 128:(c + 1) * 128, :], D, cscale=0.5)
            convert(wbb[:, c * D:(c + 1) * D], wb_d[c * 128:(c + 1) * 128, :], D, cscale=0.5)
        for c in range(8):
            convert(wob[:, c * D:(c + 1) * D], wo_d[c * 128:(c + 1) * 128, :], D, cscale=0.5)
        for c0 in range(0, 5120, 1024):
            convert(biasb[:, c0:c0 + 1024], bias_d[:, c0:c0 + 1024], 1024)

        SCR = ["stg0", "stg1", "stg2", "stg3"]
        if stop == 'setup':
            NT = 0

        def proj_tok(bk, col0, width, off=0, first=True, last=True):
            for kc in range(8):
                mm(bank(bk)[:, off:off + width], xnT[:, kc * 128:(kc + 1) * 128],
                   winb[:, kc * CW + col0: kc * CW + col0 + width], kc == 0, kc == 7,
                   ["xnT", "wts"], [K[bk]], inc=(kc == 7 and last))

        def proj_feat(bk, col0, off):
            for kc in range(8):
                mm(bank(bk)[:, off:off + 128], winb[:, kc * CW + col0: kc * CW + col0 + 128],
                   xnT[:, kc * 128:(kc + 1) * 128], kc == 0, kc == 7, ["xnT", "wts"], [K[bk]], inc=(kc == 7))

        def rope_apply(dst3, src3, cos2, sin2, H, half, tmp, reads, writes):
            n = H * half
            cb = cos2.unsqueeze(1).broadcast_to([128, H, half])
            sb = sin2.unsqueeze(1).broadcast_to([128, H, half])
            t = [tmp[:, i * n:(i + 1) * n].rearrange("p (h d) -> p h d", h=H) for i in range(4)]
            x1 = src3[:, :, 0:half]
            x2 = src3[:, :, half:2 * half]
            rr = list(reads) + ["rope"]
            tt(t[0], x1, cb, ALU.mult, rr, ["rtmp"])
            tt(t[1], x2, sb, ALU.mult, rr, ["rtmp"])
            tt(t[2], x2, cb, ALU.mult, rr, ["rtmp"])
            tt(t[3], x1, sb, ALU.mult, rr, ["rtmp"])
            tt(dst3[:, :, 0:half], t[0], t[1], ALU.subtract, ["rtmp"], writes)
            tt(dst3[:, :, half:2 * half], t[2], t[3], ALU.add, ["rtmp"], writes)

        rtmp = sm

        for b in range(NT):
            slot = b % 5
            N = 128 * (b + 1)
            rp = rope[:, b * 40:(b + 1) * 40]
            x_sb = PT[:].bitcast(F32)
            PTALL = ["PT0", "PT1", "R2", "R3"]
            xnb = Rr[:]
            XNB = ["R0", "R1"]
            rt = sc[:, 1536:2560]

            def x_norm_stats():
                act(xnb, x_sb, AF.Square, PTALL, XNB + ["nsm"], accum=sm[:, 0:1])
                act(sm[:, 1:2], sm[:, 0:1], AF.Sqrt, ["nsm"], ["nsm"], scale=1.0 / D, bias=sm[:, 63:64])
                P.op("dve", lambda e, o=sm[:, 2:3], i=sm[:, 1:2]: e.reciprocal(out=o, in_=i), reads=["nsm"], writes=["nsm"])

            def x_norm_apply():
                ts(xnb, x_sb, sm[:, 2:3], None, ALU.mult, None, ["nsm"] + PTALL, XNB)
                P.op("dve", lambda e, a=PT[0:64, 0:1024]: e.memset(a, 0.0), reads=[], writes=["PT0"])

            def x_transposes():
                for kc in range(8):
                    tr(bankb(7)[:, kc * 128:(kc + 1) * 128], xnb[:, kc * 128:(kc + 1) * 128], XNB, [K[7]], inc=(kc == 7))

            if b == 0:
                dma("sp", x_sb, x_d[0:128, :], [], PTALL)
                x_norm_stats()
                x_norm_apply()
                x_transposes()
                cp(xnT[:], bankb(7), [K[7]], ["xnT"])
            act(dmy[:, 0:1], dmy[:, 1:2], AF.Tanh, [], ["dmy"])
            if stop == 'p1a':
                break
            proj_tok(0, VA, 512)
            va_dst = vaug[:, slot * 520:(slot + 1) * 520].rearrange("p (h d) -> p h d", h=8)[:, :, 0:64]
            act(va_dst, bank(0).rearrange("p (h d) -> p h d", h=8), AF.Copy, [K[0]], [f"vaug{slot}"])
            proj_tok(1, GA, 512)
            tnh = sc[:, 2816:3328]
            act(tnh, bank(1), AF.Tanh, [K[1]] + SCR, SCR, scale=0.5)
            stt(gas[:], tnh, 1.0, bank(1), ALU.add, ALU.mult, [K[1]] + SCR, ["gas"])
            if stop == 'p1b':
                break
            proj_tok(2, QB, 512)
            QI3 = QI[:].rearrange("p (h d) -> p h d", h=8)
            b03 = bank(2).rearrange("p (h d) -> p h d", h=8)
            ts(QI3[:, :, 16:64], b03[:, :, 16:64], 0.125, None, ALU.mult, None, [K[2]], ["QI"])
            rope_apply(QI3[:, :, 0:16], b03[:, :, 0:16], rp[:, 16:24], rp[:, 24:32], 8, 8, rt, [K[2]] + SCR, ["QI"] + SCR)
            if stop == 'p1c':
                break
            proj_tok(3, KBO, 128, off=0, last=False)
            proj_tok(3, IQ, 296, off=128)
            for hp in range(4):
                proj_feat(4, QA + hp * 128, hp * 128)
            act(qaT[:], bank(4), AF.Copy, [K[4]], ["qaT"], scale=0.125)
            for hp in range(4):
                proj_feat(5, KA + hp * 128, hp * 128)
            proj_tok(6, GB, 512)
            b1 = bank(3)
            rope_apply(KI[:, 0:64].rearrange("p (h d) -> p h d", h=1), b1[:, 0:64].rearrange("p (h d) -> p h d", h=1),
                       rp[:, 0:8], rp[:, 8:16], 1, 8, rt, [K[3]] + SCR, ["KI"] + SCR)
            cp(KI[:, 16:64], b1[:, 16:64], [K[3]], ["KI"])
            cp(vbc[:, b * 128:b * 128 + 64], b1[:, 64:128], [K[3]], ["vbc"])
            iw_ps = b1[:, 128 + 288:128 + 296]
            ts(sm[:, 16:24], iw_ps, 0.0, 2.0, ALU.is_ge, ALU.mult, [K[3]], ["sm"])
            ts(sm[:, 16:24], sm[:, 16:24], -1.0, None, ALU.add, None, ["sm"], ["sm"])
            stt(sm[:, 8:16], iw_ps, 0.0625, sm[:, 16:24], ALU.mult, ALU.mult, [K[3], "sm"], ["sm"])
            iqs = sc[:, 2560:2816].rearrange("p (h d) -> p h d", h=8)
            tt(iqs, b1[:, 128:384].rearrange("p (h d) -> p h d", h=8),
               sm[:, 8:16].unsqueeze(2).broadcast_to([128, 8, 32]), ALU.mult, [K[3], "sm"] + SCR, SCR)
            rope_apply(QI3[:, :, 64:72], iqs[:, :, 0:8], rp[:, 32:36], rp[:, 36:40], 8, 4, rt, SCR, ["QI"] + SCR)
            cp(QI3[:, :, 72:96], iqs[:, :, 8:32], SCR, ["QI"])
            rope_apply(KI[:, 64:72].rearrange("p (h d) -> p h d", h=1),
                       b1[:, 384:392].rearrange("p (h d) -> p h d", h=1),
                       rp[:, 32:36], rp[:, 36:40], 1, 4, rt, [K[3]] + SCR, ["KI"] + SCR)
            cp(KI[:, 72:96], b1[:, 392:416], [K[3]], ["KI"])
            tt(Dh[:].rearrange("p (h q) -> p h q", h=8), ident.unsqueeze(1).broadcast_to([128, 8, 128]),
               sm[:, 16:24].unsqueeze(2).broadcast_to([128, 8, 128]), ALU.mult, ["cbf", "sm"], ["Dh"])
            if stop == 'p1d':
                break
            for h in range(8):
                tr(bankb(7)[:, h * 128:(h + 1) * 128], QI[:, h * 128:(h + 1) * 128], ["QI"], [K[7]], inc=(h == 7))
            cp(QIT[:], bankb(7), [K[7]], ["QIT"])
            tr(bankb(0)[:, 0:128], KI[:], ["KI"], [K[0]], inc=True)
            cp(kcache[:, b * 128:(b + 1) * 128], bankb(0)[:, 0:128], [K[0]], ["kcache"])
            ka_dst = kaT[:].rearrange("p (c s t) -> p c s t", c=4, s=5)[:, :, slot, :]
            cp(ka_dst, bank(5).rearrange("p (c t) -> p c t", c=4), [K[5]], [f"kaT{slot}"])
            tnh2 = sc[:, 3328:3840]
            act(tnh2, bank(6), AF.Tanh, [K[6]] + SCR, SCR, scale=0.5)
            stt(gbT[:], tnh2, 1.0, bank(6), ALU.add, ALU.mult, [K[6]] + SCR, ["gbT"])
            if dbg and b == NT - 1:
                cp(sc[:, 3072:4096], QIT[:], ["QIT"], SCR)
                dma("sp", dbg_d["d_qit"], sc[:, 3072:4096], SCR, [])
                cp(sc[:, 3072:3584], kcache[:, 0:512], ["kcache"], SCR)
                dma("sp", dbg_d["d_kc"], sc[:, 3072:3584], SCR, [])

            if stop == 'p1':
                break
            jjs = [jj for jj in range(5) if b - 4 + jj >= 0]
            yacc_v = [Rr[:].bitcast(F32)[:, 0:260], Dh[:].bitcast(F32)[:, 0:260]]
            yacc_r = [["R0", "R1"], ["Dh"]]
            yaf_all = QI[:].bitcast(F32)

            def a_keytile(ji):
                jj = jjs[ji]
                j = b - 4 + jj
                sj = j % 5
                pts = ji % 2
                for half in range(2):
                    bk = 4 + half
                    for hh in range(4):
                        h = half * 4 + hh
                        pr = (h % 2) * 64
                        c = h // 2
                        mm(bank(bk)[:, hh * 128:(hh + 1) * 128],
                           kaT[pr:pr + 64, (c * 5 + sj) * 128:(c * 5 + sj + 1) * 128],
                           qaT[pr:pr + 64, c * 128:(c + 1) * 128], True, False,
                           [f"kaT{sj}", "qaT"], [K[bk]])
                        mm(bank(bk)[:, hh * 128:(hh + 1) * 128], ident,
                           biasb[:, (jj * 8 + h) * 128:(jj * 8 + h + 1) * 128], False, True,
                           ["cbf", "wts"], [K[bk]])
                pta = PT[:, pts * 1024:(pts + 1) * 1024]
                for half in range(2):
                    act(pta[:, half * 512:(half + 1) * 512], bank(4 + half), AF.Exp, [K[4 + half]], PTW[pts])
                for h in range(8):
                    bk = 6 + h // 4
                    o = (h % 4) * 65
                    mm(bank(bk)[:, o:o + 65], pta[:, h * 128:(h + 1) * 128],
                       vaug[:, (sj * 8 + h) * 65:(sj * 8 + h + 1) * 65], True, True,
                       [f"PT{pts}", f"vaug{sj}"], [K[bk]])
                for k2 in range(2):
                    if ji == 0:
                        cp(yacc_v[k2], bank(6 + k2)[:, 0:260], [K[6 + k2]], yacc_r[k2])
                    else:
                        tt(yacc_v[k2], yacc_v[k2], bank(6 + k2)[:, 0:260], ALU.add, [K[6 + k2]] + yacc_r[k2], yacc_r[k2])

            def a_finish():
                for k2 in range(2):
                    yv = yacc_v[k2].rearrange("p (h d) -> p h d", h=4)
                    rec = sm[:, 24 + 4 * k2:28 + 4 * k2].rearrange("p (h o) -> p h o", h=4, o=1)
                    P.op("dve", lambda e, o=rec, i=yv[:, :, 64:65]: e.reciprocal(out=o, in_=i),
                         reads=yacc_r[k2], writes=["sm"])
                    yaf = yaf_all[:, k2 * 256:(k2 + 1) * 256].rearrange("p (h d) -> p h d", h=4)
                    tt(yaf, yv[:, :, 0:64], rec.broadcast_to([128, 4, 64]), ALU.mult, ["sm", "QI"] + yacc_r[k2], ["QI"])
                tt(yab[:], yaf_all, gas[:], ALU.mult, ["QI", "gas"], ["yab"])
                for c in range(4):
                    tr(bankb(0)[:, c * 128:(c + 1) * 128], yab[:, c * 128:(c + 1) * 128], ["yab"], [K[0]])
                act(yaT[:], bankb(0)[:, 0:512], AF.Copy, [K[0]], ["yaT"])

            if stop == 'A':
                break
            nblk = (N + 511) // 512
            items = [(kb, h) for kb in range(nblk) for h in range(8)]

            LB = (0, 1, 4, 5)
            RS = (Rr[:, 0:512], Rr[:, 512:1024], PT[:, 1024:1536], PT[:, 1536:2048])
            RSR = ("R0", "R1", "R2", "R3")

            def idx_L(i):
                kb, h = items[i]
                k0 = kb * 512
                w = min(512, N - k0)
                lb = LB[i % 4]
                mm(bank(lb)[:, 0:w], IQz[:, h * 128:(h + 1) * 128], kcache[:, k0:k0 + w],
                   True, True, ["PT0", "kcache"], [K[lb]])
                rs = i % 4
                if h not in (1, 3, 5):
                    act(RS[rs][:, 0:w], bank(lb)[:, 0:w], AF.Relu, [K[lb]], [RSR[rs]])
                else:
                    ts(RS[rs][:, 0:w], bank(lb)[:, 0:w], 0.0, None, ALU.max, None, [K[lb]], [RSR[rs]])

            def idx_D(i):
                kb, h = items[i]
                k0 = kb * 512
                w = min(512, N - k0)
                rs = i % 4
                sb = 2 + kb % 2
                mm(bank(sb)[:, 0:w], Dh[:, h * 128:(h + 1) * 128], RS[rs][:, 0:w], h == 0, h == 7,
                   ["Dh", RSR[rs]], [K[sb]])
                if h == 7:
                    ts(sc[:, k0:k0 + w], bank(sb)[:, 0:w], 0.0, None, ALU.add, ALU.max, [K[sb]], SCR + ["bst"],
                       accum=bst[:, kb:kb + 1])

            IQz = PT[:, 0:1024]
            cp(IQz[64:128, :], QIT[64:128, :], ["QIT"], ["PT0", "PT1", "R2", "R3"])
            LA = 3
            for i in range(min(LA, len(items))):
                idx_L(i)
            for i in range(len(items)):
                idx_D(i)
                if i + LA < len(items):
                    idx_L(i + LA)
            P.op("dve", lambda e, a=sc[0:64, N - 64:N]: e.memset(a, -1e30), reads=[], writes=SCR)
            if dbg and b == NT - 1:
                dma("sp", dbg_d["d_sc"], sc[:, 0:512], SCR, [])
            if stop == 'idx':
                break
            LO, HI, D0, MID, CNT, U, SA, T2 = (sm[:, 32:33], sm[:, 33:34], sm[:, 34:35], sm[:, 35:36], sm[:, 36:37],
                                               sm[:, 37:38], sm[:, 38:39], sm[:, 39:40])
            if b < 2:
                P.op("dve", lambda e, a=LO: e.memset(a, -1e29), reads=[], writes=["bLO"])
                for ji in range(len(jjs)):
                    a_keytile(ji)
                a_finish()
            else:
                P.op("dve", lambda e, o=HI, i=bst[:, 0:nblk]: e.tensor_reduce(out=o, in_=i, axis=AX.X, op=ALU.max),
                     reads=["bst"], writes=["bHI"])
                P.op("dve", lambda e, o=LO, i=sc[:, 0:256]: e.tensor_reduce(out=o, in_=i, axis=AX.X, op=ALU.min),
                     reads=SCR, writes=["bLO"])
                tt(D0, HI, LO, ALU.subtract, ["bHI", "bLO"], ["bD0"])
                ts(DI, CI, D0, None, ALU.mult, None, ["bD0", "CI"], ["bDI"])
                n1 = 64 * max(1, int(round(DVE_SHARE * N / 64)))
                nA = N - n1
                jo2 = junk2[:, 0:64].unsqueeze(1).broadcast_to([128, nA // 64, 64])
                sview2 = sc[:, n1:N].rearrange("p (a c) -> p a c", c=64)
                tt(MID, LO, DI[:, 0:1], ALU.add, ["bLO", "bDI"], ["bMID"])
                jo = junk[:, 0:64].unsqueeze(1).broadcast_to([128, n1 // 64, 64])
                sview = sc[:, 0:n1].rearrange("p (a c) -> p a c", c=64)
                a_at = {3 * ji + 3: ji for ji in range(len(jjs))}
                MIDm = sm[:, 43:44]
                thrp = (511.0 - nA) / 2.0
                for it in range(NBIS):
                    last = it == NBIS - 1
                    stt(MIDm, DI[:, it:it + 1], -1.0 if last else -0.5, MID, ALU.mult, ALU.add, ["bDI", "bMID"], ["bMIDm"])
                    ts(jo, sview, MID, thrp, ALU.is_ge, ALU.subtract, SCR + ["bMID"], ["junk", "bCNT"], accum=CNT)
                    act(jo2, sview2, AF.Sign, SCR + ["bMID"], ["junk2", "bSA"], scale=-1.0, bias=MID, accum=SA)
                    stt(U, SA, -0.5, CNT, ALU.mult, ALU.is_ge, ["bCNT", "bSA"], ["bU"])
                    if not last:
                        stt(MID, U, DI[:, it:it + 1], MIDm, ALU.mult, ALU.add, ["bU", "bDI", "bMIDm"], ["bMID"])
                    else:
                        stt(LO, U, DI[:, NBIS - 1:NBIS], MIDm, ALU.mult, ALU.add, ["bU", "bDI", "bMIDm"], ["bLO"])
                    if it in a_at:
                        a_keytile(a_at[it])
                a_finish()
            if dbg and b == NT - 1:
                jo = junk[:, 0:64].unsqueeze(1).broadcast_to([128, N // 64, 64])
                sview = sc[:, 0:N].rearrange("p (a c) -> p a c", c=64)
                ts(jo, sview, LO, None, ALU.is_ge, ALU.add, SCR + ["bLO"], ["junk", "bCNT"], accum=CNT)
                dma("sp", dbg_d["d_lo"], LO, ["bLO"], [])
                dma("sp", dbg_d["d_cnt"], CNT, ["bCNT"], [])
            if stop == 'bis':
                break
            QBz = Dh
            P.op("dve", lambda e, a=QBz[64:128, :]: e.memset(a, 0.0), reads=[], writes=["Dh"])
            cp(QBz[0:64, :], QIT[0:64, :], ["QIT"], ["Dh"])

            def b_stage1(j):
                ns = j % 2
                nm = nmr[:, ns * 128:(ns + 1) * 128]
                ts(nm, sc[:, j * 128:(j + 1) * 128], LO, NEGM, ALU.is_lt, ALU.mult, SCR + ["bLO"], [f"nm{ns}"])
                pts = j % 2
                ptb = PT[:, pts * 1024:(pts + 1) * 1024]
                sb0 = 2 if j % 2 == 0 else 6
                for half in range(2):
                    bk = sb0 + half
                    mm(bank(bk), kcache[:, j * 128:(j + 1) * 128], QBz[:, half * 512:(half + 1) * 512],
                       True, False, ["kcache", "Dh"], [K[bk]])
                    mm(bank(bk), nm, I4, False, True, [f"nm{ns}", "cbf"], [K[bk]])
                for half in range(2):
                    act(ptb[:, half * 512:(half + 1) * 512], bank(sb0 + half), AF.Exp, [K[sb0 + half]], PTW[pts])

            def b_stage2(j):
                pts = j % 2
                ptb = PT[:, pts * 1024:(pts + 1) * 1024]
                for half in range(2):
                    mm(bank(4 + half), vbc[:, j * 128:(j + 1) * 128], ptb[:, half * 512:(half + 1) * 512],
                       j == 0, j == b, [f"PT{pts}", "vbc"], [K[4 + half]])

            for j in range(b + 1):
                b_stage1(j)
                if j >= 1:
                    b_stage2(j - 1)
            b_stage2(b)
            act(sc[:, 0:512], bank(4), AF.Copy, [K[4]] + SCR, SCR)
            cp(sc[:, 512:1024], bank(5), [K[5]] + SCR, SCR)
            for h in range(8):
                P.op("pe", lambda e, o=bank(h // 4)[:, (h % 4) * 128:(h % 4 + 1) * 128],
                     i=sc[:, h * 128:(h + 1) * 128]: e.transpose(out=o, in_=i, identity=identf[:]),
                     reads=SCR + ["identf"], writes=[K[h // 4]])
            for k2 in range(2):
                yv = bank(k2).rearrange("p (h d) -> p h d", h=4)
                rec = sm[:, 24 + 4 * k2:28 + 4 * k2].rearrange("p (h o) -> p h o", h=4, o=1)
                P.op("dve", lambda e, o=rec, i=yv[:, :, 64:65]: e.reciprocal(out=o, in_=i), reads=[K[k2]], writes=["sm"])
                ybf = sc[:, 1024 + k2 * 256:1024 + (k2 + 1) * 256].rearrange("p (h d) -> p h d", h=4)
                tt(ybf, yv[:, :, 0:64], rec.broadcast_to([128, 4, 64]), ALU.mult, [K[k2], "sm"] + SCR, SCR)
            tt(yab[:], sc[:, 1024:1536], gbT[:], ALU.mult, SCR + ["gbT"], ["yab"])
            for c in range(4):
                tr(bankb(2)[:, c * 128:(c + 1) * 128], yab[:, c * 128:(c + 1) * 128], ["yab"], [K[2]])
            cp(ybT[:], bankb(2)[:, 0:512], [K[2]], ["ybT"])
            if dbg and b == NT - 1:
                cp(sc[:, 3072:3584], ybT[:], ["ybT"], SCR)
                dma("sp", dbg_d["d_yb"], sc[:, 3072:3584], SCR, [])

            if stop == 'B':
                break
            hbuf = sc[:, 2048:3072]
            dma("sp", hbuf, x_d[b * 128:(b + 1) * 128, :], [], SCR)
            if b + 1 < NT:
                dma("sp", x_sb, x_d[(b + 1) * 128:(b + 2) * 128, :], [], PTALL)
            for nb in range(2):
                cs = slice(nb * 512, (nb + 1) * 512)
                for c in range(4):
                    mm(bank(0), yaT[:, c * 128:(c + 1) * 128], wab[:, c * D + nb * 512: c * D + (nb + 1) * 512],
                       c == 0, c == 3, ["yaT", "wts"], [K[0]], inc=(c == 3))
                for c in range(4):
                    mm(bank(1), ybT[:, c * 128:(c + 1) * 128], wbb[:, c * D + nb * 512: c * D + (nb + 1) * 512],
                       c == 0, c == 3, ["ybT", "wts"], [K[1]], inc=(c == 3))
                for gi, (zc, bk) in enumerate(((ZA, 2), (ZB, 3))):
                    for kc in range(8):
                        mm(bank(bk), xnT[:, kc * 128:(kc + 1) * 128],
                           winb[:, kc * CW + zc + nb * 512: kc * CW + zc + (nb + 1) * 512], kc == 0, False,
                           ["xnT", "wts"], [K[bk]])
                    mm(bank(bk), ones[0:1, 0:128], bmb[0:1, gi * D + nb * 512: gi * D + (nb + 1) * 512], False, True,
                       ["cbf", "bmb"], [K[bk]], inc=True)
                g_a = sc[:, 0:512]
                g_b = sc[:, 512:1024]
                act(g_a, bank(2), AF.Tanh, [K[2]] + SCR, SCR, scale=0.5)
                act(g_b, bank(3), AF.Tanh, [K[3]] + SCR, SCR, scale=0.5)
                stt(g_a, g_a, 1.0, bank(0), ALU.add, ALU.mult, [K[0]] + SCR, SCR)
                stt(g_b, g_b, 1.0, bank(1), ALU.add, ALU.mult, [K[1]] + SCR, SCR)
                tt(mgb[:, cs], g_a, g_b, ALU.add, SCR, ["QI"])
            if dbg and b == NT - 1:
                cp(sc[:, 3072:4096], mgb[:], ["QI"], SCR)
                dma("sp", dbg_d["d_mg"], sc[:, 3072:4096], SCR, [])
            act(dmy[:, 0:1], dmy[:, 1:2], AF.Sqrt, [], ["dmy"])
            if b + 1 < NT:
                x_norm_stats()
                x_norm_apply()
            for kc in range(8):
                tr(bankb(4)[:, kc * 128:(kc + 1) * 128], mgb[:, kc * 128:(kc + 1) * 128], ["QI"], [K[4]], inc=(kc == 7))
            cp(mgT[:], bankb(4), [K[4]], ["QIT"])
            if b + 1 < NT:
                x_transposes()
                act(xnT[:], bankb(7), AF.Copy, [K[7]], ["xnT"])
            for nb in range(2):
                for kc in range(8):
                    mm(bank(5 + nb), mgT[:, kc * 128:(kc + 1) * 128],
                       wob[:, kc * D + nb * 512: kc * D + (nb + 1) * 512], kc == 0, kc == 7,
                       ["QIT", "wts"], [K[5 + nb]], inc=(kc == 7))
            for nb in range(2):
                tt(hbuf[:, nb * 512:(nb + 1) * 512], hbuf[:, nb * 512:(nb + 1) * 512], bank(5 + nb), ALU.add,
                   [K[5 + nb]] + SCR, SCR)
            act(sc[:, 0:1024], hbuf, AF.Square, SCR, SCR + ["sm"], accum=sm[:, 40:41])
            act(sm[:, 41:42], sm[:, 40:41], AF.Sqrt, ["sm"], ["sm"], scale=1.0 / D, bias=sm[:, 63:64])
            P.op("dve", lambda e, o=sm[:, 42:43], i=sm[:, 41:42]: e.reciprocal(out=o, in_=i), reads=["sm"], writes=["sm"])
            stt(hbuf, hbuf, sm[:, 42:43], fgb[:], ALU.mult, ALU.mult, SCR + ["sm", "fgb"], SCR)
            dma("sp", out_d[b * 128:(b + 1) * 128, :], hbuf, SCR, [])

        P.finish_waits("sp")
        P.emit()
    return nc


def _host_consts():
    bf = ml_dtypes.bfloat16
    cb = np.zeros((128, 896), np.float32)
    cb[:, 0:128] = np.eye(128)
    for r in range(4):
        cb[:, 128 + r * 128:128 + (r + 1) * 128] = np.eye(128)
    cb[:, 640:768] = 1.0
    cb = cb.astype(bf)
    pos = np.arange(S, dtype=np.float32)
    invB = (np.float32(500000.0) ** (-np.arange(8, dtype=np.float32) / np.float32(8))).astype(np.float32)
    invI = (np.float32(500000.0) ** (-np.arange(4, dtype=np.float32) / np.float32(4))).astype(np.float32)
    angB = (pos[:, None] * invB[None, :]).astype(np.float32)
    angI = (pos[:, None] * invI[None, :]).astype(np.float32)
    tab = np.concatenate([np.cos(angB), np.sin(angB), np.cos(angB) * 0.125, np.sin(angB) * 0.125,
                          np.cos(angI), np.sin(angI)], axis=1).astype(np.float32)
    rope = np.ascontiguousarray(tab.reshape(32, 128, 40).transpose(1, 0, 2).reshape(128, 32 * 40))
    kk = np.arange(128)[:, None, None]
    jj = np.arange(5)[None, :, None]
    q = np.arange(128)[None, None, :]
    dist = 128 * (4 - jj) + q - kk
    idx = np.clip(dist, -256, 256) + 256
    dchunk = 2 * (jj - 4) + (kk >= 64).astype(np.int64) - (q >= 64).astype(np.int64)
    bad = (dchunk < -8) | (dchunk > 0)
    return cb, rope, idx, bad


def _bias_table(rel_bias, idx, bad):
    t = rel_bias[:, idx]
    t = np.where(bad[None], np.float32(NEGM), t).astype(np.float32)
    return np.ascontiguousarray(t.transpose(1, 2, 0, 3).reshape(128, 5 * 8 * 128))


def make_in_maps(x, norm_gain, w_in, b_merge, rel_bias, w_branch_a, w_branch_b, w_out, final_norm_gain):
    cb, rope, idx, bad = _host_consts()
    shared = {
        "w_in": np.ascontiguousarray(w_in[0], dtype=np.float32),
        "wa": np.ascontiguousarray(w_branch_a[0], dtype=np.float32),
        "wb": np.ascontiguousarray(w_branch_b[0], dtype=np.float32),
        "wo": np.ascontiguousarray(w_out[0], dtype=np.float32),
        "gcol": np.ascontiguousarray(np.asarray(norm_gain[0], np.float32).reshape(8, 128).T),
        "fgain": np.ascontiguousarray(np.asarray(final_norm_gain, np.float32).reshape(1, D)),
        "bmrow": np.ascontiguousarray(np.asarray(b_merge[0], np.float32).reshape(1, 2 * D)),
        "biasT": _bias_table(np.asarray(rel_bias[0], np.float32), idx, bad),
        "rope": rope,
        "cbf": cb,
        "identf": np.eye(128, dtype=np.float32),
    }
    return [dict(shared, x=np.ascontiguousarray(x[i], dtype=np.float32)) for i in range(x.shape[0])]


def kernel(x, norm_gain, w_in, b_merge, rel_bias, w_branch_a, w_branch_b, w_out, final_norm_gain):
    args = [np.asarray(a) for a in (x, norm_gain, w_in, b_merge, rel_bias, w_branch_a, w_branch_b, w_out,
                                    final_norm_gain)]
    in_maps = make_in_maps(*args)
    nc = build()
    res = run_bass_kernel_spmd(nc, in_maps, core_ids=list(range(8)))
    return np.stack([np.asarray(r["out"], dtype=np.float32) for r in res.results], axis=0)
```
